# Optimizing a Trainium2 kernel written in Bass

```python
import jax, jax.numpy as jnp
from jax import lax
import numpy as np

D_MODEL = 2048
BATCH = 1
SEQ = 8192
DEPTH = 4

CHUNK = 64
N_A_LAYERS = DEPTH // 2
N_B_LAYERS = DEPTH - N_A_LAYERS
RWKV_HEAD = 64
RWKV_HEADS = D_MODEL // RWKV_HEAD
DECAY_LORA = 96
AAA_LORA = 96
MV_LORA = 64
GATE_LORA = 256
N_MIX = 6
LNX_EPS = 1e-5 * RWKV_HEAD
FOX_HEAD = 128
FOX_HEADS = D_MODEL // FOX_HEAD
Q_BLOCK = 128
D_FF = 4 * D_MODEL
RMS_EPS = 1e-6

kernel_name = "rwkv7_fox_yoco_hybrid"


def rms_norm(x, gain):
    xf = x.astype(jnp.float32)
    y = xf * lax.rsqrt(jnp.mean(xf * xf, axis=-1, keepdims=True) + RMS_EPS)
    return (y * gain.astype(jnp.float32)).astype(x.dtype)


def squared_relu_mlp(h, w_up, w_down):
    return jnp.square(jax.nn.relu(h @ w_up)) @ w_down


def rwkv7_recurrence(r, decay, k, v, a, b):
    B, S, H, N = r.shape
    n_chunks = S // CHUNK

    def to_chunks(t):
        return t.reshape(B, n_chunks, CHUNK, H, N).transpose(1, 2, 0, 3, 4)

    def step(state, inp):
        r_t, w_t, k_t, v_t, a_t, b_t = inp
        sa = jnp.einsum('bhvk,bhk->bhv', state, a_t)
        state = (state * w_t[:, :, None, :]
                 + sa[..., None] * b_t[:, :, None, :]
                 + v_t[..., None] * k_t[:, :, None, :])
        y_t = jnp.einsum('bhvk,bhk->bhv', state, r_t)
        return state, y_t

    def chunk_step(state, chunk_inp):
        return lax.scan(step, state, chunk_inp)

    state0 = jnp.zeros((B, H, N, N), jnp.float32)
    inputs = (to_chunks(r), to_chunks(decay), to_chunks(k), to_chunks(v), to_chunks(a), to_chunks(b))
    _, y = lax.scan(chunk_step, state0, inputs)
    return y.transpose(2, 0, 1, 3, 4).reshape(B, S, H, N)


def rwkv7_time_mix(h, x_mix, w_rkv, w0, w1, w2, a0, a1, a2, g1, g2, k_k, k_a, r_k,
                   lnx_w, lnx_b, w_o, v_first, v_res):
    B, S, D = h.shape
    H, N = RWKV_HEADS, RWKV_HEAD
    f32 = jnp.float32
    prev = jnp.pad(h, ((0, 0), (1, 0), (0, 0)))[:, :S]
    xs = h[:, :, None, :] + (prev - h)[:, :, None, :] * x_mix
    rkv = jnp.einsum('bsnd,nde->bsne', xs[:, :, :3], w_rkv)
    r, k, v = rkv[:, :, 0], rkv[:, :, 1], rkv[:, :, 2]
    xv, xw, xa, xg = xs[:, :, 2], xs[:, :, 3], xs[:, :, 4], xs[:, :, 5]

    w_log = -jax.nn.softplus(-(w0 + jnp.tanh(xw @ w1) @ w2).astype(f32)) - 0.5
    decay = jnp.exp(-jnp.exp(w_log))
    if v_res is None:
        v_first = v
    else:
        v0, v1, v2 = v_res
        v = v + (v_first - v) * jax.nn.sigmoid(v0 + (xv @ v1) @ v2)
    a = jax.nn.sigmoid((a0 + (xa @ a1) @ a2).astype(f32))
    g = jax.nn.sigmoid(xg @ g1) @ g2

    heads = lambda t: t.astype(f32).reshape(B, S, H, N)
    kk = heads(k * k_k)
    kk = kk / jnp.maximum(jnp.sqrt(jnp.sum(kk * kk, axis=-1, keepdims=True)), 1e-12)
    k = k.astype(f32) * (1.0 + (a - 1.0) * k_a.astype(f32))
    rh, kh, vh, ah, dh = heads(r), heads(k), heads(v), heads(a), heads(decay)

    y = rwkv7_recurrence(rh, dh, kh, vh, -kk, kk * ah)
    mu = jnp.mean(y, axis=-1, keepdims=True)
    var = jnp.mean(jnp.square(y - mu), axis=-1, keepdims=True)
    y = ((y - mu) * lax.rsqrt(var + LNX_EPS)).reshape(B, S, D)
    y = y * lnx_w.astype(f32) + lnx_b.astype(f32)
    bonus = jnp.sum(rh * kh * r_k.astype(f32), axis=-1, keepdims=True) * vh
    y = y + bonus.reshape(B, S, D)
    out = (y * g.astype(f32)).astype(w_o.dtype) @ w_o
    return out.astype(h.dtype), v_first


def shared_kv_forget(x, kv_norm, w_kvf, b_f):
    B, S, D = x.shape
    proj = rms_norm(x, kv_norm) @ w_kvf
    k = proj[..., :D].reshape(B, S, FOX_HEADS, FOX_HEAD).transpose(0, 2, 1, 3)
    v = proj[..., D:2 * D].reshape(B, S, FOX_HEADS, FOX_HEAD).transpose(0, 2, 1, 3)
    log_f = jax.nn.log_sigmoid((proj[..., 2 * D:] + b_f).astype(jnp.float32))
    cum_log_f = jnp.cumsum(log_f.transpose(0, 2, 1), axis=-1)
    return k, v, cum_log_f


def forgetting_attention(h, w_q, w_o, k, v, cum_log_f):
    B, S, D = h.shape
    q = (h @ w_q).reshape(B, S, FOX_HEADS, FOX_HEAD).transpose(0, 2, 1, 3)
    scale = FOX_HEAD ** -0.5
    outs = []
    for blk in range(S // Q_BLOCK):
        q0, q1 = blk * Q_BLOCK, (blk + 1) * Q_BLOCK
        kb, vb = k[:, :, :q1], v[:, :, :q1]
        s = jnp.einsum('bhqd,bhkd->bhqk', q[:, :, q0:q1], kb).astype(jnp.float32) * scale
        s = s + cum_log_f[:, :, q0:q1, None] - cum_log_f[:, :, None, :q1]
        causal = jnp.arange(q0, q1)[:, None] >= jnp.arange(q1)[None, :]
        p = jax.nn.softmax(jnp.where(causal, s, -jnp.inf), axis=-1)
        outs.append(jnp.einsum('bhqk,bhkd->bhqd', p.astype(vb.dtype), vb))
    o = jnp.concatenate(outs, axis=2).transpose(0, 2, 1, 3).reshape(B, S, D)
    return o @ w_o


def setup_inputs(seed: int = 0) -> dict:
    key = jax.random.key(seed)
    ks = iter(jax.random.split(key, 40))
    D = D_MODEL
    nrm = lambda shape, scale: jax.random.normal(next(ks), shape, jnp.float32) * scale
    unif = lambda shape, lo, hi: jax.random.uniform(next(ks), shape, jnp.float32, lo, hi)
    na, nb = N_A_LAYERS, N_B_LAYERS
    return {
        "x": nrm((BATCH, SEQ, D), 1.0),
        "mix_norm": 1.0 + nrm((DEPTH, D), 0.02),
        "ffn_norm": 1.0 + nrm((DEPTH, D), 0.02),
        "final_norm": 1.0 + nrm((D,), 0.02),
        "rwkv_x_mix": unif((na, N_MIX, D), 0.0, 1.0),
        "rwkv_w_rkv": nrm((na, 3, D, D), D ** -0.5),
        "rwkv_w0": unif((na, D), -6.0, -1.0),
        "rwkv_w1": nrm((na, D, DECAY_LORA), D ** -0.5),
        "rwkv_w2": nrm((na, DECAY_LORA, D), 0.5 * DECAY_LORA ** -0.5),
        "rwkv_a0": nrm((na, D), 0.1),
        "rwkv_a1": nrm((na, D, AAA_LORA), D ** -0.5),
        "rwkv_a2": nrm((na, AAA_LORA, D), AAA_LORA ** -0.5),
        "rwkv_v0": nrm((na - 1, D), 0.1),
        "rwkv_v1": nrm((na - 1, D, MV_LORA), D ** -0.5),
        "rwkv_v2": nrm((na - 1, MV_LORA, D), MV_LORA ** -0.5),
        "rwkv_g1": nrm((na, D, GATE_LORA), D ** -0.5),
        "rwkv_g2": nrm((na, GATE_LORA, D), GATE_LORA ** -0.5),
        "rwkv_k_k": 0.85 + nrm((na, D), 0.02),
        "rwkv_k_a": 1.0 + nrm((na, D), 0.02),
        "rwkv_r_k": nrm((na, RWKV_HEADS, RWKV_HEAD), 0.1),
        "rwkv_lnx_w": 1.0 + nrm((na, D), 0.02),
        "rwkv_lnx_b": nrm((na, D), 0.02),
        "rwkv_w_o": nrm((na, D, D), D ** -0.5),
        "kv_norm": 1.0 + nrm((D,), 0.02),
        "w_kvf": nrm((D, 2 * D + FOX_HEADS), D ** -0.5),
        "b_f": nrm((FOX_HEADS,), 0.1),
        "fox_w_q": nrm((nb, D, D), D ** -0.5),
        "fox_w_o": nrm((nb, D, D), D ** -0.5),
        "mlp_w_up": nrm((DEPTH, D, D_FF), D ** -0.5),
        "mlp_w_down": nrm((DEPTH, D_FF, D), D_FF ** -0.5),
    }


def reference(x, mix_norm, ffn_norm, final_norm, rwkv_x_mix, rwkv_w_rkv, rwkv_w0, rwkv_w1,
              rwkv_w2, rwkv_a0, rwkv_a1, rwkv_a2, rwkv_v0, rwkv_v1, rwkv_v2, rwkv_g1, rwkv_g2,
              rwkv_k_k, rwkv_k_a, rwkv_r_k, rwkv_lnx_w, rwkv_lnx_b, rwkv_w_o, kv_norm, w_kvf,
              b_f, fox_w_q, fox_w_o, mlp_w_up, mlp_w_down):
    v_first = None
    k_sh = v_sh = c_sh = None
    for layer in range(DEPTH):
        if layer < N_A_LAYERS:
            i = layer
            v_res = None if i == 0 else (rwkv_v0[i - 1], rwkv_v1[i - 1], rwkv_v2[i - 1])
            h = rms_norm(x, mix_norm[layer])
            mixed, v_first = rwkv7_time_mix(
                h, rwkv_x_mix[i], rwkv_w_rkv[i], rwkv_w0[i], rwkv_w1[i], rwkv_w2[i],
                rwkv_a0[i], rwkv_a1[i], rwkv_a2[i], rwkv_g1[i], rwkv_g2[i], rwkv_k_k[i],
                rwkv_k_a[i], rwkv_r_k[i], rwkv_lnx_w[i], rwkv_lnx_b[i], rwkv_w_o[i],
                v_first, v_res)
        else:
            if layer == N_A_LAYERS:
                k_sh, v_sh, c_sh = shared_kv_forget(x, kv_norm, w_kvf, b_f)
            j = layer - N_A_LAYERS
            h = rms_norm(x, mix_norm[layer])
            mixed = forgetting_attention(h, fox_w_q[j], fox_w_o[j], k_sh, v_sh, c_sh)
        x = x + mixed
        x = x + squared_relu_mlp(rms_norm(x, ffn_norm[layer]), mlp_w_up[layer], mlp_w_down[layer])
    return rms_norm(x, final_norm)
```

```python
import contextlib
import numpy as np
import concourse.bass as bass
import concourse.mybir as mybir
from concourse.bass_utils import run_bass_kernel_spmd

F32 = mybir.dt.float32
BF16 = mybir.dt.bfloat16
AF = mybir.ActivationFunctionType
ALU = mybir.AluOpType
AX = mybir.AxisListType

ENGS = ("tensor", "vector", "scalar", "gpsimd", "sync")


class Dep:
    __slots__ = ("w", "r")

    def __init__(self):
        self.w = None
        self.r = {}


class Buf:
    def __init__(self, t, ndeps=0):
        self.t = t
        self.dep = Dep()
        self.deps = [self.dep for _ in range(ndeps)]

    def __getitem__(self, idx):
        return self.t[idx]


class Prog:
    def __init__(self, n_dma_sems=6):
        self.nc = bass.Bass("TRN2", target_bir_lowering=False)
        self.es = contextlib.ExitStack()
        self.ops = {e: [] for e in ENGS}
        self.cnt = {e: 0 for e in ENGS}
        self.semobj = {}
        for e in ENGS:
            if e != "sync":
                self.semobj[e] = self.es.enter_context(self.nc.semaphore("s_" + e))
        self.dq = {}
        for q in ("sync", "gpsimd", "scalar"):
            for i in range(n_dma_sems):
                self.semobj[(q, i)] = self.es.enter_context(self.nc.semaphore(f"d_{q}{i}"))
            self.dq[q] = 0
        self.nds = n_dma_sems
        self.waited = {e: {} for e in ENGS}
        self.nbuf = 0

    def dram(self, name, shape, dt, kind):
        return self.nc.dram_tensor(name, list(shape), dt, kind=kind).ap()

    def sbuf(self, shape, dt, name=None, ndeps=0):
        self.nbuf += 1
        t = self.es.enter_context(self.nc.sbuf_tensor(name or f"sb{self.nbuf}", list(shape), dt))
        return Buf(t, ndeps)

    def psum(self, shape, dt, name=None, ndeps=0):
        self.nbuf += 1
        t = self.es.enter_context(self.nc.psum_tensor(name or f"ps{self.nbuf}", list(shape), dt))
        return Buf(t, ndeps)

    def _need(self, e, tok):
        if tok is None:
            return
        k, v = tok
        if self.waited[e].get(k, 0) >= v:
            return
        self.waited[e][k] = v
        self.ops[e].append(("wait", k, v))

    @staticmethod
    def _deps(xs):
        out = []
        for x in xs:
            out.append(x.dep if isinstance(x, Buf) else x)
        return out

    def op(self, e, fn, R=(), W=()):
        R = self._deps(R)
        W = self._deps(W)
        for d in R:
            self._need(e, d.w)
        for d in W:
            if d.w is not None and (d.w[0] != e or e != "tensor"):
                self._need(e, d.w)
            for k, v in d.r.items():
                if k != e or e != "tensor":
                    self._need(e, (k, v))
        self.cnt[e] += 1
        tok = (e, self.cnt[e])
        self.ops[e].append(("op", fn))
        for d in R:
            d.r[e] = tok[1]
        for d in W:
            d.w = tok
            d.r = {}
        return tok

    def dma(self, q, out_ap, in_ap, R=(), W=()):
        R = self._deps(R)
        W = self._deps(W)
        for d in R:
            self._need(q, d.w)
        for d in W:
            self._need(q, d.w)
            for k, v in d.r.items():
                self._need(q, (k, v))
        j = self.dq[q]
        self.dq[q] += 1
        s = j % self.nds
        v = 16 * (j // self.nds + 1)
        key = (q, s)
        if j >= self.nds:
            self._need(q, (key, v - 16))
        self.ops[q].append(("dma", out_ap, in_ap, key))
        tok = (key, v)
        for d in R:
            d.r[key] = v
        for d in W:
            d.w = tok
            d.r = {}
        return tok

    def finish(self):
        for q in ("sync", "gpsimd", "scalar"):
            j = self.dq[q]
            for s in range(min(j, self.nds)):
                n = (j - 1 - s) // self.nds + 1
                self._need(q, ((q, s), 16 * n))
        with self.nc.Block() as block:
            for e in ENGS:
                if not self.ops[e]:
                    continue

                def body(eng, e=e):
                    for o in self.ops[e]:
                        if o[0] == "wait":
                            eng.wait_ge(self.semobj[o[1]], o[2])
                        elif o[0] == "op":
                            o[1](eng).then_inc(self.semobj[e], 1)
                        else:
                            eng.dma_start(out=o[1], in_=o[2]).then_inc(self.semobj[o[3]], 16)

                getattr(block, e)(body)
        self.es.close()
        return self.nc

    def mm(self, out, lhsT, rhs, start, stop, R=(), W=()):
        return self.op("tensor", lambda e: e.matmul(out, lhsT, rhs, start=start, stop=stop), R, W)

    def act(self, out, in_, func, R=(), W=(), bias=None, scale=1.0, accum_out=None, eng="scalar"):
        kw = {}
        if bias is not None:
            kw["bias"] = bias
        if accum_out is not None:
            kw["accum_out"] = accum_out
        return self.op(eng, lambda e: e.activation(out=out, in_=in_, func=func, scale=scale, **kw), R, W)

    def tt(self, eng, out, in0, in1, op, R=(), W=()):
        return self.op(eng, lambda e: e.tensor_tensor(out=out, in0=in0, in1=in1, op=op), R, W)

    def cp(self, eng, out, in_, R=(), W=()):
        if eng == "scalar":
            return self.op(eng, lambda e: e.copy(out, in_), R, W)
        return self.op(eng, lambda e: e.tensor_copy(out, in_), R, W)

    def ts(self, eng, out, in0, s1, s2, op0, op1=None, R=(), W=()):
        if op1 is None:
            return self.op(eng, lambda e: e.tensor_scalar(out, in0, s1, None, op0), R, W)
        return self.op(eng, lambda e: e.tensor_scalar(out, in0, s1, s2, op0, op1), R, W)

    def stt(self, out, in0, scalar, in1, op0, op1, R=(), W=(), eng="vector"):
        return self.op(eng, lambda e: e.scalar_tensor_tensor(out=out, in0=in0, scalar=scalar, in1=in1, op0=op0, op1=op1), R, W)

    def memset(self, eng, ap, val, W=()):
        return self.op(eng, lambda e: e.memset(ap, val), (), W)

import numpy as np

import os
LIMIT = int(os.environ.get('LIMIT', '99'))
C = 64
NH = 4
HD = 64


def build_rstage(NCH, G=4):
    p = Prog()
    nc = p.nc
    NG = NCH // G
    fm = p.dram("fm", [NG, 64, G, NH, 4, 64], BF16, "ExternalInput")
    tm = p.dram("tm", [NG, 128, G, NH, 64], BF16, "ExternalInput")
    vv = p.dram("vv", [NG, 64, G, NH, 64], BF16, "ExternalInput")
    pc = p.dram("pc", [64, NCH, NH], F32, "ExternalInput")
    cst = p.dram("cst", [128, 3, 128], F32, "ExternalInput")
    yT = p.dram("yT", [NG, 64, G, NH, 64], F32, "ExternalOutput")

    cst_sb = p.sbuf([128, 3, 128], F32)
    p.dma("sync", cst_sb[:], cst, W=[cst_sb])
    pc_sb = p.sbuf([64, NCH, NH], F32)
    p.dma("sync", pc_sb[:], pc, W=[pc_sb])
    maskS4 = p.sbuf([128, NH, 128], F32)
    maskL4 = p.sbuf([64, NH, 64], F32)
    identF4 = p.sbuf([64, NH, 64], F32)
    ident_bf = p.sbuf([64, 64], BF16)
    for h in range(NH):
        p.cp("vector", maskS4[:, h, :], cst_sb[:, 0, :], R=[cst_sb], W=[maskS4])
        p.cp("vector", maskL4[:, h, :], cst_sb[0:64, 1, 0:64], R=[cst_sb], W=[maskL4])
        p.cp("vector", identF4[:, h, :], cst_sb[0:64, 2, 0:64], R=[cst_sb], W=[identF4])
    p.cp("vector", ident_bf[:], cst_sb[0:64, 2, 0:64], R=[cst_sb], W=[ident_bf])

    Hs = [p.sbuf([64, NH, 64], F32, name=f"H{i}") for i in range(2)]
    p.memset("vector", Hs[0][:], 0.0, W=[Hs[0]])

    NB = 2
    FMg = [p.sbuf([64, G, NH, 4, 64], BF16, name=f"FMg{i}") for i in range(NB)]
    TMg = [p.sbuf([128, G, NH, 64], BF16, name=f"TMg{i}") for i in range(NB)]
    UVg = [p.sbuf([128, G, NH, 64], BF16, name=f"UVg{i}") for i in range(NB)]
    UVv = [Dep() for _ in range(NB)]
    UVu = [[Dep() for _ in range(G)] for _ in range(NB)]
    OUTg = [p.sbuf([64, G, NH, 64], F32, name=f"OUTg{i}") for i in range(NB)]
    ZVg = [p.sbuf([128, G, NH, 64], BF16, name=f"ZVg{i}") for i in range(NB)]
    for i in range(NB):
        p.memset("gpsimd", ZVg[i][0:64], 0.0, W=[ZVg[i]])

    def perchunk(shape, dt, name, n=2):
        return [p.sbuf(shape, dt, name=f"{name}{i}") for i in range(n)]

    S_f = perchunk([128, NH, 128], F32, "S_f")
    S_bf = perchunk([128, NH, 128], BF16, "S_bf")
    Pa = perchunk([64, NH, 64], F32, "Pa")
    PaT = perchunk([64, NH, 64], F32, "PaT")
    Ya = perchunk([64, NH, 128], F32, "Ya")
    W_bf = perchunk([64, NH, 64], BF16, "W_bf")
    QT_f = perchunk([64, NH, 64], F32, "QT_f")
    O0_f = perchunk([64, NH, 64], F32, "O0_f")
    MT_f = perchunk([64, NH, 64], F32, "MT_f")
    N0_f = perchunk([64, NH, 64], F32, "N0_f")
    diagP = perchunk([64, NH, 64], F32, "diagP")

    bS = p.psum([128, NH, 128], F32, name="bS")
    bX = p.psum([128, NH, 128], F32, name="bX")
    bY = p.psum([128, NH, 128], F32, name="bY")
    bP = p.psum([128, 2, NH, 64], F32, name="bP", ndeps=2)
    bQ = p.psum([128, 2, NH, 64], F32, name="bQ", ndeps=2)
    bO = p.psum([128, 2, NH, 64], F32, name="bO", ndeps=2)
    bN = p.psum([128, 2, NH, 64], F32, name="bN", ndeps=2)
    bH = p.psum([128, 2, NH, 64], F32, name="bH", ndeps=2)

    def flat(ap):
        return ap.rearrange("p a b -> p (a b)")

    def load_group(g):
        b = g % NB
        p.dma("sync", FMg[b][:], fm[g], W=[FMg[b]])
        p.dma("sync", TMg[b][:], tm[g], W=[TMg[b]])
        p.dma("sync", UVg[b][64:128], vv[g], W=[UVv[b]])
        p.dma("sync", ZVg[b][64:128], vv[g], W=[ZVg[b]])

    def pre(n):
        g, c = divmod(n, G)
        b = g % NB
        i = n % 2
        FM = FMg[b]
        TM = TMg[b]
        UV = UVg[b]
        for h in range(NH):
            p.mm(bS[:, h, :], flat(FM[:, c, h, 2:4, :]), flat(FM[:, c, h, 0:2, :]), True, True, R=[FM], W=[bS])
        p.tt("vector", S_f[i][:], bS[:], maskS4[:], ALU.mult, R=[bS, maskS4], W=[S_f[i]])
        p.cp("gpsimd", S_bf[i][:], S_f[i][:], R=[S_f[i]], W=[S_bf[i]])
        if LIMIT < 2:
            return
        for h in range(NH):
            p.mm(bQ[0:64, 0, h, :], FM[:, c, h, 0, :], FM[:, c, h, 2, :], True, True, R=[FM], W=[bQ.deps[0]])
        Pj, PjT = Pa[0], PaT[0]
        p.tt("vector", Pj[:], bQ[0:64, 0], maskL4[:], ALU.mult, R=[bQ.deps[0], maskL4], W=[Pj])
        p.cp("scalar", PjT[:], S_f[i][0:64, :, 0:64], R=[S_f[i]], W=[PjT])
        if LIMIT < 3:
            return
        for h in range(NH):
            p.mm(bX[0:64, h, 0:64], FM[:, c, h, 0, :], ident_bf[:], True, True, R=[FM, ident_bf], W=[bX])
            p.mm(bX[0:64, h, 64:128], S_bf[i][:, h, 0:64], ZVg[b][:, c, h, :], True, True,
                 R=[S_bf[i], ZVg[b]], W=[bX])
        Y = Ya[0]
        p.cp("scalar", Y[:], bX[0:64], R=[bX], W=[Y])
        if LIMIT < 4:
            return
        for j in range(6):
            Pj, PjT = Pa[j % 2], PaT[j % 2]
            Pn, PnT = Pa[(j + 1) % 2], PaT[(j + 1) % 2]
            Yc, Yn = Ya[j % 2], Ya[(j + 1) % 2]
            for h in range(NH):
                p.mm(bY[0:64, h, :], PjT[:, h, :], Yc[:, h, :], True, True, R=[PjT, Yc], W=[bY])
            p.tt("vector", Yn[:], bY[0:64], Yc[:], ALU.add, R=[bY, Yc], W=[Yn])
            if j < 5:
                for h in range(NH):
                    p.mm(bP[0:64, 1, h, :], Pj[:, h, :], PjT[:, h, :], True, True, R=[Pj, PjT], W=[bP.deps[1]])
                p.cp("scalar", PnT[:], bP[0:64, 1], R=[bP.deps[1]], W=[PnT])
                if j < 4:
                    for h in range(NH):
                        p.mm(bP[0:64, 0, h, :], PjT[:, h, :], Pj[:, h, :], True, True, R=[Pj, PjT], W=[bP.deps[0]])
                    p.cp("scalar", Pn[:], bP[0:64, 0], R=[bP.deps[0]], W=[Pn])
        Y6 = Ya[0]
        if LIMIT < 5:
            return
        p.cp("gpsimd", W_bf[i][:], Y6[:, :, 0:64], R=[Y6], W=[W_bf[i]])
        p.cp("gpsimd", UV[0:64, c], Y6[:, :, 64:128], R=[Y6], W=[UVu[b][c]])
        p.tt("gpsimd", diagP[i][:], identF4[:], pc_sb[:, n, :].unsqueeze(2).broadcast_to([64, NH, 64]), ALU.mult,
             R=[identF4, pc_sb], W=[diagP[i]])
        if LIMIT < 6:
            return
        for h in range(NH):
            p.mm(bQ[0:64, 1, h, :], W_bf[i][:, h, :], S_bf[i][0:64, h, 64:128], True, False, R=[W_bf[i], S_bf[i]], W=[bQ.deps[1]])
            p.mm(bQ[0:64, 1, h, :], ident_bf[:], FM[:, c, h, 1, :], False, True, R=[ident_bf, FM], W=[bQ.deps[1]])
        p.cp("scalar", QT_f[i][:], bQ[0:64, 1], R=[bQ.deps[1]], W=[QT_f[i]])
        if LIMIT < 7:
            return
        for h in range(NH):
            p.mm(bO[0:64, 0, h, :], UV[:, c, h, :], S_bf[i][:, h, 64:128], True, True,
                 R=[UVu[b][c], UVv[b], S_bf[i]], W=[bO.deps[0]])
        p.cp("scalar", O0_f[i][:], bO[0:64, 0], R=[bO.deps[0]], W=[O0_f[i]])
        if LIMIT < 8:
            return
        for h in range(NH):
            p.mm(bO[0:64, 1, h, :], W_bf[i][:, h, :], TM[0:64, c, h, :], True, True, R=[W_bf[i], TM], W=[bO.deps[1]])
        p.tt("vector", MT_f[i][:], bO[0:64, 1], diagP[i][:], ALU.add, R=[bO.deps[1], diagP[i]], W=[MT_f[i]])
        if LIMIT < 9:
            return
        for h in range(NH):
            p.mm(bN[0:64, 0, h, :], TM[:, c, h, :], UV[:, c, h, :], True, True,
                 R=[TM, UVu[b][c], UVv[b]], W=[bN.deps[0]])
        p.cp("scalar", N0_f[i][:], bN[0:64, 0], R=[bN.deps[0]], W=[N0_f[i]])

    def seq(n):
        if LIMIT < 10:
            return
        g, c = divmod(n, G)
        b = g % NB
        i = n % 2
        H0, H1 = Hs[n % 2], Hs[(n + 1) % 2]
        for h in range(NH):
            p.mm(bH[0:64, 0, h, :], H0[:, h, :], QT_f[i][:, h, :], True, True, R=[H0, QT_f[i]], W=[bH.deps[0]])
        for h in range(NH):
            p.mm(bH[0:64, 1, h, :], MT_f[i][:, h, :], H0[:, h, :], True, True, R=[H0, MT_f[i]], W=[bH.deps[1]])
        p.tt("vector", H1[:], bH[0:64, 1], N0_f[i][:], ALU.add, R=[bH.deps[1], N0_f[i]], W=[H1])
        p.tt("vector", OUTg[b][:, c], bH[0:64, 0], O0_f[i][:], ALU.add, R=[bH.deps[0], O0_f[i]], W=[OUTg[b]])
        if c == G - 1:
            p.dma("sync", yT[g], OUTg[b][:], R=[OUTg[b]])

    load_group(0)
    for n in range(NCH):
        g, c = divmod(n, G)
        if c == 0 and g + 1 < NG:
            load_group(g + 1)
        pre(n)
        if n > 0:
            seq(n - 1)
    seq(NCH - 1)
    return p.finish()


def consts():
    cst = np.zeros((128, 3, 128), np.float32)
    s = np.arange(64)[:, None]
    t = np.arange(64)[None, :]
    strict = (s < t).astype(np.float32)
    incl = (s <= t).astype(np.float32)
    cst[0:64, 0, 0:64] = strict
    cst[0:64, 0, 64:128] = incl
    cst[64:128, 0, 0:64] = strict
    cst[64:128, 0, 64:128] = incl
    cst[0:64, 1, 0:64] = (np.arange(64)[:, None] > np.arange(64)[None, :]).astype(np.float32)
    cst[:, 2, :] = np.eye(128, dtype=np.float32)
    return cst

import numpy as np

D = 2048
NC16 = 16
TP = 512
DFF = 8192
RMS_EPS = 1e-6
LNX_EPS = 1e-5 * 64

PV = {n: i for i, n in enumerate(
    ["mix_norm", "ffn_norm", "aux_norm", "mix_r", "mix_k", "mix_v", "mix_w", "mix_a", "mix_g",
     "w0", "a0", "v0", "k_k", "k_a", "r_k", "lnx_w", "lnx_b"])}
NPV = len(PV)


class TS:
    def __init__(self, cfg):
        self.cfg = cfg
        self.p = p = Prog()
        NP = cfg["npass"]
        self.NP = NP
        dr = lambda n, s, dt=F32, kind="ExternalInput": p.dram(n, s, dt, kind)
        self.xin = dr("xT", [NP, 128, NC16, TP + 1])
        self.pv_d = dr("pv", [128, NPV, NC16])
        self.cst_d = dr("cst", [128, 4, 128])
        self.rmask_d = dr("rmask", [128, TP])
        p_ = cfg.get("pre")
        po = cfg.get("post")
        if po == "rwkv":
            self.yin = dr("yT", [NP, 128, NC16, TP])
            self.bonus_in = dr("bonus", [NP, 128, NC16, TP])
            self.g_in = dr("gT", [NP, 128, NC16, TP])
            self.w_o = dr("w_o", [D, D])
        if po == "fox":
            self.oin = dr("oT", [NP, 128, NC16, TP])
            self.w_o = dr("w_o", [D, D])
        if cfg.get("mlp"):
            self.w_up = dr("w_up", [D, DFF])
            self.w_down = dr("w_down", [DFF, D])
        if po or cfg.get("mlp"):
            self.xout = dr("xT_out", [NP, 128, NC16, TP], kind="ExternalOutput")
        if p_ == "rwkv":
            self.w_rkv = dr("w_rkv", [3, D, D])
            self.w1 = dr("w1", [D, 96]); self.w2 = dr("w2", [96, D])
            self.a1 = dr("a1", [D, 96]); self.a2 = dr("a2", [96, D])
            self.g1 = dr("g1", [D, 256]); self.g2 = dr("g2", [256, D])
            if cfg.get("vres"):
                self.v1 = dr("v1", [D, 64]); self.v2 = dr("v2", [64, D])
                self.vfirst_in = dr("vfirst", [NP, 128, NC16, TP])
            self.fm_out = dr("fm_out", [NP, NC16, 128, 4, TP], BF16, "ExternalOutput")
            self.tm_out = dr("tm_out", [NP, NC16, 4, 128, 3, 128], BF16, "ExternalOutput")
            self.pc_out = dr("pc_out", [NP, 128, NC16, TP // 64], F32, "ExternalOutput")
            self.bonus_out = dr("bonus_out", [NP, 128, NC16, TP], F32, "ExternalOutput")
            self.g_out = dr("g_out", [NP, 128, NC16, TP], F32, "ExternalOutput")
            self.v_out = dr("v_out", [NP, 128, NC16, TP], F32, "ExternalOutput")
        if p_ == "kvq":
            self.w_kvf = dr("w_kvf", [D, 2 * D + 16])
            self.bf_d = dr("b_f", [16, 1])
            self.kT_out = dr("kT_out", [NP, NC16, 128, TP], BF16, "ExternalOutput")
            self.vtm_out = dr("vtm_out", [NP, NC16, 4, 128, 128], BF16, "ExternalOutput")
            self.lf_out = dr("lf_out", [NP, 16, TP], F32, "ExternalOutput")
        if p_ in ("kvq", "q"):
            self.w_q = dr("w_q", [D, D])
            self.qT_out = dr("qT_out", [NP, NC16, 128, TP], BF16, "ExternalOutput")
        if p_ == "final":
            self.fin_out = dr("fin_out", [NP, 128, NC16, TP], F32, "ExternalOutput")
        self.build()

    def ps(self):
        self._psi = (self._psi + 1) % len(self.PS)
        return self.PS[self._psi]

    def linear(self, W, KC, kp, mcols, rhs_fn, rhs_deps, consume, c0=0):
        p = self.p
        for (m0, mw) in mcols:
            self._wi = (self._wi + 1) % len(self.WB)
            wb = self.WB[self._wi]
            for k0 in range(0, KC, 16):
                k1 = min(KC, k0 + 16)
                src = W[k0 * kp:k1 * kp, m0:m0 + mw].rearrange("(c p) n -> p c n", p=kp)
                p.dma("gpsimd", wb[0:kp, k0:k1, 0:mw], src, W=[wb])
            acc = self.ps()
            for c in range(KC):
                p.mm(acc[0:mw, :], wb[0:kp, c, 0:mw], rhs_fn(c), c == 0, c == KC - 1, R=[wb] + rhs_deps, W=[acc])
            consume(m0, mw, acc)

    def rmsnorm(self, xT, gain_slot, out_bf, lo, hi, out_off=0):
        p = self.p
        n = hi - lo
        acc = self.ps()
        for c in range(NC16):
            sq = self.SQ[c % 2]
            p.act(sq[:, 0:n], xT[:, c, lo:hi], AF.Square, R=[xT], W=[sq])
            p.mm(acc[:, 0:n], self.ones_bf[:], sq[:, 0:n], c == 0, c == NC16 - 1, R=[sq, self.ones_bf], W=[acc])
        rstd = self.rstd
        p.act(rstd[:, 0:n], acc[:, 0:n], AF.Sqrt, R=[acc, self.eps_rms], W=[rstd], scale=1.0 / D, bias=self.eps_rms[:])
        p.op("vector", lambda e: e.reciprocal(rstd[:, 0:n], rstd[:, 0:n]), R=[rstd], W=[rstd])
        for c in range(NC16):
            p.stt(out_bf[:, c, out_off + lo:out_off + hi], xT[:, c, lo:hi], self.pv[:, gain_slot, c:c + 1], rstd[:, 0:n],
                  ALU.mult, ALU.mult, R=[xT, self.pv, rstd], W=[out_bf])

    def build(self):
        p = self.p
        cfg = self.cfg
        self.PS = [p.psum([128, 512], F32, name=f"PS{i}") for i in range(7)]
        self._psi = -1
        self.WB = [p.sbuf([128, 16, 128], BF16, name=f"WB{i}") for i in range(3)]
        self._wi = -1
        self.pv = p.sbuf([128, NPV, NC16], F32, name="pv_sb")
        p.dma("sync", self.pv[:], self.pv_d, W=[self.pv])
        cst = p.sbuf([128, 4, 128], F32, name="cst_sb")
        p.dma("sync", cst[:], self.cst_d, W=[cst])
        self.ones_bf = p.sbuf([128, 128], BF16, name="ones_bf")
        p.cp("vector", self.ones_bf[:], cst[:, 0, :], R=[cst], W=[self.ones_bf])
        self.blk_f = cst
        self.ident_bf = p.sbuf([128, 128], BF16, name="ident_bf")
        p.cp("vector", self.ident_bf[:], cst[:, 2, :], R=[cst], W=[self.ident_bf])
        self.cst = cst
        self.eps_rms = p.sbuf([128, 1], F32, name="eps_rms")
        p.memset("vector", self.eps_rms[:], RMS_EPS, W=[self.eps_rms])
        self.eps_lnx = p.sbuf([128, 1], F32, name="eps_lnx")
        p.memset("vector", self.eps_lnx[:], LNX_EPS, W=[self.eps_lnx])
        self.SQ = [p.sbuf([128, 512], BF16, name=f"SQ{i}") for i in range(2)]
        self.rstd = p.sbuf([128, 513], F32, name="rstd")
        self.xT = p.sbuf([128, NC16, TP + 1], F32, name="xT_sb")
        isr = cfg.get("pre") == "rwkv"
        if isr:
            self.hbf = p.sbuf([128, NC16, TP + 1], BF16, name="hbf")
        self.T = [p.sbuf([128, TP], F32, name=f"T{i}") for i in range(12 if isr else 8)]
        self.B = [p.sbuf([128, NC16, TP], BF16, name=f"B{i}") for i in range(2 if isr else 1)]
        if not isr:
            self.B.append(self.B[0])
        if cfg.get("mlp"):
            self.hid = p.sbuf([128, 64, TP], BF16, name="hid")
            self.WD = [p.sbuf([128, 64, 128], BF16, name=f"WD{i}") for i in range(2)]
        for ps_ in range(self.NP):
            self.one_pass(ps_)
        self.nc = p.finish()

    def one_pass(self, ip):
        p = self.p
        cfg = self.cfg
        xT = self.xT
        p.dma("sync", xT[:], self.xin[ip], W=[xT])
        if cfg.get("post") == "rwkv":
            self.post_rwkv(ip)
        if cfg.get("post") == "fox":
            self.post_fox(ip)
        if cfg.get("mlp"):
            self.mlp(ip)
        if cfg.get("post") or cfg.get("mlp"):
            p.dma("sync", self.xout[ip], xT[:, :, 1:TP + 1], R=[xT])
        pre = cfg.get("pre")
        if pre == "rwkv":
            self.pre_rwkv(ip)
        if pre == "kvq":
            self.pre_kv(ip)
        if pre in ("kvq", "q"):
            self.pre_q(ip)
        if pre == "final":
            self.final(ip)

    def add_to_x(self, m0, mw, acc):
        p = self.p
        c = m0 // 128
        p.tt("vector", self.xT[:, c, 1:TP + 1], self.xT[:, c, 1:TP + 1], acc[:, :], ALU.add, R=[acc, self.xT], W=[self.xT])

    def post_fox(self, ip):
        p = self.p
        z = self.B[0]
        st = self.T[0]
        for c in range(NC16):
            p.dma("sync", st[:], self.oin[ip, :, c, :], W=[st])
            p.cp("vector", z[:, c, :], st[:], R=[st], W=[z])
        self.linear(self.w_o, NC16, 128, [(m * 128, 128) for m in range(NC16)], lambda c: z[:, c, :], [z], self.add_to_x)

    def post_rwkv(self, ip):
        p = self.p
        z = self.B[0]
        y, bo, g, t1, t2, mu, rs = self.T[0:7]
        blk = self.cst[:, 1, :]
        for c in range(NC16):
            p.dma("sync", y[:], self.yin[ip, :, c, :], W=[y])
            p.dma("sync", bo[:], self.bonus_in[ip, :, c, :], W=[bo])
            p.dma("sync", g[:], self.g_in[ip, :, c, :], W=[g])
            a1 = self.ps()
            p.mm(a1[:], blk, y[:], True, True, R=[self.cst, y], W=[a1])
            p.stt(t1[:], a1[:], -1.0 / 64, y[:], ALU.mult, ALU.add, R=[a1, y], W=[t1])
            p.act(t2[:], t1[:], AF.Square, R=[t1], W=[t2])
            a2 = self.ps()
            p.mm(a2[:], blk, t2[:], True, True, R=[self.cst, t2], W=[a2])
            p.act(rs[:], a2[:], AF.Sqrt, R=[a2, self.eps_lnx], W=[rs], scale=1.0 / 64, bias=self.eps_lnx[:])
            p.op("vector", lambda e: e.reciprocal(rs[:], rs[:]), R=[rs], W=[rs])
            p.tt("vector", t1[:], t1[:], rs[:], ALU.mult, R=[t1, rs], W=[t1])
            p.ts("vector", t1[:], t1[:], self.pv[:, PV["lnx_w"], c:c + 1], self.pv[:, PV["lnx_b"], c:c + 1], ALU.mult, ALU.add,
                 R=[t1, self.pv], W=[t1])
            p.tt("vector", t1[:], t1[:], bo[:], ALU.add, R=[t1, bo], W=[t1])
            p.tt("vector", z[:, c, :], t1[:], g[:], ALU.mult, R=[t1, g], W=[z])
        self.linear(self.w_o, NC16, 128, [(m * 128, 128) for m in range(NC16)], lambda c: z[:, c, :], [z], self.add_to_x)

    def mlp(self, ip):
        p = self.p
        h2 = self.B[1]
        self.rmsnorm(self.xT, PV["ffn_norm"], h2, 1, TP + 1, out_off=-1)
        hid = self.hid
        t = self.T[7]

        def up_consume(m0, mw, acc):
            m = m0 // 128
            p.act(t[:], acc[:], AF.Relu, R=[acc], W=[t])
            p.tt("vector", hid[:, m, :], t[:], t[:], ALU.mult, R=[t], W=[hid])

        self.linear(self.w_up, NC16, 128, [(m * 128, 128) for m in range(64)], lambda c: h2[:, c, :], [h2], up_consume)
        for m in range(NC16):
            wb = self.WD[m % 2]
            for k0 in range(0, 64, 16):
                src = self.w_down[k0 * 128:(k0 + 16) * 128, m * 128:(m + 1) * 128].rearrange("(c p) n -> p c n", p=128)
                p.dma("gpsimd", wb[:, k0:k0 + 16, :], src, W=[wb])
            acc = self.ps()
            for c in range(64):
                p.mm(acc[:], wb[:, c, :], hid[:, c, :], c == 0, c == 63, R=[wb, hid], W=[acc])
            self.add_to_x(m * 128, 128, acc)

    def final(self, ip):
        p = self.p
        xT = self.xT
        acc = self.ps()
        for c in range(NC16):
            sq = self.SQ[c % 2]
            p.act(sq[:], xT[:, c, 1:TP + 1], AF.Square, R=[xT], W=[sq])
            p.mm(acc[:], self.ones_bf[:], sq[:], c == 0, c == NC16 - 1, R=[sq, self.ones_bf], W=[acc])
        rstd = self.rstd
        p.act(rstd[:, 0:TP], acc[:], AF.Sqrt, R=[acc, self.eps_rms], W=[rstd], scale=1.0 / D, bias=self.eps_rms[:])
        p.op("vector", lambda e: e.reciprocal(rstd[:, 0:TP], rstd[:, 0:TP]), R=[rstd], W=[rstd])
        for c in range(NC16):
            o = self.T[c % 4]
            p.stt(o[:], xT[:, c, 1:TP + 1], self.pv[:, PV["aux_norm"], c:c + 1], rstd[:, 0:TP], ALU.mult, ALU.mult,
                  R=[xT, self.pv, rstd], W=[o])
            p.dma("sync", self.fin_out[ip, :, c, :], o[:], R=[o])

    def transpose_out(self, src_bf, dst, R):
        p = self.p
        pt = self.PT
        for tb in range(4):
            p.op("tensor", lambda e, tb=tb: e.transpose(pt[:, tb, :], src_bf[:, tb * 128:(tb + 1) * 128], self.ident_bf[:]),
                 R=R + [self.ident_bf], W=[pt])
        sb = self.TSB[self._tsi % 2]
        self._tsi += 1
        p.cp("scalar", sb[:], pt[:], R=[pt], W=[sb])
        p.dma("sync", dst.rearrange("t p f -> p t f"), sb[:], R=[sb])

    def pre_q(self, ip):
        p = self.p
        if not hasattr(self, "OB"):
            self.alloc_out_bufs()
        h = self.B[0]
        self.rmsnorm(self.xT, PV["mix_norm"], h, 1, TP + 1, out_off=-1)
        scale = 128 ** -0.5

        def consume(m0, mw, acc):
            m = m0 // 128
            o = self.OB[m % 2]
            p.act(o[:], acc[:], AF.Copy, R=[acc], W=[o], scale=scale)
            p.dma("sync", self.qT_out[ip, m], o[:], R=[o])

        self.linear(self.w_q, NC16, 128, [(m * 128, 128) for m in range(NC16)], lambda c: h[:, c, :], [h], consume)

    def pre_kv(self, ip):
        p = self.p
        if not hasattr(self, "OB"):
            self.alloc_out_bufs()
        h = self.B[1]
        self.rmsnorm(self.xT, PV["aux_norm"], h, 1, TP + 1, out_off=-1)

        def consume_k(m0, mw, acc):
            m = m0 // 128
            o = self.OB[m % 2]
            p.cp("scalar", o[:], acc[:], R=[acc], W=[o])
            p.dma("sync", self.kT_out[ip, m], o[:], R=[o])

        def consume_v(m0, mw, acc):
            m = (m0 - D) // 128
            o = self.OB[m % 2]
            p.cp("scalar", o[:], acc[:], R=[acc], W=[o])
            self.transpose_out(o, self.vtm_out[ip, m], [o])

        def consume_f(m0, mw, acc):
            t, t2 = self.T[0], self.T[1]
            p.act(t[0:16, :], acc[0:16, :], AF.Sigmoid, R=[acc, self.bf_sb], W=[t], bias=self.bf_sb[:])
            p.act(t2[0:16, :], t[0:16, :], AF.Ln, R=[t], W=[t2])
            p.dma("sync", self.lf_out[ip], t2[0:16, :], R=[t2])

        self.linear(self.w_kvf, NC16, 128, [(m * 128, 128) for m in range(NC16)], lambda c: h[:, c, :], [h], consume_k)
        self.linear(self.w_kvf, NC16, 128, [(D + m * 128, 128) for m in range(NC16)], lambda c: h[:, c, :], [h], consume_v)
        self.linear(self.w_kvf, NC16, 128, [(2 * D, 16)], lambda c: h[:, c, :], [h], consume_f)

    def alloc_out_bufs(self):
        p = self.p
        self.OB = [p.sbuf([128, TP], BF16, name=f"OB{i}") for i in range(2)]
        self.PT = p.psum([128, 4, 128], BF16, name="PTb")
        self.TSB = [p.sbuf([128, 4, 128], BF16, name=f"TSB{i}") for i in range(2)]
        self._tsi = 0
        if self.cfg.get("pre") == "kvq":
            self.bf_sb = p.sbuf([16, 1], F32, name="bf_sb")
            p.dma("sync", self.bf_sb[:], self.bf_d, W=[self.bf_sb])

    def pre_rwkv(self, ip):
        p = self.p
        cfg = self.cfg
        if not hasattr(self, "OB"):
            self.alloc_out_bufs()
            self.rmask = p.sbuf([128, TP], F32, name="rmask_sb")
            p.dma("sync", self.rmask[:], self.rmask_d, W=[self.rmask])
            self.dbf = p.sbuf([128, NC16, TP], BF16, name="dbf")
            self.XV = p.sbuf([128, NC16, TP], BF16, name="XV")
            self.lora = {n: p.sbuf([128, 2, TP], BF16, name="lo_" + n) for n in ("w", "a", "v", "g")}
            self.W2 = {n: p.sbuf([128, 2 if n == "g" else 1, D], BF16, name="w2_" + n) for n in ("w", "a", "v", "g")}
            self.FMo = [p.sbuf([128, 4, TP], BF16, name=f"FMo{i}") for i in range(2)]
            self.pco = p.sbuf([128, NC16, TP // 64], F32, name="pco")
            self.TM3 = [p.sbuf([128, 3, TP], BF16, name=f"TM3{i}") for i in range(2)]
            for n, w, kk in (("w", self.w2, 96), ("a", self.a2, 96), ("g", self.g2, 256)) + (
                    (("v", self.v2, 64),) if cfg.get("vres") else ()):
                for j in range((kk + 127) // 128):
                    r0, r1 = j * 128, min(kk, (j + 1) * 128)
                    p.dma("gpsimd", self.W2[n][0:r1 - r0, j, :], w[r0:r1, :], W=[self.W2[n]])
        xT, hbf, dbf = self.xT, self.hbf, self.dbf
        pv = self.pv
        self.rmsnorm(xT, PV["mix_norm"], hbf, 0, 1)
        self.rmsnorm(xT, PV["mix_norm"], hbf, 1, TP + 1)
        for c in range(NC16):
            p.tt("vector", dbf[:, c, :], hbf[:, c, 0:TP], hbf[:, c, 1:TP + 1], ALU.subtract, R=[hbf], W=[dbf])

        def make_xs(slot, dst):
            for c in range(NC16):
                p.stt(dst[:, c, :], dbf[:, c, :], pv[:, slot, c:c + 1], hbf[:, c, 1:TP + 1], ALU.mult, ALU.add,
                      R=[dbf, pv, hbf], W=[dst])

        def lora1(name, slot, W, width, func):
            xs = self.B[0]
            make_xs(slot, xs)
            lo = self.lora[name]

            def consume(m0, mw, acc):
                j = m0 // 128
                p.act(lo[0:mw, j, :], acc[0:mw, :], func, R=[acc], W=[lo])

            self.linear(W, NC16, 128, [(j * 128, min(128, width - j * 128)) for j in range((width + 127) // 128)],
                        lambda c: xs[:, c, :], [xs], consume)

        lora1("w", PV["mix_w"], self.w1, 96, AF.Tanh)
        lora1("a", PV["mix_a"], self.a1, 96, AF.Copy)
        lora1("g", PV["mix_g"], self.g1, 256, AF.Sigmoid)
        xr, xk, xv = self.B[0], self.B[1], self.XV
        make_xs(PV["mix_v"], xv)
        if cfg.get("vres"):
            lo = self.lora["v"]

            def consume_v1(m0, mw, acc):
                p.cp("scalar", lo[0:mw, 0, :], acc[0:mw, :], R=[acc], W=[lo])

            self.linear(self.v1, NC16, 128, [(0, 64)], lambda c: xv[:, c, :], [xv], consume_v1)
        make_xs(PV["mix_r"], xr)
        make_xs(PV["mix_k"], xk)

        T = self.T
        blk = self.cst[:, 1, :]
        for m in range(NC16):
            res = {}

            def grab(name):
                def consume(m0, mw, acc):
                    res[name] = acc
                return consume

            mc = [(m * 128, 128)]
            self.linear(self.w_rkv[0], NC16, 128, mc, lambda c: xr[:, c, :], [xr], grab("r"))
            self.linear(self.w_rkv[1], NC16, 128, mc, lambda c: xk[:, c, :], [xk], grab("k"))
            self.linear(self.w_rkv[2], NC16, 128, mc, lambda c: xv[:, c, :], [xv], grab("v"))
            r_f, k_f, v_f, lw, cum, al, kk, t1, t2, t3, g_f, bon = T[0:12]
            p.cp("scalar", r_f[:], res["r"][:], R=[res["r"]], W=[r_f])
            p.cp("scalar", k_f[:], res["k"][:], R=[res["k"]], W=[k_f])
            p.cp("scalar", v_f[:], res["v"][:], R=[res["v"]], W=[v_f])
            def lora2(name, kk_, nj):
                acc = self.ps()
                for j in range(nj):
                    kp = min(128, kk_ - j * 128)
                    p.mm(acc[:], self.W2[name][0:kp, j, m * 128:(m + 1) * 128], self.lora[name][0:kp, j, :], j == 0, j == nj - 1,
                         R=[self.W2[name], self.lora[name]], W=[acc])
                return acc
            aw = lora2("w", 96, 1)
            p.act(lw[:], aw[:], AF.Sigmoid, R=[aw, pv], W=[lw], bias=pv[:, PV["w0"], m:m + 1])
            p.ts("vector", lw[:], lw[:], -float(np.exp(-0.5)), None, ALU.mult, R=[lw], W=[lw])
            aa = lora2("a", 96, 1)
            p.act(al[:], aa[:], AF.Sigmoid, R=[aa, pv], W=[al], bias=pv[:, PV["a0"], m:m + 1])
            ag = lora2("g", 256, 2)
            p.cp("scalar", g_f[:], ag[:], R=[ag], W=[g_f])
            p.dma("sync", self.g_out[ip, :, m, :], g_f[:], R=[g_f])
            if cfg.get("vres"):
                av = lora2("v", 64, 1)
                p.act(t1[:], av[:], AF.Sigmoid, R=[av, pv], W=[t1], bias=pv[:, PV["v0"], m:m + 1])
                p.dma("sync", t2[:], self.vfirst_in[ip, :, m, :], W=[t2])
                p.tt("vector", t2[:], t2[:], v_f[:], ALU.subtract, R=[t2, v_f], W=[t2])
                p.tt("vector", t2[:], t2[:], t1[:], ALU.mult, R=[t2, t1], W=[t2])
                p.tt("vector", v_f[:], v_f[:], t2[:], ALU.add, R=[v_f, t2], W=[v_f])
            p.dma("sync", self.v_out[ip, :, m, :], v_f[:], R=[v_f])
            p.op("vector", lambda e, cum=cum, lw=lw: e.tensor_tensor_scan(cum[:], self.rmask[:], lw[:], 0.0, ALU.mult, ALU.add),
                 R=[self.rmask, lw], W=[cum])
            p.ts("vector", kk[:], k_f[:], pv[:, PV["k_k"], m:m + 1], None, ALU.mult, R=[k_f, pv], W=[kk])
            p.act(t1[:], kk[:], AF.Square, R=[kk], W=[t1])
            a1 = self.ps()
            p.mm(a1[:], blk, t1[:], True, True, R=[self.cst, t1], W=[a1])
            p.act(t1[:], a1[:], AF.Sqrt, R=[a1], W=[t1])
            p.ts("vector", t1[:], t1[:], 1e-12, None, ALU.max, R=[t1], W=[t1])
            p.op("vector", lambda e, t1=t1: e.reciprocal(t1[:], t1[:]), R=[t1], W=[t1])
            p.tt("vector", kk[:], kk[:], t1[:], ALU.mult, R=[kk, t1], W=[kk])
            p.ts("vector", t1[:], al[:], -1.0, pv[:, PV["k_a"], m:m + 1], ALU.add, ALU.mult, R=[al, pv], W=[t1])
            p.stt(k_f[:], t1[:], 1.0, k_f[:], ALU.add, ALU.mult, R=[t1, k_f], W=[k_f])
            p.stt(t1[:], r_f[:], pv[:, PV["r_k"], m:m + 1], k_f[:], ALU.mult, ALU.mult, R=[r_f, pv, k_f], W=[t1])
            a2 = self.ps()
            p.mm(a2[:], blk, t1[:], True, True, R=[self.cst, t1], W=[a2])
            p.tt("vector", bon[:], a2[:], v_f[:], ALU.mult, R=[a2, v_f], W=[bon])
            p.dma("sync", self.bonus_out[ip, :, m, :], bon[:], R=[bon])
            p.tt("vector", t3[:], kk[:], al[:], ALU.mult, R=[kk, al], W=[t3])
            fmo = self.FMo[m % 2]
            tm3 = self.TM3[m % 2]
            p.act(t1[:], cum[:], AF.Exp, R=[cum], W=[t1])
            p.tt("vector", fmo[:, 1, :], r_f[:], t1[:], ALU.mult, R=[r_f, t1], W=[fmo])
            p.cp("vector", self.pco[:, m, :], t1[:, 63::64], R=[t1], W=[self.pco])
            p.tt("vector", t2[:], cum[:], lw[:], ALU.subtract, R=[cum, lw], W=[t2])
            p.act(t2[:], t2[:], AF.Exp, R=[t2], W=[t2])
            p.stt(fmo[:, 0, :], kk[:], -1.0, t2[:], ALU.mult, ALU.mult, R=[kk, t2], W=[fmo])
            p.act(t1[:], cum[:], AF.Exp, R=[cum], W=[t1], scale=-1.0)
            p.tt("vector", fmo[:, 2, :], t3[:], t1[:], ALU.mult, R=[t3, t1], W=[fmo])
            p.tt("vector", fmo[:, 3, :], k_f[:], t1[:], ALU.mult, R=[k_f, t1], W=[fmo])
            p.dma("sync", self.fm_out[ip, m], fmo[:], R=[fmo])
            cumC = cum[:, 63::64].unsqueeze(2).broadcast_to([128, TP // 64, 64])
            p.tt("vector", t2[:].rearrange("p (a b) -> p a b", b=64), cumC, cum[:].rearrange("p (a b) -> p a b", b=64),
                 ALU.subtract, R=[cum], W=[t2])
            p.act(t2[:], t2[:], AF.Exp, R=[t2], W=[t2])
            p.tt("vector", tm3[:, 0, :], t3[:], t2[:], ALU.mult, R=[t3, t2], W=[tm3])
            p.tt("vector", tm3[:, 1, :], k_f[:], t2[:], ALU.mult, R=[k_f, t2], W=[tm3])
            p.cp("vector", tm3[:, 2, :], v_f[:], R=[v_f], W=[tm3])
            for q in range(3):
                self.transpose_out(tm3[:, q, :], self.tm_out[ip, m, :, :, q, :], [tm3])
        p.dma("sync", self.pc_out[ip], self.pco[:], R=[self.pco])

import numpy as np

NEG = -30000.0


def build_astage(S=8192, NHA=2):
    p = Prog()
    NJ = S // 128
    NI = S // 512
    qT = p.dram("qT", [128, NHA, S], BF16, "ExternalInput")
    kT = p.dram("kT", [128, NHA, S], BF16, "ExternalInput")
    vtm = p.dram("vtm", [128, NHA, NJ, 128], BF16, "ExternalInput")
    lfc = p.dram("lfc", [128, NHA, NJ], F32, "ExternalInput")
    cst = p.dram("cst", [128, 4, 128], F32, "ExternalInput")
    maskb = p.dram("maskb", [128, 4, 512], F32, "ExternalInput")
    oT = p.dram("oT", [128, NHA, S], F32, "ExternalOutput")

    q_sb = p.sbuf([128, NHA, S], BF16, name="q_sb", ndeps=NHA)
    k_sb = p.sbuf([128, NHA, S], BF16, name="k_sb", ndeps=NHA)
    v_sb = p.sbuf([128, NHA, NJ, 128], BF16, name="v_sb", ndeps=NHA)
    q_sb.deps = [Dep() for _ in range(NHA)]
    k_sb.deps = [Dep() for _ in range(NHA)]
    v_sb.deps = [Dep() for _ in range(NHA)]
    cst_sb = p.sbuf([128, 4, 128], F32, name="cst_sb")
    p.dma("sync", cst_sb[:], cst, W=[cst_sb])
    lf_sb = p.sbuf([128, NHA, NJ], F32, name="lf_sb")
    p.dma("sync", lf_sb[:], lfc, W=[lf_sb])
    mk_f = p.sbuf([128, 4, 512], F32, name="mk_f")
    p.dma("sync", mk_f[:], maskb, W=[mk_f])
    for h in range(NHA):
        LC = min(S, 2048)
        for a in range(0, S, LC):
            p.dma("sync", q_sb[:, h, a:a + LC], qT[:, h, a:a + LC], W=[q_sb.deps[h]])
            p.dma("sync", k_sb[:, h, a:a + LC], kT[:, h, a:a + LC], W=[k_sb.deps[h]])
            p.dma("sync", v_sb[:, h, a // 128:(a + LC) // 128, :], vtm[:, h, a // 128:(a + LC) // 128, :], W=[v_sb.deps[h]])
    ones_bf = p.sbuf([128, 128], BF16, name="ones_bf")
    ident_bf = p.sbuf([128, 128], BF16, name="ident_bf")
    p.cp("vector", ones_bf[:], cst_sb[:, 0, :], R=[cst_sb], W=[ones_bf])
    p.cp("vector", ident_bf[:], cst_sb[:, 2, :], R=[cst_sb], W=[ident_bf])

    PSA = [p.psum([128, 512], F32, name=f"PSA{i}") for i in range(2)]
    PO = p.psum([128, 512], F32, name="PO")
    PD = p.psum([128, 512], F32, name="PD")
    PX = [p.psum([128, 512], F32, name=f"PX{i}") for i in range(2)]
    PTb = [p.sbuf([128, 512], BF16, name=f"PTb{i}") for i in range(3)]
    sq = [p.sbuf([128, 512], BF16, name=f"sq{i}") for i in range(2)]
    mx = p.sbuf([128, 2, 32], F32, name="mx")
    Mst = p.sbuf([128, 4], F32, name="Mst")
    c_col = p.sbuf([128, NJ], F32, name="c_col")
    offs = p.sbuf([128, NJ], F32, name="offs")
    rel = p.sbuf([128, NJ], F32, name="rel")
    refM = p.sbuf([128, NI], F32, name="refM")
    tot = p.sbuf([128, 1], F32, name="tot")
    totbc = p.sbuf([128, 128], F32, name="totbc")
    cbias = [p.sbuf([128, NJ], F32, name=f"cbias{i}") for i in range(2)]
    dg = [p.sbuf([128, 128], BF16, name=f"dg{i}") for i in range(2)]
    Roff = [p.sbuf([128, 512], BF16, name=f"Roff{i}") for i in range(2)]
    Rdg = [p.sbuf([128, 4, 512], BF16, name=f"Rdg{i}") for i in range(2)]
    rD = p.sbuf([128, 512], F32, name="rD")
    ob = [p.sbuf([128, 512], F32, name=f"ob{i}") for i in range(2)]

    for h in range(NHA):
        for which, src, dep in ((0, q_sb, q_sb.deps[h]), (1, k_sb, k_sb.deps[h])):
            for i in range(NI):
                s_ = sq[i % 2]
                p.act(s_[:], src[:, h, i * 512:(i + 1) * 512], AF.Square, R=[dep], W=[s_])
                px = PX[i % 2]
                p.mm(px[:], ones_bf[:], s_[:], True, True, R=[ones_bf, s_], W=[px])
                p.op("vector", lambda e, px=px, which=which, i=i: e.reduce_max(mx[:, which, i:i + 1], px[:], AX.X), R=[px], W=[mx])
            p.op("vector", lambda e, which=which: e.reduce_max(Mst[:, which:which + 1], mx[:, which, 0:NI], AX.X), R=[mx], W=[Mst])
        p.tt("vector", Mst[:, 2:3], Mst[:, 0:1], Mst[:, 1:2], ALU.mult, R=[Mst], W=[Mst])
        p.act(Mst[:, 3:4], Mst[:, 2:3], AF.Sqrt, R=[Mst], W=[Mst])
        px = PX[0]
        p.mm(px[0:NJ, 0:1], lf_sb[:, h, :], cst_sb[:, 0, 0:1], True, True, R=[lf_sb, cst_sb], W=[px])
        p.cp("vector", tot[0:NJ, :], px[0:NJ, 0:1], R=[px], W=[tot])
        p.ts("vector", totbc[0:NJ, :], cst_sb[0:NJ, 0, :], tot[0:NJ, 0:1], None, ALU.mult, R=[cst_sb, tot], W=[totbc])
        px = PX[1]
        p.mm(px[:, 0:NJ], totbc[0:NJ, :], cst_sb[0:NJ, 3, 0:NJ], True, True, R=[totbc, cst_sb], W=[px])
        p.cp("vector", offs[:], px[:, 0:NJ], R=[px], W=[offs])
        px = PX[0]
        p.mm(px[:, 0:NJ], cst_sb[:, 1, :], lf_sb[:, h, :], True, True, R=[cst_sb, lf_sb], W=[px])
        p.tt("vector", c_col[:], px[:, 0:NJ], offs[:], ALU.add, R=[px, offs], W=[c_col])
        p.tt("vector", rel[:].rearrange("p (a b) -> p a b", b=4), c_col[:].rearrange("p (a b) -> p a b", b=4),
             offs[:, 0::4].unsqueeze(2).broadcast_to([128, NI, 4]), ALU.subtract, R=[c_col, offs], W=[rel])
        p.ts("vector", refM[:], offs[:, 0::4], Mst[:, 3:4], None, ALU.subtract, R=[offs, Mst], W=[refM])

        for i in range(NI):
            cb = cbias[i % 2]
            p.ts("vector", cb[:], c_col[:], -1.0, refM[:, i:i + 1], ALU.mult, ALU.add, R=[c_col, refM], W=[cb])
            px = PX[i % 2]
            for jj in range(4):
                d_ = dg[jj % 2]
                p.ts("vector", d_[:], cst_sb[:, 2, :], rel[:, 4 * i + jj:4 * i + jj + 1], None, ALU.mult, R=[cst_sb, rel], W=[d_])
                p.mm(px[:, jj * 128:(jj + 1) * 128], ones_bf[:], d_[:], True, True, R=[ones_bf, d_], W=[px])
            ro = Roff[i % 2]
            p.cp("scalar", ro[:], px[:], R=[px], W=[ro])
            rd = Rdg[i % 2]
            for r in range(4):
                p.tt("vector", rd[:, r, :], ro[:], mk_f[:, r, :], ALU.add, R=[ro, mk_f], W=[rd])
            nj = 4 * i + 4
            for j in range(nj):
                ps = PSA[j % 2]
                p.mm(ps[:], k_sb[:, h, j * 128:(j + 1) * 128], q_sb[:, h, i * 512:(i + 1) * 512], True, False,
                     R=[k_sb.deps[h], q_sb.deps[h]], W=[ps])
                rterm = ro[:] if j < 4 * i else rd[:, j - 4 * i, :]
                p.mm(ps[:], ident_bf[:], rterm, False, True, R=[ident_bf, ro, rd], W=[ps])
                pt = PTb[j % 3]
                p.act(pt[:], ps[:], AF.Exp, R=[ps, cb], W=[pt], bias=cb[:, j:j + 1])
                p.mm(PO[:], v_sb[:, h, j, :], pt[:], j == 0, j == nj - 1, R=[v_sb.deps[h], pt], W=[PO])
                p.mm(PD[:], ones_bf[:], pt[:], j == 0, j == nj - 1, R=[ones_bf, pt], W=[PD])
            p.op("vector", lambda e: e.reciprocal(rD[:], PD[:]), R=[PD], W=[rD])
            o_ = ob[i % 2]
            p.tt("vector", o_[:], PO[:], rD[:], ALU.mult, R=[PO, rD], W=[o_])
            p.dma("sync", oT[:, h, i * 512:(i + 1) * 512], o_[:], R=[o_])
    return p.finish()


def a_consts():
    cst = np.zeros((128, 4, 128), np.float32)
    cst[:, 0] = 1.0
    k = np.arange(128)[:, None]
    m = np.arange(128)[None, :]
    cst[:, 1] = (k <= m)
    cst[:, 2] = np.eye(128)
    cst[:, 3] = (k < m)
    mb = np.zeros((128, 4, 512), np.float32)
    s = np.arange(128)[:, None]
    t = np.arange(512)[None, :]
    for r in range(4):
        mb[:, r] = np.where(128 * r + s > t, NEG, 0.0)
    return cst, mb


import ml_dtypes as _mld
_BF = _mld.bfloat16
NCORES = 8
SEQ = 8192
NPASS = SEQ // TP
_PROGS = {}


def _prog(key, fn):
    if key not in _PROGS:
        _PROGS[key] = fn()
    return _PROGS[key]


def _launch(nc, in_maps):
    res = run_bass_kernel_spmd(nc, in_maps, core_ids=list(range(NCORES)))
    return res.results


def _fmcols(vec):
    return np.ascontiguousarray(np.asarray(vec, np.float32).reshape(16, 128).T)


def _make_pv(I, layer, i=None, aux=None, mixn=None):
    pv = np.zeros((128, NPV, 16), np.float32)
    pv[:, PV["mix_norm"]] = _fmcols(I["mix_norm"][layer if mixn is None else mixn])
    pv[:, PV["ffn_norm"]] = _fmcols(I["ffn_norm"][layer])
    if aux is not None:
        pv[:, PV["aux_norm"]] = _fmcols(aux)
    if i is not None:
        for j, n in enumerate(["mix_r", "mix_k", "mix_v", "mix_w", "mix_a", "mix_g"]):
            pv[:, PV[n]] = _fmcols(I["rwkv_x_mix"][i, j])
        for n, k in [("w0", "rwkv_w0"), ("a0", "rwkv_a0"), ("k_k", "rwkv_k_k"), ("k_a", "rwkv_k_a"),
                     ("lnx_w", "rwkv_lnx_w"), ("lnx_b", "rwkv_lnx_b")]:
            pv[:, PV[n]] = _fmcols(I[k][i])
        pv[:, PV["r_k"]] = _fmcols(I["rwkv_r_k"][i].reshape(-1))
        if i > 0:
            pv[:, PV["v0"]] = _fmcols(I["rwkv_v0"][i - 1])
    return pv


def _t_cst():
    c = np.zeros((128, 4, 128), np.float32)
    c[:, 0] = 1.0
    c[0:64, 1, 0:64] = 1.0
    c[64:, 1, 64:] = 1.0
    c[:, 2] = np.eye(128, dtype=np.float32)
    return c


def _rmask():
    m = np.ones((128, TP), np.float32)
    m[:, ::64] = 0.0
    return m


def _slabs(xT_full):
    xr = xT_full.reshape(16, 128, SEQ)
    out = np.zeros((NPASS, 128, 16, TP + 1), np.float32)
    for ps in range(NPASS):
        out[ps, :, :, 1:] = xr[:, :, ps * TP:(ps + 1) * TP].transpose(1, 0, 2)
        if ps > 0:
            out[ps, :, :, 0] = xr[:, :, ps * TP - 1].T
    return out


def _percore(a):
    return [np.ascontiguousarray(a[2 * c:2 * c + 2]) for c in range(NCORES)]


def _gather(results, name):
    return np.concatenate([np.asarray(r[name]) for r in results], axis=0)


def _run_tstage(key, cfg, per_core, shared):
    ts = _prog(key, lambda: TS(cfg))
    in_maps = []
    for c in range(NCORES):
        m = dict(shared)
        for k, v in per_core.items():
            m[k] = v[c]
        in_maps.append(m)
    return _launch(ts.nc, in_maps)


def _rec_inputs(FM, TM, PC):
    G = 4
    fmr = FM.reshape(NPASS, 8, 2, 2, 64, 4, 8, 64)
    tmr = TM.reshape(NPASS, 8, 2, 4, 2, 64, 3, 2, 64)
    pcr = PC.reshape(NPASS, 2, 64, 8, 2, 8)
    cst = consts()
    maps = []
    for rc in range(NCORES):
        f = fmr[:, rc]
        f = f.transpose(0, 5, 1, 2, 3, 4, 6)
        f = f.reshape(NPASS * 8 // G, G, 4, 64, 4, 64)
        f = np.ascontiguousarray(f.transpose(0, 3, 1, 2, 4, 5))
        t = tmr[:, rc]
        t = t.transpose(0, 2, 3, 5, 4, 1, 6, 7)
        t = t.reshape(NPASS * 8 // G, G, 3, 64, 4, 64)
        tmx = np.ascontiguousarray(t[:, :, 0:2].transpose(0, 2, 3, 1, 4, 5)).reshape(-1, 128, G, 4, 64)
        vvx = np.ascontiguousarray(t[:, :, 2].transpose(0, 2, 1, 3, 4))
        pc_ = pcr[:, :, :, rc]
        pc_ = np.ascontiguousarray(pc_.transpose(2, 0, 4, 3, 1)).reshape(64, NPASS * 8, 4)
        maps.append(dict(fm=f, tm=tmx, vv=vvx, pc=pc_, cst=cst))
    return maps


def _rec_to_fm(results):
    Y = np.stack([np.asarray(r["yT"]) for r in results])
    NG = Y.shape[1]
    Y = Y.reshape(8, NPASS, NG // NPASS, 64, 4, 2, 2, 64)
    Y = Y.transpose(1, 6, 3, 0, 5, 2, 4, 7)
    return np.ascontiguousarray(Y).reshape(NPASS, 128, 16, TP)


def _attn_inputs(Q, K, V, LF):
    cst, mb = a_consts()
    maps = []
    for ac in range(NCORES):
        q = np.ascontiguousarray(Q[:, 2 * ac:2 * ac + 2].transpose(2, 1, 0, 3)).reshape(128, 2, SEQ)
        k = np.ascontiguousarray(K[:, 2 * ac:2 * ac + 2].transpose(2, 1, 0, 3)).reshape(128, 2, SEQ)
        v = np.ascontiguousarray(V[:, 2 * ac:2 * ac + 2].transpose(3, 1, 0, 2, 4)).reshape(128, 2, SEQ // 128, 128)
        lf = LF[:, 2 * ac:2 * ac + 2].reshape(NPASS, 2, 4, 128)
        lf = np.ascontiguousarray(lf.transpose(3, 1, 0, 2)).reshape(128, 2, SEQ // 128)
        maps.append(dict(qT=q, kT=k, vtm=v, lfc=lf.astype(np.float32), cst=cst, maskb=mb))
    return maps


def _attn_to_fm(results):
    O = np.stack([np.asarray(r["oT"]) for r in results])
    O = O.reshape(8, 128, 2, NPASS, TP).transpose(3, 1, 0, 2, 4)
    return np.ascontiguousarray(O).reshape(NPASS, 128, 16, TP)


def _xfull(xout):
    return np.ascontiguousarray(xout.transpose(2, 1, 0, 3)).reshape(2048, SEQ)


def kernel(**I):
    I = {k: np.asarray(v) for k, v in I.items()}
    f32 = lambda a: np.ascontiguousarray(a, dtype=np.float32)
    xT = np.ascontiguousarray(I["x"][0].T.astype(np.float32))
    shared0 = dict(cst=_t_cst(), rmask=_rmask())
    vfirst = None
    for layer in range(2):
        i = layer
        slabs = _slabs(xT)
        sh = dict(shared0, pv=_make_pv(I, layer, i), w_rkv=f32(I["rwkv_w_rkv"][i]), w1=f32(I["rwkv_w1"][i]), w2=f32(I["rwkv_w2"][i]),
                  a1=f32(I["rwkv_a1"][i]), a2=f32(I["rwkv_a2"][i]), g1=f32(I["rwkv_g1"][i]), g2=f32(I["rwkv_g2"][i]))
        pc_in = dict(xT=_percore(slabs))
        if i > 0:
            sh.update(v1=f32(I["rwkv_v1"][i - 1]), v2=f32(I["rwkv_v2"][i - 1]))
            pc_in["vfirst"] = _percore(vfirst)
        R = _run_tstage(("P", i > 0), dict(pre="rwkv", npass=2, vres=i > 0), pc_in, sh)
        FM, TM, PC = _gather(R, "fm_out"), _gather(R, "tm_out"), _gather(R, "pc_out")
        bonus, g = _gather(R, "bonus_out"), _gather(R, "g_out")
        if i == 0:
            vfirst = _gather(R, "v_out")
        rnc = _prog("R", lambda: build_rstage(SEQ // 64, 4))
        RR = _launch(rnc, _rec_inputs(FM, TM, PC))
        yT = _rec_to_fm(RR)
        sh = dict(shared0, w_o=f32(I["rwkv_w_o"][i]), w_up=f32(I["mlp_w_up"][layer]), w_down=f32(I["mlp_w_down"][layer]))
        pcm = dict(xT=_percore(slabs), yT=_percore(yT), bonus=_percore(bonus), gT=_percore(g))
        if i == 0:
            sh["pv"] = _make_pv(I, layer, i)
            R = _run_tstage("M0", dict(post="rwkv", mlp=True, pre=None, npass=2), pcm, sh)
        else:
            sh["pv"] = _make_pv(I, layer, i, aux=I["kv_norm"], mixn=2)
            sh.update(w_kvf=f32(I["w_kvf"]), b_f=f32(I["b_f"]).reshape(16, 1), w_q=f32(I["fox_w_q"][0]))
            R = _run_tstage("M1", dict(post="rwkv", mlp=True, pre="kvq", npass=2), pcm, sh)
            Kt, Vt, LF, Q = _gather(R, "kT_out"), _gather(R, "vtm_out"), _gather(R, "lf_out"), _gather(R, "qT_out")
        xT = _xfull(_gather(R, "xT_out"))
    out = None
    for layer in (2, 3):
        j = layer - 2
        anc = _prog("A", lambda: build_astage(SEQ, 2))
        AR = _launch(anc, _attn_inputs(Q, Kt, Vt, LF))
        oT = _attn_to_fm(AR)
        slabs = _slabs(xT)
        sh = dict(shared0, w_o=f32(I["fox_w_o"][j]), w_up=f32(I["mlp_w_up"][layer]), w_down=f32(I["mlp_w_down"][layer]))
        pcm = dict(xT=_percore(slabs), oT=_percore(oT))
        if layer == 2:
            sh["pv"] = _make_pv(I, layer, mixn=3)
            sh["w_q"] = f32(I["fox_w_q"][1])
            R = _run_tstage("M2", dict(post="fox", mlp=True, pre="q", npass=2), pcm, sh)
            Q = _gather(R, "qT_out")
            xT = _xfull(_gather(R, "xT_out"))
        else:
            sh["pv"] = _make_pv(I, layer, aux=I["final_norm"])
            R = _run_tstage("M3", dict(post="fox", mlp=True, pre="final", npass=2), pcm, sh)
            out = _xfull(_gather(R, "fin_out"))
    return np.ascontiguousarray(out.T)[None].astype(np.float32)
```

```python
import contextlib
import numpy as np
import concourse.bass as bass
import concourse.mybir as mybir
from concourse.bass_utils import run_bass_kernel_spmd

F32 = mybir.dt.float32
BF16 = mybir.dt.bfloat16
AF = mybir.ActivationFunctionType
ALU = mybir.AluOpType
AX = mybir.AxisListType

ENGS = ("tensor", "vector", "scalar", "gpsimd", "sync")


class Dep:
    __slots__ = ("w", "r")

    def __init__(self):
        self.w = None
        self.r = {}


class Buf:
    def __init__(self, t, ndeps=0):
        self.t = t
        self.dep = Dep()
        self.deps = [self.dep for _ in range(ndeps)]

    def __getitem__(self, idx):
        return self.t[idx]


class DView:
    def __init__(self, ap, dep=None):
        self.ap = ap
        self.dep = dep or Dep()

    def __getitem__(self, idx):
        return DView(self.ap[idx], self.dep)

    def rearrange(self, *a, **k):
        return DView(self.ap.rearrange(*a, **k), self.dep)

    def dyn(self, fn):
        base = self.ap
        return DView(lambda pid: fn(base, pid), self.dep)


class Prog:
    def __init__(self, n_dma_sems=6):
        self.nc = bass.Bass("TRN2", target_bir_lowering=False)
        self.es = contextlib.ExitStack()
        self.es0 = self.es
        self.ncc = 0
        self.rp = None
        self.ops = {e: [] for e in ENGS}
        self.cnt = {e: 0 for e in ENGS}
        self.semobj = {}
        for e in ENGS:
            if e != "sync":
                self.semobj[e] = self.es.enter_context(self.nc.semaphore("s_" + e))
        self.dq = {}
        for q in ("sync", "gpsimd", "scalar"):
            for i in range(n_dma_sems):
                self.semobj[(q, i)] = self.es.enter_context(self.nc.semaphore(f"d_{q}{i}"))
            self.dq[q] = 0
        self.nds = n_dma_sems
        self.waited = {e: {} for e in ENGS}
        self.nbuf = 0

    def dram(self, name, shape, dt, kind):
        return self.nc.dram_tensor(name, list(shape), dt, kind=kind).ap()

    def dram_internal(self, name, shape, dt):
        return DView(self.nc.dram_tensor(name, list(shape), dt).ap())

    def sbuf(self, shape, dt, name=None, ndeps=0):
        self.nbuf += 1
        t = self.es.enter_context(self.nc.sbuf_tensor(f"{name or 'sb'}_{self.nbuf}", list(shape), dt))
        return Buf(t, ndeps)

    def psum(self, shape, dt, name=None, ndeps=0):
        self.nbuf += 1
        t = self.es.enter_context(self.nc.psum_tensor(f"{name or 'ps'}_{self.nbuf}", list(shape), dt))
        return Buf(t, ndeps)

    def _need(self, e, tok):
        if tok is None:
            return
        k, v = tok
        if self.waited[e].get(k, 0) >= v:
            return
        self.waited[e][k] = v
        self.ops[e].append(("wait", k, v))

    @staticmethod
    def _deps(xs):
        out = []
        for x in xs:
            out.append(x.dep if isinstance(x, Buf) else x)
        return out

    def op(self, e, fn, R=(), W=()):
        R = self._deps(R)
        W = self._deps(W)
        for d in R:
            self._need(e, d.w)
        for d in W:
            if d.w is not None and (d.w[0] != e or e != "tensor"):
                self._need(e, d.w)
            for k, v in d.r.items():
                if k != e or e != "tensor":
                    self._need(e, (k, v))
        self.cnt[e] += 1
        tok = (e, self.cnt[e])
        self.ops[e].append(("op", fn))
        for d in R:
            d.r[e] = tok[1]
        for d in W:
            d.w = tok
            d.r = {}
        return tok

    def dma(self, q, out_ap, in_ap, R=(), W=(), slow=False):
        R = list(R)
        W = list(W)
        if isinstance(out_ap, DView):
            W.append(out_ap.dep)
            out_ap = out_ap.ap
        if isinstance(in_ap, DView):
            R.append(in_ap.dep)
            in_ap = in_ap.ap
        R = self._deps(R)
        W = self._deps(W)
        for d in R:
            self._need(q, d.w)
        for d in W:
            self._need(q, d.w)
            for k, v in d.r.items():
                self._need(q, (k, v))
        j = self.dq[q]
        self.dq[q] += 1
        s = j % self.nds
        v = 16 * (j // self.nds + 1)
        key = (q, s)
        if j >= self.nds:
            self._need(q, (key, v - 16))
        self.ops[q].append(("dma", out_ap, in_ap, key, slow))
        tok = (key, v)
        for d in R:
            d.r[key] = v
        for d in W:
            d.w = tok
            d.r = {}
        return tok

    def dma_tokens(self):
        toks = []
        for q in ("sync", "gpsimd", "scalar"):
            j = self.dq[q]
            for s in range(min(j, self.nds)):
                n = (j - 1 - s) // self.nds + 1
                toks.append(((q, s), 16 * n))
        return toks

    def barrier(self):
        toks = self.dma_tokens()
        for e2 in ENGS:
            if e2 != "sync" and self.cnt[e2] > 0:
                toks.append((e2, self.cnt[e2]))
        if self.ncc:
            toks.append(("cc", self.ncc))
        for e in ENGS:
            for t in toks:
                self._need(e, t)

    def collective(self, kind, in_dv, out_dv):
        if "cc" not in self.semobj:
            self.semobj["cc"] = self.es0.enter_context(self.nc.semaphore("cc"))
        self._need("gpsimd", in_dv.dep.w)
        self._need("gpsimd", out_dv.dep.w)
        for k, v in out_dv.dep.r.items():
            self._need("gpsimd", (k, v))
        self.ncc += 1
        ia, oa = in_dv.ap, out_dv.ap
        cc = self.semobj["cc"]
        self.ops["gpsimd"].append(("raw", lambda e: e.collective_compute(
            kind, ALU.bypass, replica_groups=[list(range(8))], ins=[ia.opt()], outs=[oa.opt()]).then_inc(cc, 1)))
        tok = ("cc", self.ncc)
        in_dv.dep.r["cc"] = self.ncc
        out_dv.dep.w = tok
        out_dv.dep.r = {}
        self._need("gpsimd", tok)
        return tok

    @contextlib.contextmanager
    def scope(self):
        outer = self.es
        self.es = contextlib.ExitStack()
        try:
            yield
        finally:
            self.barrier()
            self.flush()
            self.es.close()
            self.es = outer

    def flush(self):
        with self.nc.Block() as block:
            for e in ENGS:
                if not self.ops[e]:
                    continue

                def body(eng, e=e):
                    for o in self.ops[e]:
                        if o[0] == "wait":
                            eng.wait_ge(self.semobj[o[1]], o[2])
                        elif o[0] == "op":
                            o[1](eng).then_inc(self.semobj[e], 1)
                        elif o[0] == "raw":
                            o[1](eng)
                        else:
                            src, dst = o[2], o[1]
                            if callable(src):
                                if self.rp is None:
                                    self.rp = self.es0.enter_context(self.nc.sync.register("rp"))
                                    self.ro = self.es0.enter_context(self.nc.sync.register("ro"))
                                    eng.reg_mov(self.rp, eng.partition_id())
                                ap0, ap1 = src(0), src(1)
                                eng.reg_mul(self.ro, self.rp, ap1.offset - ap0.offset)
                                eng.reg_add(self.ro, self.ro, ap0.offset)
                                src = bass.AP(ap0.tensor, self.ro, [list(v) for v in ap0.ap])
                            kw = {"allow_slow_non_contiguous": True} if (len(o) > 4 and o[4]) else {}
                            eng.dma_start(out=dst, in_=src, **kw).then_inc(self.semobj[o[3]], 16)

                getattr(block, e)(body)
        self.ops = {e: [] for e in ENGS}

    def finish(self):
        for t in self.dma_tokens():
            self._need(t[0][0], t)
        self.flush()
        self.es.close()
        if self.es0 is not self.es:
            self.es0.close()
        return self.nc

    def finish_old(self):
        with self.nc.Block() as block:
            for e in ENGS:
                if not self.ops[e]:
                    continue

                def body(eng, e=e):
                    for o in self.ops[e]:
                        if o[0] == "wait":
                            eng.wait_ge(self.semobj[o[1]], o[2])
                        elif o[0] == "op":
                            o[1](eng).then_inc(self.semobj[e], 1)
                        else:
                            eng.dma_start(out=o[1], in_=o[2]).then_inc(self.semobj[o[3]], 16)

                getattr(block, e)(body)
        self.es.close()
        return self.nc

    def mm(self, out, lhsT, rhs, start, stop, R=(), W=()):
        return self.op("tensor", lambda e: e.matmul(out, lhsT, rhs, start=start, stop=stop), R, W)

    def act(self, out, in_, func, R=(), W=(), bias=None, scale=1.0, accum_out=None, eng="scalar"):
        kw = {}
        if bias is not None:
            kw["bias"] = bias
        if accum_out is not None:
            kw["accum_out"] = accum_out
        return self.op(eng, lambda e: e.activation(out=out, in_=in_, func=func, scale=scale, **kw), R, W)

    def tt(self, eng, out, in0, in1, op, R=(), W=()):
        return self.op(eng, lambda e: e.tensor_tensor(out=out, in0=in0, in1=in1, op=op), R, W)

    def cp(self, eng, out, in_, R=(), W=()):
        if eng == "scalar":
            return self.op(eng, lambda e: e.copy(out, in_), R, W)
        return self.op(eng, lambda e: e.tensor_copy(out, in_), R, W)

    def ts(self, eng, out, in0, s1, s2, op0, op1=None, R=(), W=()):
        if op1 is None:
            return self.op(eng, lambda e: e.tensor_scalar(out, in0, s1, None, op0), R, W)
        return self.op(eng, lambda e: e.tensor_scalar(out, in0, s1, s2, op0, op1), R, W)

    def stt(self, out, in0, scalar, in1, op0, op1, R=(), W=(), eng="vector"):
        return self.op(eng, lambda e: e.scalar_tensor_tensor(out=out, in0=in0, scalar=scalar, in1=in1, op0=op0, op1=op1), R, W)

    def memset(self, eng, ap, val, W=()):
        return self.op(eng, lambda e: e.memset(ap, val), (), W)

import numpy as np

C = 64
NH = 4
HD = 64


def emit_rstage(p, io, NCH, G=4):
    nc = p.nc
    NG = NCH // G
    GFM, GTM, GPC, cst, yT = io["GFM"], io["GTM"], io["GPC"], io["cst"], io["y_out"]

    cst_sb = p.sbuf([128, 3, 128], F32, name="rcst_sb")
    p.dma("sync", cst_sb[:], cst, W=[cst_sb])
    pc_sb = p.sbuf([64, NH, NCH], F32)
    for rank in range(8):
        for lp in range(2):
            for mm in range(2):
                n0 = (rank * 2 + lp) * 8
                src = GPC[rank, lp, mm].rearrange("(hh k) c -> k hh c", hh=2)
                p.dma("sync", pc_sb[:, 2 * mm:2 * mm + 2, n0:n0 + 8], src, W=[pc_sb])
    maskS4 = p.sbuf([128, NH, 128], F32)
    maskL4 = p.sbuf([64, NH, 64], F32)
    identF4 = p.sbuf([64, NH, 64], F32)
    ident_bf = p.sbuf([64, 64], BF16)
    for h in range(NH):
        p.cp("vector", maskS4[:, h, :], cst_sb[:, 0, :], R=[cst_sb], W=[maskS4])
        p.cp("vector", maskL4[:, h, :], cst_sb[0:64, 1, 0:64], R=[cst_sb], W=[maskL4])
        p.cp("vector", identF4[:, h, :], cst_sb[0:64, 2, 0:64], R=[cst_sb], W=[identF4])
    p.cp("vector", ident_bf[:], cst_sb[0:64, 2, 0:64], R=[cst_sb], W=[ident_bf])

    Hs = [p.sbuf([64, NH, 64], F32, name=f"H{i}") for i in range(2)]
    p.memset("vector", Hs[0][:], 0.0, W=[Hs[0]])

    NB = 2
    FMg = [p.sbuf([64, NH, G, 4, 64], BF16, name=f"FMg{i}") for i in range(NB)]
    TMg = [p.sbuf([128, G, NH, 64], BF16, name=f"TMg{i}") for i in range(NB)]
    UVg = [p.sbuf([128, G, NH, 64], BF16, name=f"UVg{i}") for i in range(NB)]
    UVv = [Dep() for _ in range(NB)]
    UVu = [[Dep() for _ in range(G)] for _ in range(NB)]
    OUTg = [p.sbuf([64, NH, G * 64], F32, name=f"OUTg{i}") for i in range(NB)]
    ZVg = [p.sbuf([128, G, NH, 64], BF16, name=f"ZVg{i}") for i in range(NB)]
    for i in range(NB):
        p.memset("gpsimd", ZVg[i][0:64], 0.0, W=[ZVg[i]])

    def perchunk(shape, dt, name, n=2):
        return [p.sbuf(shape, dt, name=f"{name}{i}") for i in range(n)]

    S_f = perchunk([128, NH, 128], F32, "S_f")
    S_bf = perchunk([128, NH, 128], BF16, "S_bf")
    Pa = perchunk([64, NH, 64], F32, "Pa")
    PaT = perchunk([64, NH, 64], F32, "PaT")
    Ya = perchunk([64, NH, 128], F32, "Ya")
    W_bf = perchunk([64, NH, 64], BF16, "W_bf")
    QT_f = perchunk([64, NH, 64], F32, "QT_f")
    O0_f = perchunk([64, NH, 64], F32, "O0_f")
    MT_f = perchunk([64, NH, 64], F32, "MT_f")
    N0_f = perchunk([64, NH, 64], F32, "N0_f")
    diagP = perchunk([64, NH, 64], F32, "diagP")

    bS = p.psum([128, NH, 128], F32, name="bS")
    bX = p.psum([128, NH, 128], F32, name="bX")
    bY = p.psum([128, NH, 128], F32, name="bY")
    bP = p.psum([128, 2, NH, 64], F32, name="bP", ndeps=2)
    bQ = p.psum([128, 2, NH, 64], F32, name="bQ", ndeps=2)
    bO = p.psum([128, 2, NH, 64], F32, name="bO", ndeps=2)
    bN = p.psum([128, 2, NH, 64], F32, name="bN", ndeps=2)
    bH = p.psum([128, 2, NH, 64], F32, name="bH", ndeps=2)

    def flat(ap):
        return ap.rearrange("p a b -> p (a b)")

    def load_group(g):
        b = g % NB
        rank, lp, t0 = g // 4, (g // 2) % 2, (g % 2) * 256
        for mm in range(2):
            c0 = (g % 2) * 4
            for hh in range(2):
                p.dma("sync", FMg[b][:, 2 * mm + hh], GFM[rank, lp, mm, hh * 64:(hh + 1) * 64, c0:c0 + 4], W=[FMg[b]])
            for q in range(3):
                src = GTM[rank, lp, mm, t0:t0 + 256, q, :].rearrange("(c s) (hh i) -> s c hh i", s=64, hh=2)
                if q < 2:
                    p.dma("sync", TMg[b][q * 64:(q + 1) * 64, :, 2 * mm:2 * mm + 2, :], src, W=[TMg[b]])
                else:
                    p.dma("sync", UVg[b][64:128, :, 2 * mm:2 * mm + 2, :], src, W=[UVv[b]])
                    p.dma("sync", ZVg[b][64:128, :, 2 * mm:2 * mm + 2, :], src, W=[ZVg[b]])

    def pre(n):
        g, c = divmod(n, G)
        b = g % NB
        i = n % 2
        FM = FMg[b]
        TM = TMg[b]
        UV = UVg[b]
        for h in range(NH):
            p.mm(bS[:, h, :], flat(FM[:, h, c, 2:4, :]), flat(FM[:, h, c, 0:2, :]), True, True, R=[FM], W=[bS])
        p.tt("vector", S_f[i][:], bS[:], maskS4[:], ALU.mult, R=[bS, maskS4], W=[S_f[i]])
        p.cp("gpsimd", S_bf[i][:], S_f[i][:], R=[S_f[i]], W=[S_bf[i]])
        for h in range(NH):
            p.mm(bQ[0:64, 0, h, :], FM[:, h, c, 0, :], FM[:, h, c, 2, :], True, True, R=[FM], W=[bQ.deps[0]])
        Pj, PjT = Pa[0], PaT[0]
        p.tt("vector", Pj[:], bQ[0:64, 0], maskL4[:], ALU.mult, R=[bQ.deps[0], maskL4], W=[Pj])
        p.cp("scalar", PjT[:], S_f[i][0:64, :, 0:64], R=[S_f[i]], W=[PjT])
        for h in range(NH):
            p.mm(bX[0:64, h, 0:64], FM[:, h, c, 0, :], ident_bf[:], True, True, R=[FM, ident_bf], W=[bX])
            p.mm(bX[0:64, h, 64:128], S_bf[i][:, h, 0:64], ZVg[b][:, c, h, :], True, True,
                 R=[S_bf[i], ZVg[b]], W=[bX])
        Y = Ya[0]
        p.cp("scalar", Y[:], bX[0:64], R=[bX], W=[Y])
        for j in range(6):
            Pj, PjT = Pa[j % 2], PaT[j % 2]
            Pn, PnT = Pa[(j + 1) % 2], PaT[(j + 1) % 2]
            Yc, Yn = Ya[j % 2], Ya[(j + 1) % 2]
            for h in range(NH):
                p.mm(bY[0:64, h, :], PjT[:, h, :], Yc[:, h, :], True, True, R=[PjT, Yc], W=[bY])
            p.tt("vector", Yn[:], bY[0:64], Yc[:], ALU.add, R=[bY, Yc], W=[Yn])
            if j < 5:
                for h in range(NH):
                    p.mm(bP[0:64, 1, h, :], Pj[:, h, :], PjT[:, h, :], True, True, R=[Pj, PjT], W=[bP.deps[1]])
                p.cp("scalar", PnT[:], bP[0:64, 1], R=[bP.deps[1]], W=[PnT])
                if j < 4:
                    for h in range(NH):
                        p.mm(bP[0:64, 0, h, :], PjT[:, h, :], Pj[:, h, :], True, True, R=[Pj, PjT], W=[bP.deps[0]])
                    p.cp("scalar", Pn[:], bP[0:64, 0], R=[bP.deps[0]], W=[Pn])
        Y6 = Ya[0]
        p.cp("gpsimd", W_bf[i][:], Y6[:, :, 0:64], R=[Y6], W=[W_bf[i]])
        p.cp("gpsimd", UV[0:64, c], Y6[:, :, 64:128], R=[Y6], W=[UVu[b][c]])
        p.tt("gpsimd", diagP[i][:], identF4[:], pc_sb[:, :, n:n + 1].broadcast_to([64, NH, 64]), ALU.mult,
             R=[identF4, pc_sb], W=[diagP[i]])
        for h in range(NH):
            p.mm(bQ[0:64, 1, h, :], W_bf[i][:, h, :], S_bf[i][0:64, h, 64:128], True, False, R=[W_bf[i], S_bf[i]], W=[bQ.deps[1]])
            p.mm(bQ[0:64, 1, h, :], ident_bf[:], FM[:, h, c, 1, :], False, True, R=[ident_bf, FM], W=[bQ.deps[1]])
        p.cp("scalar", QT_f[i][:], bQ[0:64, 1], R=[bQ.deps[1]], W=[QT_f[i]])
        for h in range(NH):
            p.mm(bO[0:64, 0, h, :], UV[:, c, h, :], S_bf[i][:, h, 64:128], True, True,
                 R=[UVu[b][c], UVv[b], S_bf[i]], W=[bO.deps[0]])
        p.cp("scalar", O0_f[i][:], bO[0:64, 0], R=[bO.deps[0]], W=[O0_f[i]])
        for h in range(NH):
            p.mm(bO[0:64, 1, h, :], W_bf[i][:, h, :], TM[0:64, c, h, :], True, True, R=[W_bf[i], TM], W=[bO.deps[1]])
        p.tt("vector", MT_f[i][:], bO[0:64, 1], diagP[i][:], ALU.add, R=[bO.deps[1], diagP[i]], W=[MT_f[i]])
        for h in range(NH):
            p.mm(bN[0:64, 0, h, :], TM[:, c, h, :], UV[:, c, h, :], True, True,
                 R=[TM, UVu[b][c], UVv[b]], W=[bN.deps[0]])
        p.cp("scalar", N0_f[i][:], bN[0:64, 0], R=[bN.deps[0]], W=[N0_f[i]])

    def seq(n):
        g, c = divmod(n, G)
        b = g % NB
        i = n % 2
        H0, H1 = Hs[n % 2], Hs[(n + 1) % 2]
        for h in range(NH):
            p.mm(bH[0:64, 0, h, :], H0[:, h, :], QT_f[i][:, h, :], True, True, R=[H0, QT_f[i]], W=[bH.deps[0]])
        for h in range(NH):
            p.mm(bH[0:64, 1, h, :], MT_f[i][:, h, :], H0[:, h, :], True, True, R=[H0, MT_f[i]], W=[bH.deps[1]])
        p.tt("vector", H1[:], bH[0:64, 1], N0_f[i][:], ALU.add, R=[bH.deps[1], N0_f[i]], W=[H1])
        p.tt("vector", OUTg[b][:, :, c * 64:(c + 1) * 64], bH[0:64, 0], O0_f[i][:], ALU.add, R=[bH.deps[0], O0_f[i]], W=[OUTg[b]])
        if c == G - 1:
            p.dma("sync", yT[g].rearrange("hl v t -> v hl t"), OUTg[b][:], R=[OUTg[b]])

    load_group(0)
    for n in range(NCH):
        g, c = divmod(n, G)
        if c == 0 and g + 1 < NG:
            load_group(g + 1)
        pre(n)
        if n > 0:
            seq(n - 1)
    seq(NCH - 1)


def consts():
    cst = np.zeros((128, 3, 128), np.float32)
    s = np.arange(64)[:, None]
    t = np.arange(64)[None, :]
    strict = (s < t).astype(np.float32)
    incl = (s <= t).astype(np.float32)
    cst[0:64, 0, 0:64] = strict
    cst[0:64, 0, 64:128] = incl
    cst[64:128, 0, 0:64] = strict
    cst[64:128, 0, 64:128] = incl
    cst[0:64, 1, 0:64] = (np.arange(64)[:, None] > np.arange(64)[None, :]).astype(np.float32)
    cst[:, 2, :] = np.eye(128, dtype=np.float32)
    return cst

import numpy as np

D = 2048
NC16 = 16
TP = 512
DFF = 8192
RMS_EPS = 1e-6
LNX_EPS = 1e-5 * 64

PV = {n: i for i, n in enumerate(
    ["mix_norm", "ffn_norm", "aux_norm", "mix_r", "mix_k", "mix_v", "mix_w", "mix_a", "mix_g",
     "w0", "a0", "v0", "k_k", "k_a", "r_k", "lnx_w", "lnx_b"])}
NPV = len(PV)


class TS:
    def __init__(self, p, cfg, io):
        self.cfg = cfg
        self.p = p
        self.io = io
        self.NP = cfg["npass"]
        for k, v in io.items():
            setattr(self, k, v)
        self.pv_d, self.cst_d, self.rmask_d = io["pv"], io["cst"], io["rmask"]
        self.build()

    def ps(self):
        self._psi = (self._psi + 1) % len(self.PS)
        return self.PS[self._psi]

    def linear(self, W, KC, kp, mcols, rhs_fn, rhs_deps, consume, c0=0):
        p = self.p
        for (m0, mw) in mcols:
            self._wi = (self._wi + 1) % len(self.WB)
            wb = self.WB[self._wi]
            for k0 in range(0, KC, 16):
                k1 = min(KC, k0 + 16)
                src = W[k0 * kp:k1 * kp, m0:m0 + mw].rearrange("(c p) n -> p c n", p=kp)
                p.dma("gpsimd", wb[0:kp, k0:k1, 0:mw], src, W=[wb])
            acc = self.ps()
            for c in range(KC):
                p.mm(acc[0:mw, :], wb[0:kp, c, 0:mw], rhs_fn(c), c == 0, c == KC - 1, R=[wb] + rhs_deps, W=[acc])
            consume(m0, mw, acc)

    def rmsnorm(self, xT, gain_slot, out_bf, lo, hi, out_off=0):
        p = self.p
        n = hi - lo
        acc = self.ps()
        for c in range(NC16):
            sq = self.SQ[c % 2]
            p.act(sq[:, 0:n], xT[:, c, lo:hi], AF.Square, R=[xT], W=[sq])
            p.mm(acc[:, 0:n], self.ones_bf[:], sq[:, 0:n], c == 0, c == NC16 - 1, R=[sq, self.ones_bf], W=[acc])
        rstd = self.rstd
        p.act(rstd[:, 0:n], acc[:, 0:n], AF.Sqrt, R=[acc, self.eps_rms], W=[rstd], scale=1.0 / D, bias=self.eps_rms[:])
        p.op("vector", lambda e: e.reciprocal(rstd[:, 0:n], rstd[:, 0:n]), R=[rstd], W=[rstd])
        for c in range(NC16):
            p.stt(out_bf[:, c, out_off + lo:out_off + hi], xT[:, c, lo:hi], self.pv[:, gain_slot, c:c + 1], rstd[:, 0:n],
                  ALU.mult, ALU.mult, R=[xT, self.pv, rstd], W=[out_bf])

    def build(self):
        p = self.p
        cfg = self.cfg
        self.PS = [p.psum([128, 512], F32, name=f"PS{i}") for i in range(7)]
        self._psi = -1
        self.WB = [p.sbuf([128, 16, 128], BF16, name=f"WB{i}") for i in range(3)]
        self._wi = -1
        self.pv = p.sbuf([128, NPV, NC16], F32, name="pv_sb")
        p.dma("sync", self.pv[:], self.pv_d, W=[self.pv])
        cst = p.sbuf([128, 4, 128], F32, name="cst_sb")
        p.dma("sync", cst[:], self.cst_d, W=[cst])
        self.ones_bf = p.sbuf([128, 128], BF16, name="ones_bf")
        p.cp("vector", self.ones_bf[:], cst[:, 0, :], R=[cst], W=[self.ones_bf])
        self.blk_f = cst
        self.ident_bf = p.sbuf([128, 128], BF16, name="ident_bf")
        p.cp("vector", self.ident_bf[:], cst[:, 2, :], R=[cst], W=[self.ident_bf])
        self.cst = cst
        self.eps_rms = p.sbuf([128, 1], F32, name="eps_rms")
        p.memset("vector", self.eps_rms[:], RMS_EPS, W=[self.eps_rms])
        self.eps_lnx = p.sbuf([128, 1], F32, name="eps_lnx")
        p.memset("vector", self.eps_lnx[:], LNX_EPS, W=[self.eps_lnx])
        self.SQ = [p.sbuf([128, 512], BF16, name=f"SQ{i}") for i in range(2)]
        self.rstd = p.sbuf([128, 513], F32, name="rstd")
        self.xT = p.sbuf([128, NC16, TP + 1], F32, name="xT_sb")
        if self.io.get("hl_out") is not None:
            self.hl_sb = p.sbuf([128, NC16], F32, name="hl_sb")
        if self.io.get("hflag") is not None:
            self.hflag_sb = p.sbuf([128, 8], F32, name="hflag_sb")
            p.dma("sync", self.hflag_sb[:], self.io["hflag"], W=[self.hflag_sb])
        isr = cfg.get("pre") == "rwkv"
        if isr:
            self.hbf = p.sbuf([128, NC16, TP + 1], BF16, name="hbf")
        self.T = [p.sbuf([128, TP], F32, name=f"T{i}") for i in range(12 if isr else 8)]
        self.B = [p.sbuf([128, NC16, TP], BF16, name=f"B{i}") for i in range(2 if isr else 1)]
        if not isr:
            self.B.append(self.B[0])
        if cfg.get("mlp"):
            self.hid = p.sbuf([128, 64, TP], BF16, name="hid")
            self.WD = [p.sbuf([128, 64, 128], BF16, name=f"WD{i}") for i in range(2)]
        for ps_ in range(self.NP):
            self.one_pass(ps_)

    def one_pass(self, ip):
        p = self.p
        cfg = self.cfg
        xT = self.xT
        if self.io.get("xin") is not None:
            p.dma("sync", xT[:], self.xin[ip], W=[xT])
        else:
            p.dma("sync", xT[:, :, 1:TP + 1], self.xbody[ip], W=[xT])
            if cfg.get("pre") == "rwkv":
                if ip == 0:
                    hs = self.T[0]
                    for r in range(8):
                        p.dma("sync", hs[:, r * NC16:(r + 1) * NC16], self.io["halo_all"][r], W=[hs])
                    p.ts("vector", xT[:, :, 0], hs[:, 0:NC16], self.hflag_sb[:, 0:1], None, ALU.mult, R=[hs, self.hflag_sb], W=[xT])
                    for r in range(1, 8):
                        p.stt(xT[:, :, 0], hs[:, r * NC16:(r + 1) * NC16], self.hflag_sb[:, r:r + 1], xT[:, :, 0], ALU.mult, ALU.add,
                              R=[hs, self.hflag_sb, xT], W=[xT])
                else:
                    hs = self.T[0]
                    p.dma("sync", hs[:, 0:NC16], self.io["hl_own"], W=[hs])
                    p.cp("vector", xT[:, :, 0], hs[:, 0:NC16], R=[hs], W=[xT])
        if cfg.get("post") == "rwkv":
            self.post_rwkv(ip)
        if cfg.get("post") == "fox":
            self.post_fox(ip)
        if cfg.get("mlp"):
            self.mlp(ip)
        if cfg.get("post") or cfg.get("mlp"):
            p.dma("sync", self.xout[ip], xT[:, :, 1:TP + 1], R=[xT])
            if self.io.get("hl_out") is not None:
                p.cp("vector", self.hl_sb[:], xT[:, :, TP], R=[xT], W=[self.hl_sb])
                p.dma("sync", self.io["hl_out"][ip], self.hl_sb[:], R=[self.hl_sb])
        pre = cfg.get("pre")
        if pre == "rwkv":
            self.pre_rwkv(ip)
        if pre == "kvq":
            self.pre_kv(ip)
        if pre in ("kvq", "q"):
            self.pre_q(ip)
        if pre == "final":
            self.final(ip)

    def add_to_x(self, m0, mw, acc):
        p = self.p
        c = m0 // 128
        p.tt("vector", self.xT[:, c, 1:TP + 1], self.xT[:, c, 1:TP + 1], acc[:, :], ALU.add, R=[acc, self.xT], W=[self.xT])

    def post_fox(self, ip):
        p = self.p
        z = self.B[0]
        st = self.T[0]
        for c in range(NC16):
            p.dma("sync", st[:], self.oin(ip, c), W=[st])
            p.cp("vector", z[:, c, :], st[:], R=[st], W=[z])
        self.linear(self.w_o, NC16, 128, [(m * 128, 128) for m in range(NC16)], lambda c: z[:, c, :], [z], self.add_to_x)

    def post_rwkv(self, ip):
        p = self.p
        z = self.B[0]
        y, bo, g, t1, t2, mu, rs = self.T[0:7]
        blk = self.cst[:, 1, :]
        for c in range(NC16):
            p.dma("sync", y[:].rearrange("p (g t) -> p g t", g=2), self.yin(ip, c), W=[y])
            p.dma("sync", bo[:], self.bonus_in[ip, :, c, :], W=[bo])
            p.dma("sync", g[:], self.g_in[ip, :, c, :], W=[g])
            a1 = self.ps()
            p.mm(a1[:], blk, y[:], True, True, R=[self.cst, y], W=[a1])
            p.stt(t1[:], a1[:], -1.0 / 64, y[:], ALU.mult, ALU.add, R=[a1, y], W=[t1])
            p.act(t2[:], t1[:], AF.Square, R=[t1], W=[t2])
            a2 = self.ps()
            p.mm(a2[:], blk, t2[:], True, True, R=[self.cst, t2], W=[a2])
            p.act(rs[:], a2[:], AF.Sqrt, R=[a2, self.eps_lnx], W=[rs], scale=1.0 / 64, bias=self.eps_lnx[:])
            p.op("vector", lambda e: e.reciprocal(rs[:], rs[:]), R=[rs], W=[rs])
            p.tt("vector", t1[:], t1[:], rs[:], ALU.mult, R=[t1, rs], W=[t1])
            p.ts("vector", t1[:], t1[:], self.pv[:, PV["lnx_w"], c:c + 1], self.pv[:, PV["lnx_b"], c:c + 1], ALU.mult, ALU.add,
                 R=[t1, self.pv], W=[t1])
            p.tt("vector", t1[:], t1[:], bo[:], ALU.add, R=[t1, bo], W=[t1])
            p.tt("vector", z[:, c, :], t1[:], g[:], ALU.mult, R=[t1, g], W=[z])
        self.linear(self.w_o, NC16, 128, [(m * 128, 128) for m in range(NC16)], lambda c: z[:, c, :], [z], self.add_to_x)

    def mlp(self, ip):
        p = self.p
        h2 = self.B[1]
        self.rmsnorm(self.xT, PV["ffn_norm"], h2, 1, TP + 1, out_off=-1)
        hid = self.hid
        t = self.T[7]

        def up_consume(m0, mw, acc):
            m = m0 // 128
            p.act(t[:], acc[:], AF.Relu, R=[acc], W=[t])
            p.tt("vector", hid[:, m, :], t[:], t[:], ALU.mult, R=[t], W=[hid])

        self.linear(self.w_up, NC16, 128, [(m * 128, 128) for m in range(64)], lambda c: h2[:, c, :], [h2], up_consume)
        for m in range(NC16):
            wb = self.WD[m % 2]
            for k0 in range(0, 64, 16):
                src = self.w_down[k0 * 128:(k0 + 16) * 128, m * 128:(m + 1) * 128].rearrange("(c p) n -> p c n", p=128)
                p.dma("gpsimd", wb[:, k0:k0 + 16, :], src, W=[wb])
            acc = self.ps()
            for c in range(64):
                p.mm(acc[:], wb[:, c, :], hid[:, c, :], c == 0, c == 63, R=[wb, hid], W=[acc])
            self.add_to_x(m * 128, 128, acc)

    def final(self, ip):
        p = self.p
        xT = self.xT
        acc = self.ps()
        for c in range(NC16):
            sq = self.SQ[c % 2]
            p.act(sq[:], xT[:, c, 1:TP + 1], AF.Square, R=[xT], W=[sq])
            p.mm(acc[:], self.ones_bf[:], sq[:], c == 0, c == NC16 - 1, R=[sq, self.ones_bf], W=[acc])
        rstd = self.rstd
        p.act(rstd[:, 0:TP], acc[:], AF.Sqrt, R=[acc, self.eps_rms], W=[rstd], scale=1.0 / D, bias=self.eps_rms[:])
        p.op("vector", lambda e: e.reciprocal(rstd[:, 0:TP], rstd[:, 0:TP]), R=[rstd], W=[rstd])
        for c in range(NC16):
            o = self.T[c % 4]
            p.stt(o[:], xT[:, c, 1:TP + 1], self.pv[:, PV["aux_norm"], c:c + 1], rstd[:, 0:TP], ALU.mult, ALU.mult,
                  R=[xT, self.pv, rstd], W=[o])
            p.dma("sync", self.fin_out[ip, :, c, :], o[:], R=[o])

    def transpose_out(self, src_bf, dst, R):
        p = self.p
        pt = self.PT
        for tb in range(4):
            p.op("tensor", lambda e, tb=tb: e.transpose(pt[:, tb, :], src_bf[:, tb * 128:(tb + 1) * 128], self.ident_bf[:]),
                 R=R + [self.ident_bf], W=[pt])
        sb = self.TSB[self._tsi % 2]
        self._tsi += 1
        p.cp("scalar", sb[:], pt[:], R=[pt], W=[sb])
        p.dma("sync", dst.rearrange("t p f -> p t f"), sb[:], R=[sb])

    def pre_q(self, ip):
        p = self.p
        if not hasattr(self, "OB"):
            self.alloc_out_bufs()
        h = self.B[0]
        self.rmsnorm(self.xT, PV["mix_norm"], h, 1, TP + 1, out_off=-1)
        scale = 128 ** -0.5

        def consume(m0, mw, acc):
            m = m0 // 128
            o = self.OB[m % 2]
            p.act(o[:], acc[:], AF.Copy, R=[acc], W=[o], scale=scale)
            p.dma("sync", self.qT_out[ip, m], o[:], R=[o])

        self.linear(self.w_q, NC16, 128, [(m * 128, 128) for m in range(NC16)], lambda c: h[:, c, :], [h], consume)

    def pre_kv(self, ip):
        p = self.p
        if not hasattr(self, "OB"):
            self.alloc_out_bufs()
        h = self.B[1]
        self.rmsnorm(self.xT, PV["aux_norm"], h, 1, TP + 1, out_off=-1)

        def consume_k(m0, mw, acc):
            m = m0 // 128
            o = self.OB[m % 2]
            p.cp("scalar", o[:], acc[:], R=[acc], W=[o])
            p.dma("sync", self.kT_out[ip, m], o[:], R=[o])

        def consume_v(m0, mw, acc):
            m = (m0 - D) // 128
            o = self.OB[m % 2]
            p.cp("scalar", o[:], acc[:], R=[acc], W=[o])
            self.transpose_out(o, self.vtm_out[ip, m], [o])

        def consume_f(m0, mw, acc):
            t, t2 = self.T[0], self.T[1]
            p.act(t[0:16, :], acc[0:16, :], AF.Sigmoid, R=[acc, self.bf_sb], W=[t], bias=self.bf_sb[:])
            p.act(t2[0:16, :], t[0:16, :], AF.Ln, R=[t], W=[t2])
            pt = self.ps()
            for tb in range(4):
                p.op("tensor", lambda e, tb=tb: e.transpose(pt[:, tb * 16:(tb + 1) * 16], t2[0:16, tb * 128:(tb + 1) * 128], self.cst[0:16, 2, 0:16]),
                     R=[t2, self.cst], W=[pt])
            t3 = self.T[2]
            p.cp("vector", t3[:, 0:64], pt[:, 0:64], R=[pt], W=[t3])
            p.dma("sync", self.lf_out[ip].rearrange("b s h -> s b h"), t3[:, 0:64].rearrange("p (b h) -> p b h", b=4), R=[t3])

        self.linear(self.w_kvf, NC16, 128, [(m * 128, 128) for m in range(NC16)], lambda c: h[:, c, :], [h], consume_k)
        self.linear(self.w_kvf, NC16, 128, [(D + m * 128, 128) for m in range(NC16)], lambda c: h[:, c, :], [h], consume_v)
        self.linear(self.w_kvf, NC16, 128, [(2 * D, 16)], lambda c: h[:, c, :], [h], consume_f)

    def alloc_out_bufs(self):
        p = self.p
        self.OB = [p.sbuf([128, TP], BF16, name=f"OB{i}") for i in range(2)]
        self.PT = p.psum([128, 4, 128], BF16, name="PTb")
        self.TSB = [p.sbuf([128, 4, 128], BF16, name=f"TSB{i}") for i in range(2)]
        self._tsi = 0
        if self.cfg.get("pre") == "kvq":
            self.bf_sb = p.sbuf([16, 1], F32, name="bf_sb")
            p.dma("sync", self.bf_sb[:], self.bf_d, W=[self.bf_sb])

    def pre_rwkv(self, ip):
        p = self.p
        cfg = self.cfg
        if not hasattr(self, "OB"):
            self.alloc_out_bufs()
            self.rmask = p.sbuf([128, TP], F32, name="rmask_sb")
            p.dma("sync", self.rmask[:], self.rmask_d, W=[self.rmask])
            self.dbf = p.sbuf([128, NC16, TP], BF16, name="dbf")
            self.XV = p.sbuf([128, NC16, TP], BF16, name="XV")
            self.lora = {n: p.sbuf([128, 2, TP], BF16, name="lo_" + n) for n in ("w", "a", "v", "g")}
            self.W2 = {n: p.sbuf([128, 2 if n == "g" else 1, D], BF16, name="w2_" + n) for n in ("w", "a", "v", "g")}
            self.FMo = [p.sbuf([128, 4, TP], BF16, name=f"FMo{i}") for i in range(2)]
            self.pco = p.sbuf([128, NC16, TP // 64], F32, name="pco")
            self.TM3 = [p.sbuf([128, 3, TP], BF16, name=f"TM3{i}") for i in range(2)]
            for n, w, kk in (("w", self.w2, 96), ("a", self.a2, 96), ("g", self.g2, 256)) + (
                    (("v", self.v2, 64),) if cfg.get("vres") else ()):
                for j in range((kk + 127) // 128):
                    r0, r1 = j * 128, min(kk, (j + 1) * 128)
                    p.dma("gpsimd", self.W2[n][0:r1 - r0, j, :], w[r0:r1, :], W=[self.W2[n]])
        xT, hbf, dbf = self.xT, self.hbf, self.dbf
        pv = self.pv
        self.rmsnorm(xT, PV["mix_norm"], hbf, 0, 1)
        self.rmsnorm(xT, PV["mix_norm"], hbf, 1, TP + 1)
        for c in range(NC16):
            p.tt("vector", dbf[:, c, :], hbf[:, c, 0:TP], hbf[:, c, 1:TP + 1], ALU.subtract, R=[hbf], W=[dbf])

        def make_xs(slot, dst):
            for c in range(NC16):
                p.stt(dst[:, c, :], dbf[:, c, :], pv[:, slot, c:c + 1], hbf[:, c, 1:TP + 1], ALU.mult, ALU.add,
                      R=[dbf, pv, hbf], W=[dst])

        def lora1(name, slot, W, width, func):
            xs = self.B[0]
            make_xs(slot, xs)
            lo = self.lora[name]

            def consume(m0, mw, acc):
                j = m0 // 128
                p.act(lo[0:mw, j, :], acc[0:mw, :], func, R=[acc], W=[lo])

            self.linear(W, NC16, 128, [(j * 128, min(128, width - j * 128)) for j in range((width + 127) // 128)],
                        lambda c: xs[:, c, :], [xs], consume)

        lora1("w", PV["mix_w"], self.w1, 96, AF.Tanh)
        lora1("a", PV["mix_a"], self.a1, 96, AF.Copy)
        lora1("g", PV["mix_g"], self.g1, 256, AF.Sigmoid)
        xr, xk, xv = self.B[0], self.B[1], self.XV
        make_xs(PV["mix_v"], xv)
        if cfg.get("vres"):
            lo = self.lora["v"]

            def consume_v1(m0, mw, acc):
                p.cp("scalar", lo[0:mw, 0, :], acc[0:mw, :], R=[acc], W=[lo])

            self.linear(self.v1, NC16, 128, [(0, 64)], lambda c: xv[:, c, :], [xv], consume_v1)
        make_xs(PV["mix_r"], xr)
        make_xs(PV["mix_k"], xk)

        T = self.T
        blk = self.cst[:, 1, :]
        for m in range(NC16):
            res = {}

            def grab(name):
                def consume(m0, mw, acc):
                    res[name] = acc
                return consume

            mc = [(m * 128, 128)]
            self.linear(self.w_rkv[0], NC16, 128, mc, lambda c: xr[:, c, :], [xr], grab("r"))
            self.linear(self.w_rkv[1], NC16, 128, mc, lambda c: xk[:, c, :], [xk], grab("k"))
            self.linear(self.w_rkv[2], NC16, 128, mc, lambda c: xv[:, c, :], [xv], grab("v"))
            r_f, k_f, v_f, lw, cum, al, kk, t1, t2, t3, g_f, bon = T[0:12]
            p.cp("scalar", r_f[:], res["r"][:], R=[res["r"]], W=[r_f])
            p.cp("scalar", k_f[:], res["k"][:], R=[res["k"]], W=[k_f])
            p.cp("scalar", v_f[:], res["v"][:], R=[res["v"]], W=[v_f])
            def lora2(name, kk_, nj):
                acc = self.ps()
                for j in range(nj):
                    kp = min(128, kk_ - j * 128)
                    p.mm(acc[:], self.W2[name][0:kp, j, m * 128:(m + 1) * 128], self.lora[name][0:kp, j, :], j == 0, j == nj - 1,
                         R=[self.W2[name], self.lora[name]], W=[acc])
                return acc
            aw = lora2("w", 96, 1)
            p.act(lw[:], aw[:], AF.Sigmoid, R=[aw, pv], W=[lw], bias=pv[:, PV["w0"], m:m + 1])
            p.ts("vector", lw[:], lw[:], -float(np.exp(-0.5)), None, ALU.mult, R=[lw], W=[lw])
            aa = lora2("a", 96, 1)
            p.act(al[:], aa[:], AF.Sigmoid, R=[aa, pv], W=[al], bias=pv[:, PV["a0"], m:m + 1])
            ag = lora2("g", 256, 2)
            p.cp("scalar", g_f[:], ag[:], R=[ag], W=[g_f])
            p.dma("sync", self.g_out[ip, :, m, :], g_f[:], R=[g_f])
            if cfg.get("vres"):
                av = lora2("v", 64, 1)
                p.act(t1[:], av[:], AF.Sigmoid, R=[av, pv], W=[t1], bias=pv[:, PV["v0"], m:m + 1])
                p.dma("sync", t2[:], self.vfirst_in[ip, :, m, :], W=[t2])
                p.tt("vector", t2[:], t2[:], v_f[:], ALU.subtract, R=[t2, v_f], W=[t2])
                p.tt("vector", t2[:], t2[:], t1[:], ALU.mult, R=[t2, t1], W=[t2])
                p.tt("vector", v_f[:], v_f[:], t2[:], ALU.add, R=[v_f, t2], W=[v_f])
            p.dma("sync", self.v_out[ip, :, m, :], v_f[:], R=[v_f])
            p.op("vector", lambda e, cum=cum, lw=lw: e.tensor_tensor_scan(cum[:], self.rmask[:], lw[:], 0.0, ALU.mult, ALU.add),
                 R=[self.rmask, lw], W=[cum])
            p.ts("vector", kk[:], k_f[:], pv[:, PV["k_k"], m:m + 1], None, ALU.mult, R=[k_f, pv], W=[kk])
            p.act(t1[:], kk[:], AF.Square, R=[kk], W=[t1])
            a1 = self.ps()
            p.mm(a1[:], blk, t1[:], True, True, R=[self.cst, t1], W=[a1])
            p.act(t1[:], a1[:], AF.Sqrt, R=[a1], W=[t1])
            p.ts("vector", t1[:], t1[:], 1e-12, None, ALU.max, R=[t1], W=[t1])
            p.op("vector", lambda e, t1=t1: e.reciprocal(t1[:], t1[:]), R=[t1], W=[t1])
            p.tt("vector", kk[:], kk[:], t1[:], ALU.mult, R=[kk, t1], W=[kk])
            p.ts("vector", t1[:], al[:], -1.0, pv[:, PV["k_a"], m:m + 1], ALU.add, ALU.mult, R=[al, pv], W=[t1])
            p.stt(k_f[:], t1[:], 1.0, k_f[:], ALU.add, ALU.mult, R=[t1, k_f], W=[k_f])
            p.stt(t1[:], r_f[:], pv[:, PV["r_k"], m:m + 1], k_f[:], ALU.mult, ALU.mult, R=[r_f, pv, k_f], W=[t1])
            a2 = self.ps()
            p.mm(a2[:], blk, t1[:], True, True, R=[self.cst, t1], W=[a2])
            p.tt("vector", bon[:], a2[:], v_f[:], ALU.mult, R=[a2, v_f], W=[bon])
            p.dma("sync", self.bonus_out[ip, :, m, :], bon[:], R=[bon])
            p.tt("vector", t3[:], kk[:], al[:], ALU.mult, R=[kk, al], W=[t3])
            fmo = self.FMo[m % 2]
            tm3 = self.TM3[m % 2]
            p.act(t1[:], cum[:], AF.Exp, R=[cum], W=[t1])
            p.tt("vector", fmo[:, 1, :], r_f[:], t1[:], ALU.mult, R=[r_f, t1], W=[fmo])
            p.cp("vector", self.pco[:, m, :], t1[:, 63::64], R=[t1], W=[self.pco])
            p.tt("vector", t2[:], cum[:], lw[:], ALU.subtract, R=[cum, lw], W=[t2])
            p.act(t2[:], t2[:], AF.Exp, R=[t2], W=[t2])
            p.stt(fmo[:, 0, :], kk[:], -1.0, t2[:], ALU.mult, ALU.mult, R=[kk, t2], W=[fmo])
            p.act(t1[:], cum[:], AF.Exp, R=[cum], W=[t1], scale=-1.0)
            p.tt("vector", fmo[:, 2, :], t3[:], t1[:], ALU.mult, R=[t3, t1], W=[fmo])
            p.tt("vector", fmo[:, 3, :], k_f[:], t1[:], ALU.mult, R=[k_f, t1], W=[fmo])
            for ty in range(4):
                p.dma("sync", self.fm_out[ip, m, :, :, ty, :], fmo[:, ty, :].rearrange("p (c t) -> p c t", c=8), R=[fmo])
            cumC = cum[:, 63::64].unsqueeze(2).broadcast_to([128, TP // 64, 64])
            p.tt("vector", t2[:].rearrange("p (a b) -> p a b", b=64), cumC, cum[:].rearrange("p (a b) -> p a b", b=64),
                 ALU.subtract, R=[cum], W=[t2])
            p.act(t2[:], t2[:], AF.Exp, R=[t2], W=[t2])
            p.tt("vector", tm3[:, 0, :], t3[:], t2[:], ALU.mult, R=[t3, t2], W=[tm3])
            p.tt("vector", tm3[:, 1, :], k_f[:], t2[:], ALU.mult, R=[k_f, t2], W=[tm3])
            p.cp("vector", tm3[:, 2, :], v_f[:], R=[v_f], W=[tm3])
            for q in range(3):
                self.transpose_out(tm3[:, q, :], self.tm_out[ip, m, :, :, q, :], [tm3])
        p.dma("sync", self.pc_out[ip].rearrange("m p c -> p m c"), self.pco[:], R=[self.pco])

import numpy as np

NEG = -30000.0


def emit_astage(p, io, S=8192, NHA=2):
    NJ = S // 128
    NI = S // 512
    GQ, GK, GV, GLF, cst, maskb, oT = io["GQ"], io["GK"], io["GV"], io["GLF"], io["cst"], io["maskb"], io["o_out"]

    q_sb = p.sbuf([128, NHA, S], BF16, name="q_sb", ndeps=NHA)
    k_sb = p.sbuf([128, NHA, S], BF16, name="k_sb", ndeps=NHA)
    v_sb = p.sbuf([128, NHA, NJ, 128], BF16, name="v_sb", ndeps=NHA)
    q_sb.deps = [Dep() for _ in range(NHA)]
    k_sb.deps = [Dep() for _ in range(NHA)]
    v_sb.deps = [Dep() for _ in range(NHA)]
    cst_sb = p.sbuf([128, 4, 128], F32, name="acst_sb")
    p.dma("sync", cst_sb[:], cst, W=[cst_sb])
    lf_sb = p.sbuf([128, NHA, NJ], F32, name="lf_sb")
    mk_f = p.sbuf([128, 4, 512], F32, name="mk_f")
    p.dma("sync", mk_f[:], maskb, W=[mk_f])
    for h in range(NHA):
        for rank in range(8):
            for lp in range(2):
                ps_ = rank * 2 + lp
                p.dma("sync", q_sb[:, h, ps_ * 512:(ps_ + 1) * 512], GQ[rank, lp, h], W=[q_sb.deps[h]])
                p.dma("sync", k_sb[:, h, ps_ * 512:(ps_ + 1) * 512], GK[rank, lp, h], W=[k_sb.deps[h]])
                p.dma("sync", v_sb[:, h, ps_ * 4:(ps_ + 1) * 4, :], GV[rank, lp, h].rearrange("tb s f -> s tb f"), W=[v_sb.deps[h]])

    lf_all = p.sbuf([128, NJ, 16], F32, name="lf_all")
    hsel = p.sbuf([128, NHA, 16], F32, name="hsel_sb")
    p.dma("sync", hsel[:], io["hsel"], W=[hsel])
    for rank in range(8):
        for lp in range(2):
            ps_ = rank * 2 + lp
            p.dma("sync", lf_all[:, ps_ * 4:(ps_ + 1) * 4, :], GLF[rank, lp].rearrange("b s h -> s b h"), W=[lf_all])
    for h in range(NHA):
        p.ts("vector", lf_sb[:, h, :], lf_all[:, :, 0], hsel[:, h, 0:1], None, ALU.mult, R=[lf_all, hsel], W=[lf_sb])
        for m_ in range(1, 16):
            p.stt(lf_sb[:, h, :], lf_all[:, :, m_], hsel[:, h, m_:m_ + 1], lf_sb[:, h, :], ALU.mult, ALU.add, R=[lf_all, hsel, lf_sb], W=[lf_sb])
    ones_bf = p.sbuf([128, 128], BF16, name="ones_bf")
    ident_bf = p.sbuf([128, 128], BF16, name="ident_bf")
    p.cp("vector", ones_bf[:], cst_sb[:, 0, :], R=[cst_sb], W=[ones_bf])
    p.cp("vector", ident_bf[:], cst_sb[:, 2, :], R=[cst_sb], W=[ident_bf])

    PSA = [p.psum([128, 512], F32, name=f"PSA{i}") for i in range(2)]
    PO = p.psum([128, 512], F32, name="PO")
    PD = p.psum([128, 512], F32, name="PD")
    PX = [p.psum([128, 512], F32, name=f"PX{i}") for i in range(2)]
    PTb = [p.sbuf([128, 512], BF16, name=f"PTb{i}") for i in range(3)]
    sq = [p.sbuf([128, 512], BF16, name=f"sq{i}") for i in range(2)]
    mx = p.sbuf([128, 2, 32], F32, name="mx")
    Mst = p.sbuf([128, 4], F32, name="Mst")
    c_col = p.sbuf([128, NJ], F32, name="c_col")
    offs = p.sbuf([128, NJ], F32, name="offs")
    rel = p.sbuf([128, NJ], F32, name="rel")
    refM = p.sbuf([128, NI], F32, name="refM")
    tot = p.sbuf([128, 1], F32, name="tot")
    totbc = p.sbuf([128, 128], F32, name="totbc")
    cbias = [p.sbuf([128, NJ], F32, name=f"cbias{i}") for i in range(2)]
    dg = [p.sbuf([128, 128], BF16, name=f"dg{i}") for i in range(2)]
    Roff = [p.sbuf([128, 512], BF16, name=f"Roff{i}") for i in range(2)]
    Rdg = [p.sbuf([128, 4, 512], BF16, name=f"Rdg{i}") for i in range(2)]
    rD = p.sbuf([128, 512], F32, name="rD")
    ob = [p.sbuf([128, 512], F32, name=f"ob{i}") for i in range(2)]

    for h in range(NHA):
        for which, src, dep in ((0, q_sb, q_sb.deps[h]), (1, k_sb, k_sb.deps[h])):
            for i in range(NI):
                s_ = sq[i % 2]
                p.act(s_[:], src[:, h, i * 512:(i + 1) * 512], AF.Square, R=[dep], W=[s_])
                px = PX[i % 2]
                p.mm(px[:], ones_bf[:], s_[:], True, True, R=[ones_bf, s_], W=[px])
                p.op("vector", lambda e, px=px, which=which, i=i: e.reduce_max(mx[:, which, i:i + 1], px[:], AX.X), R=[px], W=[mx])
            p.op("vector", lambda e, which=which: e.reduce_max(Mst[:, which:which + 1], mx[:, which, 0:NI], AX.X), R=[mx], W=[Mst])
        p.tt("vector", Mst[:, 2:3], Mst[:, 0:1], Mst[:, 1:2], ALU.mult, R=[Mst], W=[Mst])
        p.act(Mst[:, 3:4], Mst[:, 2:3], AF.Sqrt, R=[Mst], W=[Mst])
        px = PX[0]
        p.mm(px[0:NJ, 0:1], lf_sb[:, h, :], cst_sb[:, 0, 0:1], True, True, R=[lf_sb, cst_sb], W=[px])
        p.cp("vector", tot[0:NJ, :], px[0:NJ, 0:1], R=[px], W=[tot])
        p.ts("vector", totbc[0:NJ, :], cst_sb[0:NJ, 0, :], tot[0:NJ, 0:1], None, ALU.mult, R=[cst_sb, tot], W=[totbc])
        px = PX[1]
        p.mm(px[:, 0:NJ], totbc[0:NJ, :], cst_sb[0:NJ, 3, 0:NJ], True, True, R=[totbc, cst_sb], W=[px])
        p.cp("vector", offs[:], px[:, 0:NJ], R=[px], W=[offs])
        px = PX[0]
        p.mm(px[:, 0:NJ], cst_sb[:, 1, :], lf_sb[:, h, :], True, True, R=[cst_sb, lf_sb], W=[px])
        p.tt("vector", c_col[:], px[:, 0:NJ], offs[:], ALU.add, R=[px, offs], W=[c_col])
        p.tt("vector", rel[:].rearrange("p (a b) -> p a b", b=4), c_col[:].rearrange("p (a b) -> p a b", b=4),
             offs[:, 0::4].unsqueeze(2).broadcast_to([128, NI, 4]), ALU.subtract, R=[c_col, offs], W=[rel])
        p.ts("vector", refM[:], offs[:, 0::4], Mst[:, 3:4], None, ALU.subtract, R=[offs, Mst], W=[refM])

        for i in range(NI):
            cb = cbias[i % 2]
            p.ts("vector", cb[:], c_col[:], -1.0, refM[:, i:i + 1], ALU.mult, ALU.add, R=[c_col, refM], W=[cb])
            px = PX[i % 2]
            for jj in range(4):
                d_ = dg[jj % 2]
                p.ts("vector", d_[:], cst_sb[:, 2, :], rel[:, 4 * i + jj:4 * i + jj + 1], None, ALU.mult, R=[cst_sb, rel], W=[d_])
                p.mm(px[:, jj * 128:(jj + 1) * 128], ones_bf[:], d_[:], True, True, R=[ones_bf, d_], W=[px])
            ro = Roff[i % 2]
            p.cp("scalar", ro[:], px[:], R=[px], W=[ro])
            rd = Rdg[i % 2]
            for r in range(4):
                p.tt("vector", rd[:, r, :], ro[:], mk_f[:, r, :], ALU.add, R=[ro, mk_f], W=[rd])
            nj = 4 * i + 4
            for j in range(nj):
                ps = PSA[j % 2]
                p.mm(ps[:], k_sb[:, h, j * 128:(j + 1) * 128], q_sb[:, h, i * 512:(i + 1) * 512], True, False,
                     R=[k_sb.deps[h], q_sb.deps[h]], W=[ps])
                rterm = ro[:] if j < 4 * i else rd[:, j - 4 * i, :]
                p.mm(ps[:], ident_bf[:], rterm, False, True, R=[ident_bf, ro, rd], W=[ps])
                pt = PTb[j % 3]
                p.act(pt[:], ps[:], AF.Exp, R=[ps, cb], W=[pt], bias=cb[:, j:j + 1])
                p.mm(PO[:], v_sb[:, h, j, :], pt[:], j == 0, j == nj - 1, R=[v_sb.deps[h], pt], W=[PO])
                p.mm(PD[:], ones_bf[:], pt[:], j == 0, j == nj - 1, R=[ones_bf, pt], W=[PD])
            p.op("vector", lambda e: e.reciprocal(rD[:], PD[:]), R=[PD], W=[rD])
            o_ = ob[i % 2]
            p.tt("vector", o_[:], PO[:], rD[:], ALU.mult, R=[PO, rD], W=[o_])
            p.dma("sync", oT[:, h, i * 512:(i + 1) * 512], o_[:], R=[o_])


def a_consts():
    cst = np.zeros((128, 4, 128), np.float32)
    cst[:, 0] = 1.0
    k = np.arange(128)[:, None]
    m = np.arange(128)[None, :]
    cst[:, 1] = (k <= m)
    cst[:, 2] = np.eye(128)
    cst[:, 3] = (k < m)
    mb = np.zeros((128, 4, 512), np.float32)
    s = np.arange(128)[:, None]
    t = np.arange(512)[None, :]
    for r in range(4):
        mb[:, r] = np.where(128 * r + s > t, NEG, 0.0)
    return cst, mb


import ml_dtypes as _mld
import os
NCORES = 8
SEQ = 8192
NPASS = SEQ // TP
_CACHE = {}


def _fmcols(vec):
    return np.ascontiguousarray(np.asarray(vec, np.float32).reshape(16, 128).T)


def _make_pv(I, layer, i=None, aux=None, mixn=None):
    pv = np.zeros((128, NPV, 16), np.float32)
    pv[:, PV["mix_norm"]] = _fmcols(I["mix_norm"][layer if mixn is None else mixn])
    pv[:, PV["ffn_norm"]] = _fmcols(I["ffn_norm"][layer])
    if aux is not None:
        pv[:, PV["aux_norm"]] = _fmcols(aux)
    if i is not None:
        for j, n in enumerate(["mix_r", "mix_k", "mix_v", "mix_w", "mix_a", "mix_g"]):
            pv[:, PV[n]] = _fmcols(I["rwkv_x_mix"][i, j])
        for n, k in [("w0", "rwkv_w0"), ("a0", "rwkv_a0"), ("k_k", "rwkv_k_k"), ("k_a", "rwkv_k_a"),
                     ("lnx_w", "rwkv_lnx_w"), ("lnx_b", "rwkv_lnx_b")]:
            pv[:, PV[n]] = _fmcols(I[k][i])
        pv[:, PV["r_k"]] = _fmcols(I["rwkv_r_k"][i].reshape(-1))
        if i > 0:
            pv[:, PV["v0"]] = _fmcols(I["rwkv_v0"][i - 1])
    return pv


def _t_cst():
    c = np.zeros((128, 4, 128), np.float32)
    c[:, 0] = 1.0
    c[0:64, 1, 0:64] = 1.0
    c[64:, 1, 64:] = 1.0
    c[:, 2] = np.eye(128, dtype=np.float32)
    return c


def _rmask():
    m = np.ones((128, TP), np.float32)
    m[:, ::64] = 0.0
    return m


def build_fused():
    p = Prog()
    nc = p.nc
    p.ext_names = set()

    def EI(n, s, dt=F32):
        p.ext_names.add(n)
        return p.dram(n, s, dt, "ExternalInput")

    STOP = int(os.environ.get("KSTOP", "99"))
    dbg = DView(p.dram("dbg_out", [128, 512], F32, "ExternalOutput")) if STOP < 99 else None

    class _Stop(Exception):
        pass

    def checkpoint(k, src=None):
        if STOP == k:
            if src is not None:
                p.dma("gpsimd", dbg if len(src.ap.shape) == 2 and src.ap.shape[1] == 512 else dbg[:, 0:src.ap.shape[-1]], src)
            raise _Stop()

    xin = EI("xT", [2, 128, NC16, TP + 1])
    class _LazyPV(dict):
        def __missing__(self, k):
            self[k] = EI("pv_" + k, [128, NPV, NC16])
            return self[k]

    pvs = _LazyPV()
    tcst = EI("tcst", [128, 4, 128])
    rmask = EI("rmask", [128, TP])
    acst = EI("acst", [128, 4, 128])
    maskb = EI("maskb", [128, 4, 512])
    WS = dict(rwkv_w_rkv=[2, 3, D, D], rwkv_w1=[2, D, 96], rwkv_w2=[2, 96, D], rwkv_a1=[2, D, 96], rwkv_a2=[2, 96, D],
              rwkv_v1=[1, D, 64], rwkv_v2=[1, 64, D], rwkv_g1=[2, D, 256], rwkv_g2=[2, 256, D], rwkv_w_o=[2, D, D],
              w_kvf=[D, 2 * D + 16], b_f=[16, 1], fox_w_q=[2, D, D], fox_w_o=[2, D, D], mlp_w_up=[4, D, DFF], mlp_w_down=[4, DFF, D])

    class _LazyW(dict):
        def __missing__(self, k):
            self[k] = EI(k, WS[k])
            return self[k]

    W = _LazyW()
    fin_out = DView(p.dram("fin_out", [2, 128, NC16, TP], F32, "ExternalOutput")) if STOP == 99 else None

    def pair(name, rows, cols, dt):
        a = p.dram_internal(name + "_c", [rows, cols], dt)
        g = p.dram_internal(name + "_g", [8 * rows, cols], dt)
        return a, g

    fm_c, GFM = pair("fm", 2 * 16 * 128, 4 * 512, BF16)
    tm_c, GTM = pair("tm", 2 * 16 * 512, 3 * 128, BF16)
    pc_c, GPC = pair("pc", 2 * 16 * 128, 8, F32)
    y_c, GY = pair("y", 32 * 4 * 64, 256, F32)
    hl_c, GH = pair("hl", 2 * 128, 16, F32)
    k_c, GK = pair("k", 2 * 16 * 128, 512, BF16)
    q_c, GQ = pair("q", 2 * 16 * 128, 512, BF16)
    v2_c, GV = pair("v2", 2 * 16 * 4 * 128, 128, BF16)
    lf_c, GLF = pair("lf", 2 * 4 * 128, 16, F32)
    o_c, GO = pair("o", 128, 2 * SEQ, F32)
    own = lambda n: p.dram_internal(n, [2, 128, NC16, TP], F32)
    bonus_c, g_c, vfirst_c, vdump_c = own("bonus_c"), own("g_c"), own("vfirst_c"), own("vdump_c")
    x1_c, x2_c, x3_c, x4_c = own("x1_c"), own("x2_c"), own("x3_c"), own("x4_c")

    def V(dv, pat, **kw):
        return DView(dv.ap.rearrange(pat, **kw), dv.dep)

    fm_v = V(fm_c, "(l m p) (c y t) -> l m p c y t", l=2, m=16, c=8, y=4)
    tm_v = V(tm_c, "(l m b k) (q f) -> l m b k q f", l=2, m=16, b=4, q=3)
    pc_v = V(pc_c, "(l m p) c -> l m p c", l=2, m=16)
    GFM_v = V(GFM, "(r l m p) (c y t) -> r l m p c y t", r=8, l=2, m=16, c=8, y=4)
    GTM_v = V(GTM, "(r l m k) (q f) -> r l m k q f", r=8, l=2, m=16, q=3)
    GPC_v = V(GPC, "(r l m p) c -> r l m p c", r=8, l=2, m=16)
    y_v = V(y_c, "(g h v) t -> g h v t", g=32, h=4)
    GY_v = V(GY, "(r g h v) t -> r g h v t", r=8, g=32, h=4)
    GH_v = V(GH, "(r l p) c -> r l p c", r=8, l=2)
    hl_v = V(hl_c, "(l p) c -> l p c", l=2)
    k_v = V(k_c, "(l m d) t -> l m d t", l=2, m=16)
    q_v = V(q_c, "(l m d) t -> l m d t", l=2, m=16)
    v2_v = V(v2_c, "(l m b k) f -> l m b k f", l=2, m=16, b=4)
    lf_v = V(lf_c, "(l b s) h -> l b s h", l=2, b=4)
    GK_v = V(GK, "(r l m d) t -> r l m d t", r=8, l=2, m=16)
    GQ_v = V(GQ, "(r l m d) t -> r l m d t", r=8, l=2, m=16)
    GV_v = V(GV, "(r l m b k) f -> r l m b k f", r=8, l=2, m=16, b=4)
    GLF_v = V(GLF, "(r l b s) h -> r l b s h", r=8, l=2, b=4)
    o_v = V(o_c, "d (h s) -> d h s", h=2)
    GO_v = V(GO, "(a d) (h s) -> a d h s", a=8, h=2)

    def loc(name, shape, dt):
        return p.dram_internal(name, shape, dt)

    fm_l = loc("fm_l", [8, 2, 2, 128, 8, 4, 64], BF16)
    tm_l = loc("tm_l", [8, 2, 2, 512, 3, 128], BF16)
    pc_l = loc("pc_l", [8, 2, 2, 128, 8], F32)
    y_l = loc("y_l", [8, 4, 4, 64, 256], F32)
    q_l = loc("q_l", [8, 2, 2, 128, 512], BF16)
    k_l = loc("k_l", [8, 2, 2, 128, 512], BF16)
    v_l = loc("v_l", [8, 2, 2, 4, 128, 128], BF16)
    o_l = loc("o_l", [8, 128, 2, 1024], F32)

    def localize(dst, gv, fn, pat_d, pat_s):
        p.dma("sync", dst.rearrange(pat_d), gv.dyn(lambda a, c: fn(a, c).rearrange(pat_s)))

    def loc_rwkv():
        localize(fm_l, GFM_v, lambda a, c: a[:, :, 2 * c:2 * c + 2], "r l m p c y t -> (r l) (m p c y t)", "r l m p c y t -> (r l) (m p c y t)")
        localize(tm_l, GTM_v, lambda a, c: a[:, :, 2 * c:2 * c + 2], "r l m k q f -> (r l) (m k q f)", "r l m k q f -> (r l) (m k q f)")
        localize(pc_l, GPC_v, lambda a, c: a[:, :, 2 * c:2 * c + 2], "r l m p c -> (r l) (m p c)", "r l m p c -> (r l) (m p c)")

    def loc_y():
        localize(y_l, GY_v, lambda a, c: a[:, 4 * c:4 * c + 4], "r g h v t -> r (g h v t)", "r g h v t -> r (g h v t)")

    def loc_q():
        localize(q_l, GQ_v, lambda a, c: a[:, :, 2 * c:2 * c + 2], "r l m d t -> (r l) (m d t)", "r l m d t -> (r l) (m d t)")

    def loc_kv():
        localize(k_l, GK_v, lambda a, c: a[:, :, 2 * c:2 * c + 2], "r l m d t -> (r l) (m d t)", "r l m d t -> (r l) (m d t)")
        localize(v_l, GV_v, lambda a, c: a[:, :, 2 * c:2 * c + 2], "r l m b k f -> (r l) (m b k f)", "r l m b k f -> (r l) (m b k f)")

    def loc_o():
        localize(o_l, GO_v, lambda a, c: a[:, :, :, 1024 * c:1024 * c + 1024], "a d h s -> (a d h) s", "a d h s -> (a d h) s")

    def yin(ip, c):
        rc, mm = c // 2, c % 2
        return y_l[rc, 2 * ip:2 * ip + 2, 2 * mm:2 * mm + 2, :, :].rearrange("g h v t -> (h v) g t")

    def oin(ip, c):
        return o_l[c // 2, :, c % 2, 512 * ip:512 * ip + 512]

    base = dict(cst=tcst, rmask=rmask)

    def rwkv_layer(i, xsrc, k0):
        io = dict(base, pv=pvs["P%d" % i], w_rkv=W["rwkv_w_rkv"][i], w1=W["rwkv_w1"][i], w2=W["rwkv_w2"][i], a1=W["rwkv_a1"][i],
                  a2=W["rwkv_a2"][i], g1=W["rwkv_g1"][i], g2=W["rwkv_g2"][i], fm_out=fm_v, tm_out=tm_v, pc_out=pc_v,
                  bonus_out=bonus_c, g_out=g_c, v_out=vfirst_c if i == 0 else vdump_c, **xsrc)
        if i > 0:
            io.update(v1=W["rwkv_v1"][0], v2=W["rwkv_v2"][0], vfirst_in=vfirst_c, halo_all=GH_v[:, 1], hl_own=hl_v[0], hflag=EI("hflag", [128, 8]))
        with p.scope():
            TS(p, dict(pre="rwkv", npass=2, vres=i > 0), io)
        checkpoint(k0 + 0, bonus_c[0, :, 0, :])
        p.collective("AllGather", fm_c, GFM)
        p.collective("AllGather", tm_c, GTM)
        p.collective("AllGather", pc_c, GPC)
        checkpoint(k0 + 1, V(GTM, "(a b) c -> a b c", b=128)[5 * 128 + 3, :, 0:128].rearrange("p c -> p c"))
        loc_rwkv()
        checkpoint(k0 + 2, V(tm_l, "r l m k q f -> (r l m) k (q f)")[0, 0:128, 0:384])
        with p.scope():
            emit_rstage(p, dict(GFM=fm_l, GTM=tm_l, GPC=pc_l, cst=_rcst(),
                                y_out=y_v), SEQ // 64, 4)
        checkpoint(k0 + 3, V(y_c, "(a b) c -> a b c", b=128)[0, :, 0:256])
        p.collective("AllGather", y_c, GY)
        loc_y()
        checkpoint(k0 + 4, V(y_l, "r g h v t -> (r g) (h v) t")[0, 0:128, :])

    rcst_h = []
    hsel_h = []

    def _hsel():
        if not hsel_h:
            hsel_h.append(EI("hsel", [128, 2, 16]))
        return hsel_h[0]

    def _rcst():
        if not rcst_h:
            rcst_h.append(EI("rcst", [128, 3, 128]))
        return rcst_h[0]

    def _body():
        rwkv_layer(0, dict(xin=xin), 0)
        with p.scope():
            TS(p, dict(post="rwkv", mlp=True, pre=None, npass=2),
               dict(base, pv=pvs["M0"], xin=xin, yin=yin, bonus_in=bonus_c, g_in=g_c, w_o=W["rwkv_w_o"][0], w_up=W["mlp_w_up"][0],
                    w_down=W["mlp_w_down"][0], xout=x1_c, hl_out=hl_v))
        checkpoint(5, x1_c[0, :, 0, :])
        p.collective("AllGather", hl_c, GH)
        rwkv_layer(1, dict(xbody=x1_c), 10)
        with p.scope():
            TS(p, dict(post="rwkv", mlp=True, pre="kvq", npass=2),
               dict(base, pv=pvs["M1"], xbody=x1_c, yin=yin, bonus_in=bonus_c, g_in=g_c, w_o=W["rwkv_w_o"][1], w_up=W["mlp_w_up"][1],
                    w_down=W["mlp_w_down"][1], xout=x2_c, w_kvf=W["w_kvf"], bf_d=W["b_f"], w_q=W["fox_w_q"][0],
                    kT_out=k_v, vtm_out=v2_v, lf_out=lf_v, qT_out=q_v))
        checkpoint(15, x2_c[0, :, 0, :])
        p.collective("AllGather", k_c, GK)
        p.collective("AllGather", v2_c, GV)
        p.collective("AllGather", lf_c, GLF)
        p.collective("AllGather", q_c, GQ)
        loc_kv()
        loc_q()
        checkpoint(16, x2_c[0, :, 0, :])
        xs = [x2_c, x3_c, x4_c]
        for j in range(2):
            with p.scope():
                emit_astage(p, dict(GQ=q_l, GK=k_l, GV=v_l, GLF=GLF_v, hsel=_hsel(), cst=acst_h[0], maskb=acst_h[1], o_out=o_v), SEQ, 2)
            checkpoint(20 + 10 * j, o_v[:, 0, 0:512])
            p.collective("AllGather", o_c, GO)
            loc_o()
            io = dict(base, pv=pvs["M%d" % (2 + j)], xbody=xs[j], oin=oin, w_o=W["fox_w_o"][j], w_up=W["mlp_w_up"][2 + j],
                      w_down=W["mlp_w_down"][2 + j], xout=xs[j + 1])
            if j == 0:
                io.update(w_q=W["fox_w_q"][1], qT_out=q_v)
                with p.scope():
                    TS(p, dict(post="fox", mlp=True, pre="q", npass=2), io)
                checkpoint(25, x3_c[0, :, 0, :])
                p.collective("AllGather", q_c, GQ)
                loc_q()
            else:
                io.update(fin_out=fin_out)
                with p.scope():
                    TS(p, dict(post="fox", mlp=True, pre="final", npass=2), io)

    acst_h = [acst, maskb]
    try:
        _body()
    except _Stop:
        pass
    nc_ = p.finish()
    nc_._ext_names = set(p.ext_names)
    return nc_


def kernel(**I):
    I = {k: np.asarray(v) for k, v in I.items()}
    f32 = lambda a: np.ascontiguousarray(a, dtype=np.float32)
    if "nc" not in _CACHE:
        _CACHE["nc"] = build_fused()
    nc = _CACHE["nc"]
    x = I["x"][0].astype(np.float32)
    xr = np.ascontiguousarray(x.T).reshape(16, 128, SEQ)
    slabs = np.zeros((NPASS, 128, 16, TP + 1), np.float32)
    for ps in range(NPASS):
        slabs[ps, :, :, 1:] = xr[:, :, ps * TP:(ps + 1) * TP].transpose(1, 0, 2)
        if ps > 0:
            slabs[ps, :, :, 0] = xr[:, :, ps * TP - 1].T
    acst_, maskb_ = a_consts()
    shared = dict(
        pv_P0=_make_pv(I, 0, 0), pv_M0=_make_pv(I, 0, 0), pv_P1=_make_pv(I, 1, 1),
        pv_M1=_make_pv(I, 1, 1, aux=I["kv_norm"], mixn=2), pv_M2=_make_pv(I, 2, mixn=3), pv_M3=_make_pv(I, 3, aux=I["final_norm"]),
        tcst=_t_cst(), rmask=_rmask(), rcst=consts(), acst=acst_, maskb=maskb_,
        b_f=f32(I["b_f"]).reshape(16, 1))
    for k in ("rwkv_w_rkv", "rwkv_w1", "rwkv_w2", "rwkv_a1", "rwkv_a2", "rwkv_v1", "rwkv_v2", "rwkv_g1", "rwkv_g2", "rwkv_w_o",
              "w_kvf", "fox_w_q", "fox_w_o", "mlp_w_up", "mlp_w_down"):
        shared[k] = f32(I[k])
    in_maps = []
    for c in range(NCORES):
        m = dict(shared)
        m["xT"] = np.ascontiguousarray(slabs[2 * c:2 * c + 2])
        hf = np.zeros((128, 8), np.float32)
        if c > 0:
            hf[:, c - 1] = 1.0
        m["hflag"] = hf
        hs_ = np.zeros((128, 2, 16), np.float32)
        hs_[:, 0, 2 * c] = 1.0
        hs_[:, 1, 2 * c + 1] = 1.0
        m["hsel"] = hs_
        in_maps.append(m)
    in_maps = [{k: v for k, v in m.items() if k in nc._ext_names} for m in in_maps]
    res = run_bass_kernel_spmd(nc, in_maps, core_ids=list(range(NCORES)))
    if "fin_out" not in res.results[0]:
        return res.results
    fin = np.concatenate([np.asarray(r["fin_out"]) for r in res.results], axis=0)
    out = np.ascontiguousarray(fin.transpose(2, 1, 0, 3)).reshape(2048, SEQ)
    return np.ascontiguousarray(out.T)[None].astype(np.float32)
```

```python
import contextlib
import numpy as np
import concourse.bass as bass
import concourse.mybir as mybir
from concourse.bass_utils import run_bass_kernel_spmd

F32 = mybir.dt.float32
BF16 = mybir.dt.bfloat16
AF = mybir.ActivationFunctionType
ALU = mybir.AluOpType
AX = mybir.AxisListType

ENGS = ("tensor", "vector", "scalar", "gpsimd", "sync")


class Dep:
    __slots__ = ("w", "r")

    def __init__(self):
        self.w = None
        self.r = {}


class Buf:
    def __init__(self, t, ndeps=0):
        self.t = t
        self.dep = Dep()
        self.deps = [self.dep for _ in range(ndeps)]

    def __getitem__(self, idx):
        return self.t[idx]


class DView:
    def __init__(self, ap, dep=None):
        self.ap = ap
        self.dep = dep or Dep()

    def __getitem__(self, idx):
        return DView(self.ap[idx], self.dep)

    def rearrange(self, *a, **k):
        return DView(self.ap.rearrange(*a, **k), self.dep)

    def dyn(self, fn):
        base = self.ap
        return DView(lambda pid: fn(base, pid), self.dep)


class Prog:
    def __init__(self, n_dma_sems=6):
        self.nc = bass.Bass("TRN2", target_bir_lowering=False)
        self.es = contextlib.ExitStack()
        self.es0 = self.es
        self.ncc = 0
        self.rp = None
        self.ops = {e: [] for e in ENGS}
        self.cnt = {e: 0 for e in ENGS}
        self.semobj = {}
        for e in ENGS:
            if e != "sync":
                self.semobj[e] = self.es.enter_context(self.nc.semaphore("s_" + e))
        self.dq = {}
        for q in ("sync", "gpsimd", "scalar"):
            for i in range(n_dma_sems):
                self.semobj[(q, i)] = self.es.enter_context(self.nc.semaphore(f"d_{q}{i}"))
            self.dq[q] = 0
        self.nds = n_dma_sems
        self.waited = {e: {} for e in ENGS}
        self.nbuf = 0

    def dram(self, name, shape, dt, kind):
        return self.nc.dram_tensor(name, list(shape), dt, kind=kind).ap()

    def dram_internal(self, name, shape, dt):
        return DView(self.nc.dram_tensor(name, list(shape), dt).ap())

    def sbuf(self, shape, dt, name=None, ndeps=0):
        self.nbuf += 1
        t = self.es.enter_context(self.nc.sbuf_tensor(f"{name or 'sb'}_{self.nbuf}", list(shape), dt))
        return Buf(t, ndeps)

    def psum(self, shape, dt, name=None, ndeps=0):
        self.nbuf += 1
        t = self.es.enter_context(self.nc.psum_tensor(f"{name or 'ps'}_{self.nbuf}", list(shape), dt))
        return Buf(t, ndeps)

    def _need(self, e, tok):
        if tok is None:
            return
        k, v = tok
        if self.waited[e].get(k, 0) >= v:
            return
        self.waited[e][k] = v
        self.ops[e].append(("wait", k, v))

    @staticmethod
    def _deps(xs):
        out = []
        for x in xs:
            out.append(x.dep if isinstance(x, Buf) else x)
        return out

    def op(self, e, fn, R=(), W=()):
        R = self._deps(R)
        W = self._deps(W)
        for d in R:
            self._need(e, d.w)
        for d in W:
            if d.w is not None and (d.w[0] != e or e != "tensor"):
                self._need(e, d.w)
            for k, v in d.r.items():
                if k != e or e != "tensor":
                    self._need(e, (k, v))
        self.cnt[e] += 1
        tok = (e, self.cnt[e])
        self.ops[e].append(("op", fn))
        for d in R:
            d.r[e] = tok[1]
        for d in W:
            d.w = tok
            d.r = {}
        return tok

    def dma(self, q, out_ap, in_ap, R=(), W=(), slow=False):
        R = list(R)
        W = list(W)
        if isinstance(out_ap, DView):
            W.append(out_ap.dep)
            out_ap = out_ap.ap
        if isinstance(in_ap, DView):
            R.append(in_ap.dep)
            in_ap = in_ap.ap
        R = self._deps(R)
        W = self._deps(W)
        for d in R:
            self._need(q, d.w)
        for d in W:
            self._need(q, d.w)
            for k, v in d.r.items():
                self._need(q, (k, v))
        j = self.dq[q]
        self.dq[q] += 1
        s = j % self.nds
        v = 16 * (j // self.nds + 1)
        key = (q, s)
        if j >= self.nds:
            self._need(q, (key, v - 16))
        self.ops[q].append(("dma", out_ap, in_ap, key, slow))
        tok = (key, v)
        for d in R:
            d.r[key] = v
        for d in W:
            d.w = tok
            d.r = {}
        return tok

    def dma_tokens(self):
        toks = []
        for q in ("sync", "gpsimd", "scalar"):
            j = self.dq[q]
            for s in range(min(j, self.nds)):
                n = (j - 1 - s) // self.nds + 1
                toks.append(((q, s), 16 * n))
        return toks

    def barrier(self):
        toks = self.dma_tokens()
        for e2 in ENGS:
            if e2 != "sync" and self.cnt[e2] > 0:
                toks.append((e2, self.cnt[e2]))
        if self.ncc:
            toks.append(("cc", self.ncc))
        for e in ENGS:
            for t in toks:
                self._need(e, t)

    def collective(self, kind, in_dv, out_dv):
        if "cc" not in self.semobj:
            self.semobj["cc"] = self.es0.enter_context(self.nc.semaphore("cc"))
        self._need("gpsimd", in_dv.dep.w)
        self._need("gpsimd", out_dv.dep.w)
        for k, v in out_dv.dep.r.items():
            self._need("gpsimd", (k, v))
        self.ncc += 1
        ia, oa = in_dv.ap, out_dv.ap
        cc = self.semobj["cc"]
        self.ops["gpsimd"].append(("raw", lambda e: e.collective_compute(
            kind, ALU.bypass, replica_groups=[list(range(8))], ins=[ia.opt()], outs=[oa.opt()]).then_inc(cc, 1)))
        tok = ("cc", self.ncc)
        in_dv.dep.r["cc"] = self.ncc
        out_dv.dep.w = tok
        out_dv.dep.r = {}
        self._need("gpsimd", tok)
        return tok

    @contextlib.contextmanager
    def scope(self):
        outer = self.es
        self.es = contextlib.ExitStack()
        try:
            yield
        finally:
            self.barrier()
            self.flush()
            self.es.close()
            self.es = outer

    def flush(self):
        with self.nc.Block() as block:
            for e in ENGS:
                if not self.ops[e]:
                    continue

                def body(eng, e=e):
                    for o in self.ops[e]:
                        if o[0] == "wait":
                            eng.wait_ge(self.semobj[o[1]], o[2])
                        elif o[0] == "op":
                            o[1](eng).then_inc(self.semobj[e], 1)
                        elif o[0] == "raw":
                            o[1](eng)
                        else:
                            src, dst = o[2], o[1]
                            if callable(src):
                                if self.rp is None:
                                    self.rp = self.es0.enter_context(self.nc.sync.register("rp"))
                                    self.ro = self.es0.enter_context(self.nc.sync.register("ro"))
                                    eng.reg_mov(self.rp, eng.partition_id())
                                ap0, ap1 = src(0), src(1)
                                eng.reg_mul(self.ro, self.rp, ap1.offset - ap0.offset)
                                eng.reg_add(self.ro, self.ro, ap0.offset)
                                src = bass.AP(ap0.tensor, self.ro, [list(v) for v in ap0.ap])
                            kw = {"allow_slow_non_contiguous": True} if (len(o) > 4 and o[4]) else {}
                            eng.dma_start(out=dst, in_=src, **kw).then_inc(self.semobj[o[3]], 16)

                getattr(block, e)(body)
        self.ops = {e: [] for e in ENGS}

    def finish(self):
        for t in self.dma_tokens():
            self._need(t[0][0], t)
        self.flush()
        self.es.close()
        if self.es0 is not self.es:
            self.es0.close()
        return self.nc

    def finish_old(self):
        with self.nc.Block() as block:
            for e in ENGS:
                if not self.ops[e]:
                    continue

                def body(eng, e=e):
                    for o in self.ops[e]:
                        if o[0] == "wait":
                            eng.wait_ge(self.semobj[o[1]], o[2])
                        elif o[0] == "op":
                            o[1](eng).then_inc(self.semobj[e], 1)
                        else:
                            eng.dma_start(out=o[1], in_=o[2]).then_inc(self.semobj[o[3]], 16)

                getattr(block, e)(body)
        self.es.close()
        return self.nc

    def mm(self, out, lhsT, rhs, start, stop, R=(), W=()):
        return self.op("tensor", lambda e: e.matmul(out, lhsT, rhs, start=start, stop=stop), R, W)

    def act(self, out, in_, func, R=(), W=(), bias=None, scale=1.0, accum_out=None, eng="scalar"):
        kw = {}
        if bias is not None:
            kw["bias"] = bias
        if accum_out is not None:
            kw["accum_out"] = accum_out
        return self.op(eng, lambda e: e.activation(out=out, in_=in_, func=func, scale=scale, **kw), R, W)

    def tt(self, eng, out, in0, in1, op, R=(), W=()):
        return self.op(eng, lambda e: e.tensor_tensor(out=out, in0=in0, in1=in1, op=op), R, W)

    def cp(self, eng, out, in_, R=(), W=()):
        if eng == "scalar":
            return self.op(eng, lambda e: e.copy(out, in_), R, W)
        return self.op(eng, lambda e: e.tensor_copy(out, in_), R, W)

    def ts(self, eng, out, in0, s1, s2, op0, op1=None, R=(), W=()):
        if op1 is None:
            return self.op(eng, lambda e: e.tensor_scalar(out, in0, s1, None, op0), R, W)
        return self.op(eng, lambda e: e.tensor_scalar(out, in0, s1, s2, op0, op1), R, W)

    def stt(self, out, in0, scalar, in1, op0, op1, R=(), W=(), eng="vector"):
        return self.op(eng, lambda e: e.scalar_tensor_tensor(out=out, in0=in0, scalar=scalar, in1=in1, op0=op0, op1=op1), R, W)

    def memset(self, eng, ap, val, W=()):
        return self.op(eng, lambda e: e.memset(ap, val), (), W)

import numpy as np

C = 64
NH = 4
HD = 64


def emit_rstage(p, io, NCH, G=4):
    nc = p.nc
    NG = NCH // G
    GFM, GTM, GPC, cst, yT = io["GFM"], io["GTM"], io["GPC"], io["cst"], io["y_out"]

    cst_sb = p.sbuf([128, 3, 128], F32, name="rcst_sb")
    p.dma("sync", cst_sb[:], cst, W=[cst_sb])
    pc_sb = p.sbuf([64, NH, NCH], F32)
    for rank in range(8):
        for lp in range(2):
            for mm in range(2):
                n0 = (rank * 2 + lp) * 8
                src = GPC[rank, lp, mm].rearrange("(hh k) c -> k hh c", hh=2)
                p.dma("sync", pc_sb[:, 2 * mm:2 * mm + 2, n0:n0 + 8], src, W=[pc_sb])
    maskS4 = p.sbuf([128, NH, 128], F32)
    maskL4 = p.sbuf([64, NH, 64], F32)
    identF4 = p.sbuf([64, NH, 64], F32)
    ident_bf = p.sbuf([64, 64], BF16)
    for h in range(NH):
        p.cp("vector", maskS4[:, h, :], cst_sb[:, 0, :], R=[cst_sb], W=[maskS4])
        p.cp("vector", maskL4[:, h, :], cst_sb[0:64, 1, 0:64], R=[cst_sb], W=[maskL4])
        p.cp("vector", identF4[:, h, :], cst_sb[0:64, 2, 0:64], R=[cst_sb], W=[identF4])
    p.cp("vector", ident_bf[:], cst_sb[0:64, 2, 0:64], R=[cst_sb], W=[ident_bf])

    Hs = [p.sbuf([64, NH, 64], F32, name=f"H{i}") for i in range(2)]
    p.memset("vector", Hs[0][:], 0.0, W=[Hs[0]])

    NB = 2
    FMg = [p.sbuf([64, NH, G, 4, 64], BF16, name=f"FMg{i}") for i in range(NB)]
    TMg = [p.sbuf([128, G, NH, 64], BF16, name=f"TMg{i}") for i in range(NB)]
    UVg = [p.sbuf([128, G, NH, 64], BF16, name=f"UVg{i}") for i in range(NB)]
    UVv = [Dep() for _ in range(NB)]
    UVu = [[Dep() for _ in range(G)] for _ in range(NB)]
    OUTg = [p.sbuf([64, NH, G * 64], F32, name=f"OUTg{i}") for i in range(NB)]
    ZVg = [p.sbuf([128, G, NH, 64], BF16, name=f"ZVg{i}") for i in range(NB)]
    for i in range(NB):
        p.memset("gpsimd", ZVg[i][0:64], 0.0, W=[ZVg[i]])

    def perchunk(shape, dt, name, n=2):
        return [p.sbuf(shape, dt, name=f"{name}{i}") for i in range(n)]

    S_f = perchunk([128, NH, 128], F32, "S_f")
    S_bf = perchunk([128, NH, 128], BF16, "S_bf")
    Pa = perchunk([64, NH, 64], F32, "Pa")
    PaT = perchunk([64, NH, 64], F32, "PaT")
    Ya = perchunk([64, NH, 128], F32, "Ya")
    W_bf = perchunk([64, NH, 64], BF16, "W_bf")
    QT_f = perchunk([64, NH, 64], F32, "QT_f")
    O0_f = perchunk([64, NH, 64], F32, "O0_f")
    MT_f = perchunk([64, NH, 64], F32, "MT_f")
    N0_f = perchunk([64, NH, 64], F32, "N0_f")
    diagP = perchunk([64, NH, 64], F32, "diagP")

    bS = p.psum([128, NH, 128], F32, name="bS")
    bX = p.psum([128, NH, 128], F32, name="bX")
    bY = p.psum([128, NH, 128], F32, name="bY")
    bP = p.psum([128, 2, NH, 64], F32, name="bP", ndeps=2)
    bQ = p.psum([128, 2, NH, 64], F32, name="bQ", ndeps=2)
    bO = p.psum([128, 2, NH, 64], F32, name="bO", ndeps=2)
    bN = p.psum([128, 2, NH, 64], F32, name="bN", ndeps=2)
    bH = p.psum([128, 2, NH, 64], F32, name="bH", ndeps=2)

    def flat(ap):
        return ap.rearrange("p a b -> p (a b)")

    def load_group(g):
        b = g % NB
        rank, lp, t0 = g // 4, (g // 2) % 2, (g % 2) * 256
        for mm in range(2):
            c0 = (g % 2) * 4
            for hh in range(2):
                p.dma("sync", FMg[b][:, 2 * mm + hh], GFM[rank, lp, mm, hh * 64:(hh + 1) * 64, c0:c0 + 4], W=[FMg[b]])
            for q in range(3):
                src = GTM[rank, lp, mm, t0:t0 + 256, q, :].rearrange("(c s) (hh i) -> s c hh i", s=64, hh=2)
                if q < 2:
                    p.dma("sync", TMg[b][q * 64:(q + 1) * 64, :, 2 * mm:2 * mm + 2, :], src, W=[TMg[b]])
                else:
                    p.dma("sync", UVg[b][64:128, :, 2 * mm:2 * mm + 2, :], src, W=[UVv[b]])
                    p.dma("sync", ZVg[b][64:128, :, 2 * mm:2 * mm + 2, :], src, W=[ZVg[b]])

    def pre(n):
        g, c = divmod(n, G)
        b = g % NB
        i = n % 2
        FM = FMg[b]
        TM = TMg[b]
        UV = UVg[b]
        for h in range(NH):
            p.mm(bS[:, h, :], flat(FM[:, h, c, 2:4, :]), flat(FM[:, h, c, 0:2, :]), True, True, R=[FM], W=[bS])
        p.tt("vector", S_f[i][:], bS[:], maskS4[:], ALU.mult, R=[bS, maskS4], W=[S_f[i]])
        p.cp("gpsimd", S_bf[i][:], S_f[i][:], R=[S_f[i]], W=[S_bf[i]])
        for h in range(NH):
            p.mm(bQ[0:64, 0, h, :], FM[:, h, c, 0, :], FM[:, h, c, 2, :], True, True, R=[FM], W=[bQ.deps[0]])
        Pj, PjT = Pa[0], PaT[0]
        p.tt("vector", Pj[:], bQ[0:64, 0], maskL4[:], ALU.mult, R=[bQ.deps[0], maskL4], W=[Pj])
        p.cp("scalar", PjT[:], S_f[i][0:64, :, 0:64], R=[S_f[i]], W=[PjT])
        for h in range(NH):
            p.mm(bX[0:64, h, 0:64], FM[:, h, c, 0, :], ident_bf[:], True, True, R=[FM, ident_bf], W=[bX])
            p.mm(bX[0:64, h, 64:128], S_bf[i][:, h, 0:64], ZVg[b][:, c, h, :], True, True,
                 R=[S_bf[i], ZVg[b]], W=[bX])
        Y = Ya[0]
        p.cp("scalar", Y[:], bX[0:64], R=[bX], W=[Y])
        for j in range(6):
            Pj, PjT = Pa[j % 2], PaT[j % 2]
            Pn, PnT = Pa[(j + 1) % 2], PaT[(j + 1) % 2]
            Yc, Yn = Ya[j % 2], Ya[(j + 1) % 2]
            for h in range(NH):
                p.mm(bY[0:64, h, :], PjT[:, h, :], Yc[:, h, :], True, True, R=[PjT, Yc], W=[bY])
            p.tt("vector", Yn[:], bY[0:64], Yc[:], ALU.add, R=[bY, Yc], W=[Yn])
            if j < 5:
                for h in range(NH):
                    p.mm(bP[0:64, 1, h, :], Pj[:, h, :], PjT[:, h, :], True, True, R=[Pj, PjT], W=[bP.deps[1]])
                p.cp("scalar", PnT[:], bP[0:64, 1], R=[bP.deps[1]], W=[PnT])
                if j < 4:
                    for h in range(NH):
                        p.mm(bP[0:64, 0, h, :], PjT[:, h, :], Pj[:, h, :], True, True, R=[Pj, PjT], W=[bP.deps[0]])
                    p.cp("scalar", Pn[:], bP[0:64, 0], R=[bP.deps[0]], W=[Pn])
        Y6 = Ya[0]
        p.cp("gpsimd", W_bf[i][:], Y6[:, :, 0:64], R=[Y6], W=[W_bf[i]])
        p.cp("gpsimd", UV[0:64, c], Y6[:, :, 64:128], R=[Y6], W=[UVu[b][c]])
        p.tt("gpsimd", diagP[i][:], identF4[:], pc_sb[:, :, n:n + 1].broadcast_to([64, NH, 64]), ALU.mult,
             R=[identF4, pc_sb], W=[diagP[i]])
        for h in range(NH):
            p.mm(bQ[0:64, 1, h, :], W_bf[i][:, h, :], S_bf[i][0:64, h, 64:128], True, False, R=[W_bf[i], S_bf[i]], W=[bQ.deps[1]])
            p.mm(bQ[0:64, 1, h, :], ident_bf[:], FM[:, h, c, 1, :], False, True, R=[ident_bf, FM], W=[bQ.deps[1]])
        p.cp("scalar", QT_f[i][:], bQ[0:64, 1], R=[bQ.deps[1]], W=[QT_f[i]])
        for h in range(NH):
            p.mm(bO[0:64, 0, h, :], UV[:, c, h, :], S_bf[i][:, h, 64:128], True, True,
                 R=[UVu[b][c], UVv[b], S_bf[i]], W=[bO.deps[0]])
        p.cp("scalar", O0_f[i][:], bO[0:64, 0], R=[bO.deps[0]], W=[O0_f[i]])
        for h in range(NH):
            p.mm(bO[0:64, 1, h, :], W_bf[i][:, h, :], TM[0:64, c, h, :], True, True, R=[W_bf[i], TM], W=[bO.deps[1]])
        p.tt("vector", MT_f[i][:], bO[0:64, 1], diagP[i][:], ALU.add, R=[bO.deps[1], diagP[i]], W=[MT_f[i]])
        for h in range(NH):
            p.mm(bN[0:64, 0, h, :], TM[:, c, h, :], UV[:, c, h, :], True, True,
                 R=[TM, UVu[b][c], UVv[b]], W=[bN.deps[0]])
        p.cp("scalar", N0_f[i][:], bN[0:64, 0], R=[bN.deps[0]], W=[N0_f[i]])

    def seq(n):
        g, c = divmod(n, G)
        b = g % NB
        i = n % 2
        H0, H1 = Hs[n % 2], Hs[(n + 1) % 2]
        for h in range(NH):
            p.mm(bH[0:64, 0, h, :], H0[:, h, :], QT_f[i][:, h, :], True, True, R=[H0, QT_f[i]], W=[bH.deps[0]])
        for h in range(NH):
            p.mm(bH[0:64, 1, h, :], MT_f[i][:, h, :], H0[:, h, :], True, True, R=[H0, MT_f[i]], W=[bH.deps[1]])
        p.tt("vector", H1[:], bH[0:64, 1], N0_f[i][:], ALU.add, R=[bH.deps[1], N0_f[i]], W=[H1])
        p.tt("vector", OUTg[b][:, :, c * 64:(c + 1) * 64], bH[0:64, 0], O0_f[i][:], ALU.add, R=[bH.deps[0], O0_f[i]], W=[OUTg[b]])
        if c == G - 1:
            p.dma("sync", yT[g].rearrange("hl v t -> v hl t"), OUTg[b][:], R=[OUTg[b]])

    load_group(0)
    for n in range(NCH):
        g, c = divmod(n, G)
        if c == 0 and g + 1 < NG:
            load_group(g + 1)
        pre(n)
        if n > 0:
            seq(n - 1)
    seq(NCH - 1)


def consts():
    cst = np.zeros((128, 3, 128), np.float32)
    s = np.arange(64)[:, None]
    t = np.arange(64)[None, :]
    strict = (s < t).astype(np.float32)
    incl = (s <= t).astype(np.float32)
    cst[0:64, 0, 0:64] = strict
    cst[0:64, 0, 64:128] = incl
    cst[64:128, 0, 0:64] = strict
    cst[64:128, 0, 64:128] = incl
    cst[0:64, 1, 0:64] = (np.arange(64)[:, None] > np.arange(64)[None, :]).astype(np.float32)
    cst[:, 2, :] = np.eye(128, dtype=np.float32)
    return cst

import numpy as np

D = 2048
NC16 = 16
TP = 512
DFF = 8192
RMS_EPS = 1e-6
LNX_EPS = 1e-5 * 64

PV = {n: i for i, n in enumerate(
    ["mix_norm", "ffn_norm", "aux_norm", "mix_r", "mix_k", "mix_v", "mix_w", "mix_a", "mix_g",
     "w0", "a0", "v0", "k_k", "k_a", "r_k", "lnx_w", "lnx_b"])}
NPV = len(PV)


class TS:
    def __init__(self, p, cfg, io):
        self.cfg = cfg
        self.p = p
        self.io = io
        self.NP = cfg["npass"]
        for k, v in io.items():
            setattr(self, k, v)
        self.pv_d, self.cst_d, self.rmask_d = io["pv"], io["cst"], io["rmask"]
        self.build()

    def ps(self):
        self._psi = (self._psi + 1) % len(self.PS)
        return self.PS[self._psi]

    def linear(self, W, KC, kp, mcols, rhs_fn, rhs_deps, consume, c0=0):
        p = self.p
        for (m0, mw) in mcols:
            self._wi = (self._wi + 1) % len(self.WB)
            wb = self.WB[self._wi]
            for k0 in range(0, KC, 16):
                k1 = min(KC, k0 + 16)
                src = W[k0 * kp:k1 * kp, m0:m0 + mw].rearrange("(c p) n -> p c n", p=kp)
                p.dma("gpsimd", wb[0:kp, k0:k1, 0:mw], src, W=[wb])
            acc = self.ps()
            for c in range(KC):
                p.mm(acc[0:mw, :], wb[0:kp, c, 0:mw], rhs_fn(c), c == 0, c == KC - 1, R=[wb] + rhs_deps, W=[acc])
            consume(m0, mw, acc)

    def rmsnorm(self, xT, gain_slot, out_bf, lo, hi, out_off=0):
        p = self.p
        n = hi - lo
        acc = self.ps()
        for c in range(NC16):
            sq = self.SQ[c % 2]
            p.act(sq[:, 0:n], xT[:, c, lo:hi], AF.Square, R=[xT], W=[sq])
            p.mm(acc[:, 0:n], self.ones_bf[:], sq[:, 0:n], c == 0, c == NC16 - 1, R=[sq, self.ones_bf], W=[acc])
        rstd = self.rstd
        p.act(rstd[:, 0:n], acc[:, 0:n], AF.Sqrt, R=[acc, self.eps_rms], W=[rstd], scale=1.0 / D, bias=self.eps_rms[:])
        p.op("vector", lambda e: e.reciprocal(rstd[:, 0:n], rstd[:, 0:n]), R=[rstd], W=[rstd])
        for c in range(NC16):
            p.stt(out_bf[:, c, out_off + lo:out_off + hi], xT[:, c, lo:hi], self.pv[:, gain_slot, c:c + 1], rstd[:, 0:n],
                  ALU.mult, ALU.mult, R=[xT, self.pv, rstd], W=[out_bf])

    def build(self):
        p = self.p
        cfg = self.cfg
        self.PS = [p.psum([128, 512], F32, name=f"PS{i}") for i in range(7)]
        self._psi = -1
        self.WB = [p.sbuf([128, 16, 128], BF16, name=f"WB{i}") for i in range(3)]
        self._wi = -1
        self.pv = p.sbuf([128, NPV, NC16], F32, name="pv_sb")
        p.dma("sync", self.pv[:], self.pv_d, W=[self.pv])
        cst = p.sbuf([128, 4, 128], F32, name="cst_sb")
        p.dma("sync", cst[:], self.cst_d, W=[cst])
        self.ones_bf = p.sbuf([128, 128], BF16, name="ones_bf")
        p.cp("vector", self.ones_bf[:], cst[:, 0, :], R=[cst], W=[self.ones_bf])
        self.blk_f = cst
        self.ident_bf = p.sbuf([128, 128], BF16, name="ident_bf")
        p.cp("vector", self.ident_bf[:], cst[:, 2, :], R=[cst], W=[self.ident_bf])
        self.cst = cst
        self.eps_rms = p.sbuf([128, 1], F32, name="eps_rms")
        p.memset("vector", self.eps_rms[:], RMS_EPS, W=[self.eps_rms])
        self.eps_lnx = p.sbuf([128, 1], F32, name="eps_lnx")
        p.memset("vector", self.eps_lnx[:], LNX_EPS, W=[self.eps_lnx])
        self.SQ = [p.sbuf([128, 512], BF16, name=f"SQ{i}") for i in range(2)]
        self.rstd = p.sbuf([128, 513], F32, name="rstd")
        self.xT = p.sbuf([128, NC16, TP + 1], F32, name="xT_sb")
        if self.io.get("hl_out") is not None:
            self.hl_sb = p.sbuf([128, NC16], F32, name="hl_sb")
        if self.io.get("hflag") is not None:
            self.hflag_sb = p.sbuf([128, 8], F32, name="hflag_sb")
            p.dma("sync", self.hflag_sb[:], self.io["hflag"], W=[self.hflag_sb])
        isr = cfg.get("pre") == "rwkv"
        if isr:
            self.hbf = p.sbuf([128, NC16, TP + 1], BF16, name="hbf")
        self.T = [p.sbuf([128, TP], F32, name=f"T{i}") for i in range(12 if isr else 8)]
        self.B = [p.sbuf([128, NC16, TP], BF16, name=f"B{i}") for i in range(2 if isr else 1)]
        if not isr:
            self.B.append(self.B[0])
        if cfg.get("mlp"):
            self.hid = p.sbuf([128, 64, TP], BF16, name="hid")
            self.WD = [p.sbuf([128, 64, 128], BF16, name=f"WD{i}") for i in range(2)]
        for ps_ in range(self.NP):
            self.one_pass(ps_)

    def one_pass(self, ip):
        p = self.p
        cfg = self.cfg
        xT = self.xT
        if self.io.get("xin") is not None:
            p.dma("sync", xT[:], self.xin[ip], W=[xT])
        else:
            p.dma("sync", xT[:, :, 1:TP + 1], self.xbody[ip], W=[xT])
            if cfg.get("pre") == "rwkv":
                if ip == 0:
                    hs = self.T[0]
                    for r in range(8):
                        p.dma("sync", hs[:, r * NC16:(r + 1) * NC16], self.io["halo_all"][r], W=[hs])
                    p.ts("vector", xT[:, :, 0], hs[:, 0:NC16], self.hflag_sb[:, 0:1], None, ALU.mult, R=[hs, self.hflag_sb], W=[xT])
                    for r in range(1, 8):
                        p.stt(xT[:, :, 0], hs[:, r * NC16:(r + 1) * NC16], self.hflag_sb[:, r:r + 1], xT[:, :, 0], ALU.mult, ALU.add,
                              R=[hs, self.hflag_sb, xT], W=[xT])
                else:
                    hs = self.T[0]
                    p.dma("sync", hs[:, 0:NC16], self.io["hl_own"], W=[hs])
                    p.cp("vector", xT[:, :, 0], hs[:, 0:NC16], R=[hs], W=[xT])
        if cfg.get("post") == "rwkv":
            self.post_rwkv(ip)
        if cfg.get("post") == "fox":
            self.post_fox(ip)
        if cfg.get("mlp"):
            self.mlp(ip)
        if cfg.get("post") or cfg.get("mlp"):
            p.dma("sync", self.xout[ip], xT[:, :, 1:TP + 1], R=[xT])
            if self.io.get("hl_out") is not None:
                p.cp("vector", self.hl_sb[:], xT[:, :, TP], R=[xT], W=[self.hl_sb])
                p.dma("sync", self.io["hl_out"][ip], self.hl_sb[:], R=[self.hl_sb])
        pre = cfg.get("pre")
        if pre == "rwkv":
            self.pre_rwkv(ip)
        if pre == "kvq":
            self.pre_kv(ip)
        if pre in ("kvq", "q"):
            self.pre_q(ip)
        if pre == "final":
            self.final(ip)

    def add_to_x(self, m0, mw, acc):
        p = self.p
        c = m0 // 128
        p.tt("vector", self.xT[:, c, 1:TP + 1], self.xT[:, c, 1:TP + 1], acc[:, :], ALU.add, R=[acc, self.xT], W=[self.xT])

    def post_fox(self, ip):
        p = self.p
        z = self.B[0]
        st = self.T[0]
        for c in range(NC16):
            p.dma("sync", st[:], self.oin(ip, c), W=[st])
            p.cp("vector", z[:, c, :], st[:], R=[st], W=[z])
        self.linear(self.w_o, NC16, 128, [(m * 128, 128) for m in range(NC16)], lambda c: z[:, c, :], [z], self.add_to_x)

    def post_rwkv(self, ip):
        p = self.p
        z = self.B[0]
        y, bo, g, t1, t2, mu, rs = self.T[0:7]
        blk = self.cst[:, 1, :]
        for c in range(NC16):
            p.dma("sync", y[:].rearrange("p (g t) -> p g t", g=2), self.yin(ip, c), W=[y])
            p.dma("sync", bo[:], self.bonus_in[ip, :, c, :], W=[bo])
            p.dma("sync", g[:], self.g_in[ip, :, c, :], W=[g])
            a1 = self.ps()
            p.mm(a1[:], blk, y[:], True, True, R=[self.cst, y], W=[a1])
            p.stt(t1[:], a1[:], -1.0 / 64, y[:], ALU.mult, ALU.add, R=[a1, y], W=[t1])
            p.act(t2[:], t1[:], AF.Square, R=[t1], W=[t2])
            a2 = self.ps()
            p.mm(a2[:], blk, t2[:], True, True, R=[self.cst, t2], W=[a2])
            p.act(rs[:], a2[:], AF.Sqrt, R=[a2, self.eps_lnx], W=[rs], scale=1.0 / 64, bias=self.eps_lnx[:])
            p.op("vector", lambda e: e.reciprocal(rs[:], rs[:]), R=[rs], W=[rs])
            p.tt("vector", t1[:], t1[:], rs[:], ALU.mult, R=[t1, rs], W=[t1])
            p.ts("vector", t1[:], t1[:], self.pv[:, PV["lnx_w"], c:c + 1], self.pv[:, PV["lnx_b"], c:c + 1], ALU.mult, ALU.add,
                 R=[t1, self.pv], W=[t1])
            p.tt("vector", t1[:], t1[:], bo[:], ALU.add, R=[t1, bo], W=[t1])
            p.tt("vector", z[:, c, :], t1[:], g[:], ALU.mult, R=[t1, g], W=[z])
        self.linear(self.w_o, NC16, 128, [(m * 128, 128) for m in range(NC16)], lambda c: z[:, c, :], [z], self.add_to_x)

    def mlp(self, ip):
        p = self.p
        h2 = self.B[1]
        self.rmsnorm(self.xT, PV["ffn_norm"], h2, 1, TP + 1, out_off=-1)
        hid = self.hid
        t = self.T[7]

        def up_consume(m0, mw, acc):
            m = m0 // 128
            p.act(t[:], acc[:], AF.Relu, R=[acc], W=[t])
            p.tt("vector", hid[:, m, :], t[:], t[:], ALU.mult, R=[t], W=[hid])

        self.linear(self.w_up, NC16, 128, [(m * 128, 128) for m in range(64)], lambda c: h2[:, c, :], [h2], up_consume)
        for m in range(NC16):
            wb = self.WD[m % 2]
            for k0 in range(0, 64, 16):
                src = self.w_down[k0 * 128:(k0 + 16) * 128, m * 128:(m + 1) * 128].rearrange("(c p) n -> p c n", p=128)
                p.dma("gpsimd", wb[:, k0:k0 + 16, :], src, W=[wb])
            acc = self.ps()
            for c in range(64):
                p.mm(acc[:], wb[:, c, :], hid[:, c, :], c == 0, c == 63, R=[wb, hid], W=[acc])
            self.add_to_x(m * 128, 128, acc)

    def final(self, ip):
        p = self.p
        xT = self.xT
        acc = self.ps()
        for c in range(NC16):
            sq = self.SQ[c % 2]
            p.act(sq[:], xT[:, c, 1:TP + 1], AF.Square, R=[xT], W=[sq])
            p.mm(acc[:], self.ones_bf[:], sq[:], c == 0, c == NC16 - 1, R=[sq, self.ones_bf], W=[acc])
        rstd = self.rstd
        p.act(rstd[:, 0:TP], acc[:], AF.Sqrt, R=[acc, self.eps_rms], W=[rstd], scale=1.0 / D, bias=self.eps_rms[:])
        p.op("vector", lambda e: e.reciprocal(rstd[:, 0:TP], rstd[:, 0:TP]), R=[rstd], W=[rstd])
        for c in range(NC16):
            o = self.T[c % 4]
            p.stt(o[:], xT[:, c, 1:TP + 1], self.pv[:, PV["aux_norm"], c:c + 1], rstd[:, 0:TP], ALU.mult, ALU.mult,
                  R=[xT, self.pv, rstd], W=[o])
            p.dma("sync", self.fin_out[ip, :, c, :], o[:], R=[o])

    def transpose_out(self, src_bf, dst, R):
        p = self.p
        pt = self.PT
        for tb in range(4):
            p.op("tensor", lambda e, tb=tb: e.transpose(pt[:, tb, :], src_bf[:, tb * 128:(tb + 1) * 128], self.ident_bf[:]),
                 R=R + [self.ident_bf], W=[pt])
        sb = self.TSB[self._tsi % 2]
        self._tsi += 1
        p.cp("scalar", sb[:], pt[:], R=[pt], W=[sb])
        p.dma("sync", dst.rearrange("t p f -> p t f"), sb[:], R=[sb])

    def pre_q(self, ip):
        p = self.p
        if not hasattr(self, "OB"):
            self.alloc_out_bufs()
        h = self.B[0]
        self.rmsnorm(self.xT, PV["mix_norm"], h, 1, TP + 1, out_off=-1)
        scale = 128 ** -0.5

        def consume(m0, mw, acc):
            m = m0 // 128
            o = self.OB[m % 2]
            p.act(o[:], acc[:], AF.Copy, R=[acc], W=[o], scale=scale)
            p.dma("sync", self.qT_out[ip, m], o[:], R=[o])

        self.linear(self.w_q, NC16, 128, [(m * 128, 128) for m in range(NC16)], lambda c: h[:, c, :], [h], consume)

    def pre_kv(self, ip):
        p = self.p
        if not hasattr(self, "OB"):
            self.alloc_out_bufs()
        h = self.B[1]
        self.rmsnorm(self.xT, PV["aux_norm"], h, 1, TP + 1, out_off=-1)

        def consume_k(m0, mw, acc):
            m = m0 // 128
            o = self.OB[m % 2]
            p.cp("scalar", o[:], acc[:], R=[acc], W=[o])
            p.dma("sync", self.kT_out[ip, m], o[:], R=[o])

        def consume_v(m0, mw, acc):
            m = (m0 - D) // 128
            o = self.OB[m % 2]
            p.cp("scalar", o[:], acc[:], R=[acc], W=[o])
            self.transpose_out(o, self.vtm_out[ip, m], [o])

        def consume_f(m0, mw, acc):
            t, t2 = self.T[0], self.T[1]
            p.act(t[0:16, :], acc[0:16, :], AF.Sigmoid, R=[acc, self.bf_sb], W=[t], bias=self.bf_sb[:])
            p.act(t2[0:16, :], t[0:16, :], AF.Ln, R=[t], W=[t2])
            pt = self.ps()
            for tb in range(4):
                p.op("tensor", lambda e, tb=tb: e.transpose(pt[:, tb * 16:(tb + 1) * 16], t2[0:16, tb * 128:(tb + 1) * 128], self.cst[0:16, 2, 0:16]),
                     R=[t2, self.cst], W=[pt])
            t3 = self.T[2]
            p.cp("vector", t3[:, 0:64], pt[:, 0:64], R=[pt], W=[t3])
            p.dma("sync", self.lf_out[ip].rearrange("b s h -> s b h"), t3[:, 0:64].rearrange("p (b h) -> p b h", b=4), R=[t3])

        self.linear(self.w_kvf, NC16, 128, [(m * 128, 128) for m in range(NC16)], lambda c: h[:, c, :], [h], consume_k)
        self.linear(self.w_kvf, NC16, 128, [(D + m * 128, 128) for m in range(NC16)], lambda c: h[:, c, :], [h], consume_v)
        self.linear(self.w_kvf, NC16, 128, [(2 * D, 16)], lambda c: h[:, c, :], [h], consume_f)

    def alloc_out_bufs(self):
        p = self.p
        self.OB = [p.sbuf([128, TP], BF16, name=f"OB{i}") for i in range(2)]
        self.PT = p.psum([128, 4, 128], BF16, name="PTb")
        self.TSB = [p.sbuf([128, 4, 128], BF16, name=f"TSB{i}") for i in range(2)]
        self._tsi = 0
        if self.cfg.get("pre") == "kvq":
            self.bf_sb = p.sbuf([16, 1], F32, name="bf_sb")
            p.dma("sync", self.bf_sb[:], self.bf_d, W=[self.bf_sb])

    def pre_rwkv(self, ip):
        p = self.p
        cfg = self.cfg
        if not hasattr(self, "OB"):
            self.alloc_out_bufs()
            self.rmask = p.sbuf([128, TP], F32, name="rmask_sb")
            p.dma("sync", self.rmask[:], self.rmask_d, W=[self.rmask])
            self.dbf = p.sbuf([128, NC16, TP], BF16, name="dbf")
            self.XV = p.sbuf([128, NC16, TP], BF16, name="XV")
            self.lora = {n: p.sbuf([128, 2, TP], BF16, name="lo_" + n) for n in ("w", "a", "v", "g")}
            self.W2 = {n: p.sbuf([128, 2 if n == "g" else 1, D], BF16, name="w2_" + n) for n in ("w", "a", "v", "g")}
            self.FMo = [p.sbuf([128, 4, TP], BF16, name=f"FMo{i}") for i in range(2)]
            self.pco = p.sbuf([128, NC16, TP // 64], F32, name="pco")
            self.TM3 = [p.sbuf([128, 3, TP], BF16, name=f"TM3{i}") for i in range(2)]
            for n, w, kk in (("w", self.w2, 96), ("a", self.a2, 96), ("g", self.g2, 256)) + (
                    (("v", self.v2, 64),) if cfg.get("vres") else ()):
                for j in range((kk + 127) // 128):
                    r0, r1 = j * 128, min(kk, (j + 1) * 128)
                    p.dma("gpsimd", self.W2[n][0:r1 - r0, j, :], w[r0:r1, :], W=[self.W2[n]])
        xT, hbf, dbf = self.xT, self.hbf, self.dbf
        pv = self.pv
        self.rmsnorm(xT, PV["mix_norm"], hbf, 0, 1)
        self.rmsnorm(xT, PV["mix_norm"], hbf, 1, TP + 1)
        for c in range(NC16):
            p.tt("vector", dbf[:, c, :], hbf[:, c, 0:TP], hbf[:, c, 1:TP + 1], ALU.subtract, R=[hbf], W=[dbf])

        def make_xs(slot, dst):
            for c in range(NC16):
                p.stt(dst[:, c, :], dbf[:, c, :], pv[:, slot, c:c + 1], hbf[:, c, 1:TP + 1], ALU.mult, ALU.add,
                      R=[dbf, pv, hbf], W=[dst])

        def lora1(name, slot, W, width, func):
            xs = self.B[0]
            make_xs(slot, xs)
            lo = self.lora[name]

            def consume(m0, mw, acc):
                j = m0 // 128
                p.act(lo[0:mw, j, :], acc[0:mw, :], func, R=[acc], W=[lo])

            self.linear(W, NC16, 128, [(j * 128, min(128, width - j * 128)) for j in range((width + 127) // 128)],
                        lambda c: xs[:, c, :], [xs], consume)

        lora1("w", PV["mix_w"], self.w1, 96, AF.Tanh)
        lora1("a", PV["mix_a"], self.a1, 96, AF.Copy)
        lora1("g", PV["mix_g"], self.g1, 256, AF.Sigmoid)
        xr, xk, xv = self.B[0], self.B[1], self.XV
        make_xs(PV["mix_v"], xv)
        if cfg.get("vres"):
            lo = self.lora["v"]

            def consume_v1(m0, mw, acc):
                p.cp("scalar", lo[0:mw, 0, :], acc[0:mw, :], R=[acc], W=[lo])

            self.linear(self.v1, NC16, 128, [(0, 64)], lambda c: xv[:, c, :], [xv], consume_v1)
        make_xs(PV["mix_r"], xr)
        make_xs(PV["mix_k"], xk)

        T = self.T
        blk = self.cst[:, 1, :]
        for m in range(NC16):
            res = {}

            def grab(name):
                def consume(m0, mw, acc):
                    res[name] = acc
                return consume

            mc = [(m * 128, 128)]
            self.linear(self.w_rkv[0], NC16, 128, mc, lambda c: xr[:, c, :], [xr], grab("r"))
            self.linear(self.w_rkv[1], NC16, 128, mc, lambda c: xk[:, c, :], [xk], grab("k"))
            self.linear(self.w_rkv[2], NC16, 128, mc, lambda c: xv[:, c, :], [xv], grab("v"))
            r_f, k_f, v_f, lw, cum, al, kk, t1, t2, t3, g_f, bon = T[0:12]
            p.cp("scalar", r_f[:], res["r"][:], R=[res["r"]], W=[r_f])
            p.cp("scalar", k_f[:], res["k"][:], R=[res["k"]], W=[k_f])
            p.cp("scalar", v_f[:], res["v"][:], R=[res["v"]], W=[v_f])
            def lora2(name, kk_, nj):
                acc = self.ps()
                for j in range(nj):
                    kp = min(128, kk_ - j * 128)
                    p.mm(acc[:], self.W2[name][0:kp, j, m * 128:(m + 1) * 128], self.lora[name][0:kp, j, :], j == 0, j == nj - 1,
                         R=[self.W2[name], self.lora[name]], W=[acc])
                return acc
            aw = lora2("w", 96, 1)
            p.act(lw[:], aw[:], AF.Sigmoid, R=[aw, pv], W=[lw], bias=pv[:, PV["w0"], m:m + 1])
            p.ts("vector", lw[:], lw[:], -float(np.exp(-0.5)), None, ALU.mult, R=[lw], W=[lw])
            aa = lora2("a", 96, 1)
            p.act(al[:], aa[:], AF.Sigmoid, R=[aa, pv], W=[al], bias=pv[:, PV["a0"], m:m + 1])
            ag = lora2("g", 256, 2)
            p.cp("scalar", g_f[:], ag[:], R=[ag], W=[g_f])
            p.dma("sync", self.g_out[ip, :, m, :], g_f[:], R=[g_f])
            if cfg.get("vres"):
                av = lora2("v", 64, 1)
                p.act(t1[:], av[:], AF.Sigmoid, R=[av, pv], W=[t1], bias=pv[:, PV["v0"], m:m + 1])
                p.dma("sync", t2[:], self.vfirst_in[ip, :, m, :], W=[t2])
                p.tt("vector", t2[:], t2[:], v_f[:], ALU.subtract, R=[t2, v_f], W=[t2])
                p.tt("vector", t2[:], t2[:], t1[:], ALU.mult, R=[t2, t1], W=[t2])
                p.tt("vector", v_f[:], v_f[:], t2[:], ALU.add, R=[v_f, t2], W=[v_f])
            p.dma("sync", self.v_out[ip, :, m, :], v_f[:], R=[v_f])
            p.op("vector", lambda e, cum=cum, lw=lw: e.tensor_tensor_scan(cum[:], self.rmask[:], lw[:], 0.0, ALU.mult, ALU.add),
                 R=[self.rmask, lw], W=[cum])
            p.ts("vector", kk[:], k_f[:], pv[:, PV["k_k"], m:m + 1], None, ALU.mult, R=[k_f, pv], W=[kk])
            p.act(t1[:], kk[:], AF.Square, R=[kk], W=[t1])
            a1 = self.ps()
            p.mm(a1[:], blk, t1[:], True, True, R=[self.cst, t1], W=[a1])
            p.act(t1[:], a1[:], AF.Sqrt, R=[a1], W=[t1])
            p.ts("vector", t1[:], t1[:], 1e-12, None, ALU.max, R=[t1], W=[t1])
            p.op("vector", lambda e, t1=t1: e.reciprocal(t1[:], t1[:]), R=[t1], W=[t1])
            p.tt("vector", kk[:], kk[:], t1[:], ALU.mult, R=[kk, t1], W=[kk])
            p.ts("vector", t1[:], al[:], -1.0, pv[:, PV["k_a"], m:m + 1], ALU.add, ALU.mult, R=[al, pv], W=[t1])
            p.stt(k_f[:], t1[:], 1.0, k_f[:], ALU.add, ALU.mult, R=[t1, k_f], W=[k_f])
            p.stt(t1[:], r_f[:], pv[:, PV["r_k"], m:m + 1], k_f[:], ALU.mult, ALU.mult, R=[r_f, pv, k_f], W=[t1])
            a2 = self.ps()
            p.mm(a2[:], blk, t1[:], True, True, R=[self.cst, t1], W=[a2])
            p.tt("vector", bon[:], a2[:], v_f[:], ALU.mult, R=[a2, v_f], W=[bon])
            p.dma("sync", self.bonus_out[ip, :, m, :], bon[:], R=[bon])
            p.tt("vector", t3[:], kk[:], al[:], ALU.mult, R=[kk, al], W=[t3])
            fmo = self.FMo[m % 2]
            tm3 = self.TM3[m % 2]
            p.act(t1[:], cum[:], AF.Exp, R=[cum], W=[t1])
            p.tt("vector", fmo[:, 1, :], r_f[:], t1[:], ALU.mult, R=[r_f, t1], W=[fmo])
            p.cp("vector", self.pco[:, m, :], t1[:, 63::64], R=[t1], W=[self.pco])
            p.tt("vector", t2[:], cum[:], lw[:], ALU.subtract, R=[cum, lw], W=[t2])
            p.act(t2[:], t2[:], AF.Exp, R=[t2], W=[t2])
            p.stt(fmo[:, 0, :], kk[:], -1.0, t2[:], ALU.mult, ALU.mult, R=[kk, t2], W=[fmo])
            p.act(t1[:], cum[:], AF.Exp, R=[cum], W=[t1], scale=-1.0)
            p.tt("vector", fmo[:, 2, :], t3[:], t1[:], ALU.mult, R=[t3, t1], W=[fmo])
            p.tt("vector", fmo[:, 3, :], k_f[:], t1[:], ALU.mult, R=[k_f, t1], W=[fmo])
            for ty in range(4):
                p.dma("sync", self.fm_out[ip, m, :, :, ty, :], fmo[:, ty, :].rearrange("p (c t) -> p c t", c=8), R=[fmo])
            cumC = cum[:, 63::64].unsqueeze(2).broadcast_to([128, TP // 64, 64])
            p.tt("vector", t2[:].rearrange("p (a b) -> p a b", b=64), cumC, cum[:].rearrange("p (a b) -> p a b", b=64),
                 ALU.subtract, R=[cum], W=[t2])
            p.act(t2[:], t2[:], AF.Exp, R=[t2], W=[t2])
            p.tt("vector", tm3[:, 0, :], t3[:], t2[:], ALU.mult, R=[t3, t2], W=[tm3])
            p.tt("vector", tm3[:, 1, :], k_f[:], t2[:], ALU.mult, R=[k_f, t2], W=[tm3])
            p.cp("vector", tm3[:, 2, :], v_f[:], R=[v_f], W=[tm3])
            for q in range(3):
                self.transpose_out(tm3[:, q, :], self.tm_out[ip, m, :, :, q, :], [tm3])
        p.dma("sync", self.pc_out[ip].rearrange("m p c -> p m c"), self.pco[:], R=[self.pco])

import numpy as np

NEG = -30000.0


def emit_astage(p, io, S=8192, NHA=2):
    NJ = S // 128
    NI = S // 512
    GQ, GK, GV, GLF, cst, maskb, oT = io["GQ"], io["GK"], io["GV"], io["GLF"], io["cst"], io["maskb"], io["o_out"]

    q_sb = p.sbuf([128, NHA, S], BF16, name="q_sb", ndeps=NHA)
    k_sb = p.sbuf([128, NHA, S], BF16, name="k_sb", ndeps=NHA)
    v_sb = p.sbuf([128, NHA, NJ, 128], BF16, name="v_sb", ndeps=NHA)
    q_sb.deps = [Dep() for _ in range(NHA)]
    k_sb.deps = [Dep() for _ in range(NHA)]
    v_sb.deps = [Dep() for _ in range(NHA)]
    cst_sb = p.sbuf([128, 4, 128], F32, name="acst_sb")
    p.dma("sync", cst_sb[:], cst, W=[cst_sb])
    lf_sb = p.sbuf([128, NHA, NJ], F32, name="lf_sb")
    mk_f = p.sbuf([128, 4, 512], F32, name="mk_f")
    p.dma("sync", mk_f[:], maskb, W=[mk_f])
    for h in range(NHA):
        for rank in range(8):
            for lp in range(2):
                ps_ = rank * 2 + lp
                p.dma("sync", q_sb[:, h, ps_ * 512:(ps_ + 1) * 512], GQ[rank, lp, h], W=[q_sb.deps[h]])
                p.dma("sync", k_sb[:, h, ps_ * 512:(ps_ + 1) * 512], GK[rank, lp, h], W=[k_sb.deps[h]])
                p.dma("sync", v_sb[:, h, ps_ * 4:(ps_ + 1) * 4, :], GV[rank, lp, h].rearrange("tb s f -> s tb f"), W=[v_sb.deps[h]])

    lf_all = p.sbuf([128, NJ, 16], F32, name="lf_all")
    hsel = p.sbuf([128, NHA, 16], F32, name="hsel_sb")
    p.dma("sync", hsel[:], io["hsel"], W=[hsel])
    for rank in range(8):
        for lp in range(2):
            ps_ = rank * 2 + lp
            p.dma("sync", lf_all[:, ps_ * 4:(ps_ + 1) * 4, :], GLF[rank, lp].rearrange("b s h -> s b h"), W=[lf_all])
    for h in range(NHA):
        p.ts("vector", lf_sb[:, h, :], lf_all[:, :, 0], hsel[:, h, 0:1], None, ALU.mult, R=[lf_all, hsel], W=[lf_sb])
        for m_ in range(1, 16):
            p.stt(lf_sb[:, h, :], lf_all[:, :, m_], hsel[:, h, m_:m_ + 1], lf_sb[:, h, :], ALU.mult, ALU.add, R=[lf_all, hsel, lf_sb], W=[lf_sb])
    ones_bf = p.sbuf([128, 128], BF16, name="ones_bf")
    ident_bf = p.sbuf([128, 128], BF16, name="ident_bf")
    p.cp("vector", ones_bf[:], cst_sb[:, 0, :], R=[cst_sb], W=[ones_bf])
    p.cp("vector", ident_bf[:], cst_sb[:, 2, :], R=[cst_sb], W=[ident_bf])

    PSA = [p.psum([128, 512], F32, name=f"PSA{i}") for i in range(2)]
    PO = p.psum([128, 512], F32, name="PO")
    PD = p.psum([128, 512], F32, name="PD")
    PX = [p.psum([128, 512], F32, name=f"PX{i}") for i in range(2)]
    PTb = [p.sbuf([128, 512], BF16, name=f"PTb{i}") for i in range(3)]
    sq = [p.sbuf([128, 512], BF16, name=f"sq{i}") for i in range(2)]
    mx = p.sbuf([128, 2, 32], F32, name="mx")
    Mst = p.sbuf([128, 4], F32, name="Mst")
    c_col = p.sbuf([128, NJ], F32, name="c_col")
    offs = p.sbuf([128, NJ], F32, name="offs")
    rel = p.sbuf([128, NJ], F32, name="rel")
    refM = p.sbuf([128, NI], F32, name="refM")
    tot = p.sbuf([128, 1], F32, name="tot")
    totbc = p.sbuf([128, 128], F32, name="totbc")
    cbias = [p.sbuf([128, NJ], F32, name=f"cbias{i}") for i in range(2)]
    dg = [p.sbuf([128, 128], BF16, name=f"dg{i}") for i in range(2)]
    Roff = [p.sbuf([128, 512], BF16, name=f"Roff{i}") for i in range(2)]
    Rdg = [p.sbuf([128, 4, 512], BF16, name=f"Rdg{i}") for i in range(2)]
    rD = p.sbuf([128, 512], F32, name="rD")
    ob = [p.sbuf([128, 512], F32, name=f"ob{i}") for i in range(2)]

    for h in range(NHA):
        for which, src, dep in ((0, q_sb, q_sb.deps[h]), (1, k_sb, k_sb.deps[h])):
            for i in range(NI):
                s_ = sq[i % 2]
                p.act(s_[:], src[:, h, i * 512:(i + 1) * 512], AF.Square, R=[dep], W=[s_])
                px = PX[i % 2]
                p.mm(px[:], ones_bf[:], s_[:], True, True, R=[ones_bf, s_], W=[px])
                p.op("vector", lambda e, px=px, which=which, i=i: e.reduce_max(mx[:, which, i:i + 1], px[:], AX.X), R=[px], W=[mx])
            p.op("vector", lambda e, which=which: e.reduce_max(Mst[:, which:which + 1], mx[:, which, 0:NI], AX.X), R=[mx], W=[Mst])
        p.tt("vector", Mst[:, 2:3], Mst[:, 0:1], Mst[:, 1:2], ALU.mult, R=[Mst], W=[Mst])
        p.act(Mst[:, 3:4], Mst[:, 2:3], AF.Sqrt, R=[Mst], W=[Mst])
        px = PX[0]
        p.mm(px[0:NJ, 0:1], lf_sb[:, h, :], cst_sb[:, 0, 0:1], True, True, R=[lf_sb, cst_sb], W=[px])
        p.cp("vector", tot[0:NJ, :], px[0:NJ, 0:1], R=[px], W=[tot])
        p.ts("vector", totbc[0:NJ, :], cst_sb[0:NJ, 0, :], tot[0:NJ, 0:1], None, ALU.mult, R=[cst_sb, tot], W=[totbc])
        px = PX[1]
        p.mm(px[:, 0:NJ], totbc[0:NJ, :], cst_sb[0:NJ, 3, 0:NJ], True, True, R=[totbc, cst_sb], W=[px])
        p.cp("vector", offs[:], px[:, 0:NJ], R=[px], W=[offs])
        px = PX[0]
        p.mm(px[:, 0:NJ], cst_sb[:, 1, :], lf_sb[:, h, :], True, True, R=[cst_sb, lf_sb], W=[px])
        p.tt("vector", c_col[:], px[:, 0:NJ], offs[:], ALU.add, R=[px, offs], W=[c_col])
        p.tt("vector", rel[:].rearrange("p (a b) -> p a b", b=4), c_col[:].rearrange("p (a b) -> p a b", b=4),
             offs[:, 0::4].unsqueeze(2).broadcast_to([128, NI, 4]), ALU.subtract, R=[c_col, offs], W=[rel])
        p.ts("vector", refM[:], offs[:, 0::4], Mst[:, 3:4], None, ALU.subtract, R=[offs, Mst], W=[refM])

        for i in range(NI):
            cb = cbias[i % 2]
            p.ts("vector", cb[:], c_col[:], -1.0, refM[:, i:i + 1], ALU.mult, ALU.add, R=[c_col, refM], W=[cb])
            px = PX[i % 2]
            for jj in range(4):
                d_ = dg[jj % 2]
                p.ts("vector", d_[:], cst_sb[:, 2, :], rel[:, 4 * i + jj:4 * i + jj + 1], None, ALU.mult, R=[cst_sb, rel], W=[d_])
                p.mm(px[:, jj * 128:(jj + 1) * 128], ones_bf[:], d_[:], True, True, R=[ones_bf, d_], W=[px])
            ro = Roff[i % 2]
            p.cp("scalar", ro[:], px[:], R=[px], W=[ro])
            rd = Rdg[i % 2]
            for r in range(4):
                p.tt("vector", rd[:, r, :], ro[:], mk_f[:, r, :], ALU.add, R=[ro, mk_f], W=[rd])
            nj = 4 * i + 4
            def score(j):
                ps = PSA[j % 2]
                p.mm(ps[:], k_sb[:, h, j * 128:(j + 1) * 128], q_sb[:, h, i * 512:(i + 1) * 512], True, False,
                     R=[k_sb.deps[h], q_sb.deps[h]], W=[ps])
                rterm = ro[:] if j < 4 * i else rd[:, j - 4 * i, :]
                p.mm(ps[:], ident_bf[:], rterm, False, True, R=[ident_bf, ro, rd], W=[ps])

            score(0)
            for j in range(nj):
                if j + 1 < nj:
                    score(j + 1)
                ps = PSA[j % 2]
                pt = PTb[j % 3]
                p.act(pt[:], ps[:], AF.Exp, R=[ps, cb], W=[pt], bias=cb[:, j:j + 1])
                p.mm(PO[:], v_sb[:, h, j, :], pt[:], j == 0, j == nj - 1, R=[v_sb.deps[h], pt], W=[PO])
                p.mm(PD[:], ones_bf[:], pt[:], j == 0, j == nj - 1, R=[ones_bf, pt], W=[PD])
            p.op("vector", lambda e: e.reciprocal(rD[:], PD[:]), R=[PD], W=[rD])
            o_ = ob[i % 2]
            p.tt("vector", o_[:], PO[:], rD[:], ALU.mult, R=[PO, rD], W=[o_])
            p.dma("sync", oT[:, h, i * 512:(i + 1) * 512], o_[:], R=[o_])


def a_consts():
    cst = np.zeros((128, 4, 128), np.float32)
    cst[:, 0] = 1.0
    k = np.arange(128)[:, None]
    m = np.arange(128)[None, :]
    cst[:, 1] = (k <= m)
    cst[:, 2] = np.eye(128)
    cst[:, 3] = (k < m)
    mb = np.zeros((128, 4, 512), np.float32)
    s = np.arange(128)[:, None]
    t = np.arange(512)[None, :]
    for r in range(4):
        mb[:, r] = np.where(128 * r + s > t, NEG, 0.0)
    return cst, mb


import ml_dtypes as _mld
import os
NCORES = 8
SEQ = 8192
NPASS = SEQ // TP
_CACHE = {}


def _fmcols(vec):
    return np.ascontiguousarray(np.asarray(vec, np.float32).reshape(16, 128).T)


def _make_pv(I, layer, i=None, aux=None, mixn=None):
    pv = np.zeros((128, NPV, 16), np.float32)
    pv[:, PV["mix_norm"]] = _fmcols(I["mix_norm"][layer if mixn is None else mixn])
    pv[:, PV["ffn_norm"]] = _fmcols(I["ffn_norm"][layer])
    if aux is not None:
        pv[:, PV["aux_norm"]] = _fmcols(aux)
    if i is not None:
        for j, n in enumerate(["mix_r", "mix_k", "mix_v", "mix_w", "mix_a", "mix_g"]):
            pv[:, PV[n]] = _fmcols(I["rwkv_x_mix"][i, j])
        for n, k in [("w0", "rwkv_w0"), ("a0", "rwkv_a0"), ("k_k", "rwkv_k_k"), ("k_a", "rwkv_k_a"),
                     ("lnx_w", "rwkv_lnx_w"), ("lnx_b", "rwkv_lnx_b")]:
            pv[:, PV[n]] = _fmcols(I[k][i])
        pv[:, PV["r_k"]] = _fmcols(I["rwkv_r_k"][i].reshape(-1))
        if i > 0:
            pv[:, PV["v0"]] = _fmcols(I["rwkv_v0"][i - 1])
    return pv


def _t_cst():
    c = np.zeros((128, 4, 128), np.float32)
    c[:, 0] = 1.0
    c[0:64, 1, 0:64] = 1.0
    c[64:, 1, 64:] = 1.0
    c[:, 2] = np.eye(128, dtype=np.float32)
    return c


def _rmask():
    m = np.ones((128, TP), np.float32)
    m[:, ::64] = 0.0
    return m


def build_fused():
    p = Prog()
    nc = p.nc
    p.ext_names = set()

    def EI(n, s, dt=F32):
        p.ext_names.add(n)
        return p.dram(n, s, dt, "ExternalInput")

    STOP = int(os.environ.get("KSTOP", "99"))
    dbg = DView(p.dram("dbg_out", [128, 512], F32, "ExternalOutput")) if STOP < 99 else None

    class _Stop(Exception):
        pass

    def checkpoint(k, src=None):
        if STOP == k:
            if src is not None:
                p.dma("gpsimd", dbg if len(src.ap.shape) == 2 and src.ap.shape[1] == 512 else dbg[:, 0:src.ap.shape[-1]], src)
            raise _Stop()

    xin = EI("xT", [2, 128, NC16, TP + 1])
    class _LazyPV(dict):
        def __missing__(self, k):
            self[k] = EI("pv_" + k, [128, NPV, NC16])
            return self[k]

    pvs = _LazyPV()
    tcst = EI("tcst", [128, 4, 128])
    rmask = EI("rmask", [128, TP])
    acst = EI("acst", [128, 4, 128])
    maskb = EI("maskb", [128, 4, 512])
    WS = dict(rwkv_w_rkv=[2, 3, D, D], rwkv_w1=[2, D, 96], rwkv_w2=[2, 96, D], rwkv_a1=[2, D, 96], rwkv_a2=[2, 96, D],
              rwkv_v1=[1, D, 64], rwkv_v2=[1, 64, D], rwkv_g1=[2, D, 256], rwkv_g2=[2, 256, D], rwkv_w_o=[2, D, D],
              w_kvf=[D, 2 * D + 16], b_f=[16, 1], fox_w_q=[2, D, D], fox_w_o=[2, D, D], mlp_w_up=[4, D, DFF], mlp_w_down=[4, DFF, D])

    class _LazyW(dict):
        def __missing__(self, k):
            self[k] = EI(k, WS[k])
            return self[k]

    W = _LazyW()
    fin_out = DView(p.dram("fin_out", [2, 128, NC16, TP], F32, "ExternalOutput")) if STOP == 99 else None

    def pair(name, rows, cols, dt):
        a = p.dram_internal(name + "_c", [rows, cols], dt)
        g = p.dram_internal(name + "_g", [8 * rows, cols], dt)
        return a, g

    fm_c, GFM = pair("fm", 2 * 16 * 128, 4 * 512, BF16)
    tm_c, GTM = pair("tm", 2 * 16 * 512, 3 * 128, BF16)
    pc_c, GPC = pair("pc", 2 * 16 * 128, 8, F32)
    y_c, GY = pair("y", 32 * 4 * 64, 256, F32)
    hl_c, GH = pair("hl", 2 * 128, 16, F32)
    k_c, GK = pair("k", 2 * 16 * 128, 512, BF16)
    q_c, GQ = pair("q", 2 * 16 * 128, 512, BF16)
    v2_c, GV = pair("v2", 2 * 16 * 4 * 128, 128, BF16)
    lf_c, GLF = pair("lf", 2 * 4 * 128, 16, F32)
    o_c, GO = pair("o", 128, 2 * SEQ, F32)
    own = lambda n: p.dram_internal(n, [2, 128, NC16, TP], F32)
    bonus_c, g_c, vfirst_c, vdump_c = own("bonus_c"), own("g_c"), own("vfirst_c"), own("vdump_c")
    x1_c, x2_c, x3_c, x4_c = own("x1_c"), own("x2_c"), own("x3_c"), own("x4_c")

    def V(dv, pat, **kw):
        return DView(dv.ap.rearrange(pat, **kw), dv.dep)

    fm_v = V(fm_c, "(l m p) (c y t) -> l m p c y t", l=2, m=16, c=8, y=4)
    tm_v = V(tm_c, "(l m b k) (q f) -> l m b k q f", l=2, m=16, b=4, q=3)
    pc_v = V(pc_c, "(l m p) c -> l m p c", l=2, m=16)
    GFM_v = V(GFM, "(r l m p) (c y t) -> r l m p c y t", r=8, l=2, m=16, c=8, y=4)
    GTM_v = V(GTM, "(r l m k) (q f) -> r l m k q f", r=8, l=2, m=16, q=3)
    GPC_v = V(GPC, "(r l m p) c -> r l m p c", r=8, l=2, m=16)
    y_v = V(y_c, "(g h v) t -> g h v t", g=32, h=4)
    GY_v = V(GY, "(r g h v) t -> r g h v t", r=8, g=32, h=4)
    GH_v = V(GH, "(r l p) c -> r l p c", r=8, l=2)
    hl_v = V(hl_c, "(l p) c -> l p c", l=2)
    k_v = V(k_c, "(l m d) t -> l m d t", l=2, m=16)
    q_v = V(q_c, "(l m d) t -> l m d t", l=2, m=16)
    v2_v = V(v2_c, "(l m b k) f -> l m b k f", l=2, m=16, b=4)
    lf_v = V(lf_c, "(l b s) h -> l b s h", l=2, b=4)
    GK_v = V(GK, "(r l m d) t -> r l m d t", r=8, l=2, m=16)
    GQ_v = V(GQ, "(r l m d) t -> r l m d t", r=8, l=2, m=16)
    GV_v = V(GV, "(r l m b k) f -> r l m b k f", r=8, l=2, m=16, b=4)
    GLF_v = V(GLF, "(r l b s) h -> r l b s h", r=8, l=2, b=4)
    o_v = V(o_c, "d (h s) -> d h s", h=2)
    GO_v = V(GO, "(a d) (h s) -> a d h s", a=8, h=2)

    def loc(name, shape, dt):
        return p.dram_internal(name, shape, dt)

    fm_l = loc("fm_l", [8, 2, 2, 128, 8, 4, 64], BF16)
    tm_l = loc("tm_l", [8, 2, 2, 512, 3, 128], BF16)
    pc_l = loc("pc_l", [8, 2, 2, 128, 8], F32)
    y_l = loc("y_l", [8, 4, 4, 64, 256], F32)
    q_l = loc("q_l", [8, 2, 2, 128, 512], BF16)
    k_l = loc("k_l", [8, 2, 2, 128, 512], BF16)
    v_l = loc("v_l", [8, 2, 2, 4, 128, 128], BF16)
    o_l = loc("o_l", [8, 128, 2, 1024], F32)

    def localize(dst, gv, fn, pat_d, pat_s):
        p.dma("sync", dst.rearrange(pat_d), gv.dyn(lambda a, c: fn(a, c).rearrange(pat_s)))

    def loc_rwkv():
        localize(fm_l, GFM_v, lambda a, c: a[:, :, 2 * c:2 * c + 2], "r l m p c y t -> (r l) (m p c y t)", "r l m p c y t -> (r l) (m p c y t)")
        localize(tm_l, GTM_v, lambda a, c: a[:, :, 2 * c:2 * c + 2], "r l m k q f -> (r l) (m k q f)", "r l m k q f -> (r l) (m k q f)")
        localize(pc_l, GPC_v, lambda a, c: a[:, :, 2 * c:2 * c + 2], "r l m p c -> (r l) (m p c)", "r l m p c -> (r l) (m p c)")

    def loc_y():
        localize(y_l, GY_v, lambda a, c: a[:, 4 * c:4 * c + 4], "r g h v t -> r (g h v t)", "r g h v t -> r (g h v t)")

    def loc_q():
        localize(q_l, GQ_v, lambda a, c: a[:, :, 2 * c:2 * c + 2], "r l m d t -> (r l) (m d t)", "r l m d t -> (r l) (m d t)")

    def loc_kv():
        localize(k_l, GK_v, lambda a, c: a[:, :, 2 * c:2 * c + 2], "r l m d t -> (r l) (m d t)", "r l m d t -> (r l) (m d t)")
        localize(v_l, GV_v, lambda a, c: a[:, :, 2 * c:2 * c + 2], "r l m b k f -> (r l) (m b k f)", "r l m b k f -> (r l) (m b k f)")

    def loc_o():
        localize(o_l, GO_v, lambda a, c: a[:, :, :, 1024 * c:1024 * c + 1024], "a d h s -> (a d h) s", "a d h s -> (a d h) s")

    def yin(ip, c):
        rc, mm = c // 2, c % 2
        return y_l[rc, 2 * ip:2 * ip + 2, 2 * mm:2 * mm + 2, :, :].rearrange("g h v t -> (h v) g t")

    def oin(ip, c):
        return o_l[c // 2, :, c % 2, 512 * ip:512 * ip + 512]

    base = dict(cst=tcst, rmask=rmask)

    def rwkv_layer(i, xsrc, k0):
        io = dict(base, pv=pvs["P%d" % i], w_rkv=W["rwkv_w_rkv"][i], w1=W["rwkv_w1"][i], w2=W["rwkv_w2"][i], a1=W["rwkv_a1"][i],
                  a2=W["rwkv_a2"][i], g1=W["rwkv_g1"][i], g2=W["rwkv_g2"][i], fm_out=fm_v, tm_out=tm_v, pc_out=pc_v,
                  bonus_out=bonus_c, g_out=g_c, v_out=vfirst_c if i == 0 else vdump_c, **xsrc)
        if i > 0:
            io.update(v1=W["rwkv_v1"][0], v2=W["rwkv_v2"][0], vfirst_in=vfirst_c, halo_all=GH_v[:, 1], hl_own=hl_v[0], hflag=EI("hflag", [128, 8]))
        with p.scope():
            TS(p, dict(pre="rwkv", npass=2, vres=i > 0), io)
        checkpoint(k0 + 0, bonus_c[0, :, 0, :])
        p.collective("AllGather", fm_c, GFM)
        p.collective("AllGather", tm_c, GTM)
        p.collective("AllGather", pc_c, GPC)
        checkpoint(k0 + 1, V(GTM, "(a b) c -> a b c", b=128)[5 * 128 + 3, :, 0:128].rearrange("p c -> p c"))
        loc_rwkv()
        checkpoint(k0 + 2, V(tm_l, "r l m k q f -> (r l m) k (q f)")[0, 0:128, 0:384])
        with p.scope():
            emit_rstage(p, dict(GFM=fm_l, GTM=tm_l, GPC=pc_l, cst=_rcst(),
                                y_out=y_v), SEQ // 64, 4)
        checkpoint(k0 + 3, V(y_c, "(a b) c -> a b c", b=128)[0, :, 0:256])
        p.collective("AllGather", y_c, GY)
        loc_y()
        checkpoint(k0 + 4, V(y_l, "r g h v t -> (r g) (h v) t")[0, 0:128, :])

    rcst_h = []
    hsel_h = []

    def _hsel():
        if not hsel_h:
            hsel_h.append(EI("hsel", [128, 2, 16]))
        return hsel_h[0]

    def _rcst():
        if not rcst_h:
            rcst_h.append(EI("rcst", [128, 3, 128]))
        return rcst_h[0]

    def _body():
        rwkv_layer(0, dict(xin=xin), 0)
        with p.scope():
            TS(p, dict(post="rwkv", mlp=True, pre=None, npass=2),
               dict(base, pv=pvs["M0"], xin=xin, yin=yin, bonus_in=bonus_c, g_in=g_c, w_o=W["rwkv_w_o"][0], w_up=W["mlp_w_up"][0],
                    w_down=W["mlp_w_down"][0], xout=x1_c, hl_out=hl_v))
        checkpoint(5, x1_c[0, :, 0, :])
        p.collective("AllGather", hl_c, GH)
        rwkv_layer(1, dict(xbody=x1_c), 10)
        with p.scope():
            TS(p, dict(post="rwkv", mlp=True, pre="kvq", npass=2),
               dict(base, pv=pvs["M1"], xbody=x1_c, yin=yin, bonus_in=bonus_c, g_in=g_c, w_o=W["rwkv_w_o"][1], w_up=W["mlp_w_up"][1],
                    w_down=W["mlp_w_down"][1], xout=x2_c, w_kvf=W["w_kvf"], bf_d=W["b_f"], w_q=W["fox_w_q"][0],
                    kT_out=k_v, vtm_out=v2_v, lf_out=lf_v, qT_out=q_v))
        checkpoint(15, x2_c[0, :, 0, :])
        p.collective("AllGather", k_c, GK)
        p.collective("AllGather", v2_c, GV)
        p.collective("AllGather", lf_c, GLF)
        p.collective("AllGather", q_c, GQ)
        loc_kv()
        loc_q()
        checkpoint(16, x2_c[0, :, 0, :])
        xs = [x2_c, x3_c, x4_c]
        for j in range(2):
            with p.scope():
                emit_astage(p, dict(GQ=q_l, GK=k_l, GV=v_l, GLF=GLF_v, hsel=_hsel(), cst=acst_h[0], maskb=acst_h[1], o_out=o_v), SEQ, 2)
            checkpoint(20 + 10 * j, o_v[:, 0, 0:512])
            p.collective("AllGather", o_c, GO)
            loc_o()
            io = dict(base, pv=pvs["M%d" % (2 + j)], xbody=xs[j], oin=oin, w_o=W["fox_w_o"][j], w_up=W["mlp_w_up"][2 + j],
                      w_down=W["mlp_w_down"][2 + j], xout=xs[j + 1])
            if j == 0:
                io.update(w_q=W["fox_w_q"][1], qT_out=q_v)
                with p.scope():
                    TS(p, dict(post="fox", mlp=True, pre="q", npass=2), io)
                checkpoint(25, x3_c[0, :, 0, :])
                p.collective("AllGather", q_c, GQ)
                loc_q()
            else:
                io.update(fin_out=fin_out)
                with p.scope():
                    TS(p, dict(post="fox", mlp=True, pre="final", npass=2), io)

    acst_h = [acst, maskb]
    try:
        _body()
    except _Stop:
        pass
    nc_ = p.finish()
    nc_._ext_names = set(p.ext_names)
    return nc_


def kernel(**I):
    I = {k: np.asarray(v) for k, v in I.items()}
    f32 = lambda a: np.ascontiguousarray(a, dtype=np.float32)
    if "nc" not in _CACHE:
        _CACHE["nc"] = build_fused()
    nc = _CACHE["nc"]
    x = I["x"][0].astype(np.float32)
    xr = np.ascontiguousarray(x.T).reshape(16, 128, SEQ)
    slabs = np.zeros((NPASS, 128, 16, TP + 1), np.float32)
    for ps in range(NPASS):
        slabs[ps, :, :, 1:] = xr[:, :, ps * TP:(ps + 1) * TP].transpose(1, 0, 2)
        if ps > 0:
            slabs[ps, :, :, 0] = xr[:, :, ps * TP - 1].T
    acst_, maskb_ = a_consts()
    shared = dict(
        pv_P0=_make_pv(I, 0, 0), pv_M0=_make_pv(I, 0, 0), pv_P1=_make_pv(I, 1, 1),
        pv_M1=_make_pv(I, 1, 1, aux=I["kv_norm"], mixn=2), pv_M2=_make_pv(I, 2, mixn=3), pv_M3=_make_pv(I, 3, aux=I["final_norm"]),
        tcst=_t_cst(), rmask=_rmask(), rcst=consts(), acst=acst_, maskb=maskb_,
        b_f=f32(I["b_f"]).reshape(16, 1))
    for k in ("rwkv_w_rkv", "rwkv_w1", "rwkv_w2", "rwkv_a1", "rwkv_a2", "rwkv_v1", "rwkv_v2", "rwkv_g1", "rwkv_g2", "rwkv_w_o",
              "w_kvf", "fox_w_q", "fox_w_o", "mlp_w_up", "mlp_w_down"):
        shared[k] = f32(I[k])
    in_maps = []
    for c in range(NCORES):
        m = dict(shared)
        m["xT"] = np.ascontiguousarray(slabs[2 * c:2 * c + 2])
        hf = np.zeros((128, 8), np.float32)
        if c > 0:
            hf[:, c - 1] = 1.0
        m["hflag"] = hf
        hs_ = np.zeros((128, 2, 16), np.float32)
        hs_[:, 0, 2 * c] = 1.0
        hs_[:, 1, 2 * c + 1] = 1.0
        m["hsel"] = hs_
        in_maps.append(m)
    in_maps = [{k: v for k, v in m.items() if k in nc._ext_names} for m in in_maps]
    res = run_bass_kernel_spmd(nc, in_maps, core_ids=list(range(NCORES)))
    if "fin_out" not in res.results[0]:
        return res.results
    fin = np.concatenate([np.asarray(r["fin_out"]) for r in res.results], axis=0)
    out = np.ascontiguousarray(fin.transpose(2, 1, 0, 3)).reshape(2048, SEQ)
    return np.ascontiguousarray(out.T)[None].astype(np.float32)
```
